# Optimizing a Trainium2 kernel written in Bass

```python
import jax, jax.numpy as jnp
from jax import lax
import numpy as np

D_MODEL = 1024
BATCH = 8
SEQ = 4096
DEPTH = 4
DEC_BATCH = 32
DEC_SEQ = 16
PAST_LEN = 2048

CHUNK = 64
Q_BLOCK = 128
N_MEM = 256
EPS = 1e-6
FOX_HEADS = 16
FOX_HEAD_DIM = D_MODEL // FOX_HEADS
FOX_WIDTH = FOX_HEADS * FOX_HEAD_DIM
FOX_IN = 3 * FOX_WIDTH + FOX_HEADS + FOX_WIDTH
FOX_SCALE = FOX_HEAD_DIM ** -0.5
MLA_HEADS = 16
MLA_NOPE = 64
MLA_ROPE = 32
MLA_V = 64
MLA_Q_LORA = 384
MLA_KV_LORA = 256
MLA_DOWN = MLA_Q_LORA + MLA_KV_LORA + MLA_ROPE
MLA_SCALE = (MLA_NOPE + MLA_ROPE) ** -0.5
ROPE_THETA = 10000.0
X_HEADS = 4
X_HEAD_DIM = D_MODEL // X_HEADS
X_WIDTH = X_HEADS * X_HEAD_DIM
X_SCALE = X_HEAD_DIM ** -0.5
D_FF = ((8 * D_MODEL + 3 * 256 - 1) // (3 * 256)) * 256
N_FOX = (DEPTH + 1) // 2
N_MLA = DEPTH // 2

kernel_name = "fox_mla_streaming_encoder_step"


def rms_normalize(x):
    x32 = x.astype(jnp.float32)
    return (x32 * lax.rsqrt(jnp.mean(x32 * x32, axis=-1, keepdims=True) + EPS)).astype(x.dtype)


def rmsnorm(x, g):
    return rms_normalize(x) * g


def rope(x, pos):
    half = x.shape[-1] // 2
    inv = ROPE_THETA ** (-jnp.arange(half, dtype=jnp.float32) / half)
    ang = pos.astype(jnp.float32)[:, None] * inv[None, :]
    if x.ndim == 4:
        ang = ang[:, None, :]
    cos, sin = jnp.cos(ang), jnp.sin(ang)
    x1 = x[..., :half].astype(jnp.float32)
    x2 = x[..., half:].astype(jnp.float32)
    return jnp.concatenate([x1 * cos - x2 * sin, x1 * sin + x2 * cos], axis=-1).astype(x.dtype)


def masked_softmax(scores, mask):
    return jax.nn.softmax(jnp.where(mask, scores, -jnp.inf), axis=-1)


def sweep_query_blocks(block_fn, q_arrays, q_pos):
    t = q_pos.shape[0]
    if t <= Q_BLOCK:
        return block_fn(*q_arrays, q_pos)
    nb = t // Q_BLOCK
    blocks = tuple(jnp.moveaxis(a.reshape((a.shape[0], nb, Q_BLOCK) + a.shape[2:]), 1, 0) for a in q_arrays)
    out = lax.map(lambda args: block_fn(*args), blocks + (q_pos.reshape(nb, Q_BLOCK),))
    out = jnp.moveaxis(out, 0, 1)
    return out.reshape((out.shape[0], t) + out.shape[3:])


def fox_mixer(h, q_pos, past, w_in, b_f, w_out):
    b, t, _ = h.shape
    proj = h @ w_in
    q, k, v, f_pre, gate = jnp.split(proj, [FOX_WIDTH, 2 * FOX_WIDTH, 3 * FOX_WIDTH, 3 * FOX_WIDTH + FOX_HEADS], axis=-1)
    q = q.reshape(b, t, FOX_HEADS, FOX_HEAD_DIM)
    k = k.reshape(b, t, FOX_HEADS, FOX_HEAD_DIM)
    v = v.reshape(b, t, FOX_HEADS, FOX_HEAD_DIM)
    logf = jax.nn.log_sigmoid((f_pre + b_f).astype(jnp.float32))
    if past is None:
        k_all, v_all, lf_all, k_pos = k, v, logf, q_pos
    else:
        k_all = jnp.concatenate([past[0], k], axis=1)
        v_all = jnp.concatenate([past[1], v], axis=1)
        lf_all = jnp.concatenate([past[2].astype(jnp.float32), logf], axis=1)
        k_pos = jnp.arange(k_all.shape[1], dtype=jnp.int32)
    cum = jnp.cumsum(lf_all, axis=1)
    ck = jnp.moveaxis(cum, 1, 2)[:, :, None, :]
    cq_all = cum[:, -t:]

    def block(qb, cqb, pq):
        s = jnp.einsum('bqhd,bkhd->bhqk', qb, k_all).astype(jnp.float32) * FOX_SCALE
        s = s + jnp.moveaxis(cqb, 1, 2)[..., None] - ck
        mask = k_pos[None, :] <= pq[:, None]
        p = masked_softmax(s, mask).astype(v_all.dtype)
        return jnp.einsum('bhqk,bkhd->bqhd', p, v_all)

    o = sweep_query_blocks(block, (q, cq_all), q_pos)
    o = o.reshape(b, t, FOX_WIDTH) * jax.nn.sigmoid(gate)
    return o @ w_out, (k, v, logf.astype(h.dtype))


def mla_mixer(h, q_pos, past, w_a, g_q, g_kv, w_qb, w_kvb, w_out):
    b, t, _ = h.shape
    a = h @ w_a
    c_q, c_kv, k_r = jnp.split(a, [MLA_Q_LORA, MLA_Q_LORA + MLA_KV_LORA], axis=-1)
    q = (rmsnorm(c_q, g_q) @ w_qb).reshape(b, t, MLA_HEADS, MLA_NOPE + MLA_ROPE)
    q_nope, q_rope = jnp.split(q, [MLA_NOPE], axis=-1)
    q_rope = rope(q_rope, q_pos)
    c_kv = rmsnorm(c_kv, g_kv)
    k_r = rope(k_r, q_pos)
    if past is None:
        ckv_all, kr_all, k_pos = c_kv, k_r, q_pos
    else:
        ckv_all = jnp.concatenate([past[0], c_kv], axis=1)
        kr_all = jnp.concatenate([past[1], k_r], axis=1)
        k_pos = jnp.arange(ckv_all.shape[1], dtype=jnp.int32)
    kv = (ckv_all @ w_kvb).reshape(b, ckv_all.shape[1], MLA_HEADS, MLA_NOPE + MLA_V)
    k_nope, v = jnp.split(kv, [MLA_NOPE], axis=-1)
    k_chunk = k_pos // CHUNK

    def block(qn, qr, pq):
        s = (jnp.einsum('bqhn,bkhn->bhqk', qn, k_nope)
             + jnp.einsum('bqhr,bkr->bhqk', qr, kr_all)).astype(jnp.float32) * MLA_SCALE
        mask = k_chunk[None, :] <= (pq // CHUNK)[:, None]
        p = masked_softmax(s, mask).astype(v.dtype)
        return jnp.einsum('bhqk,bkhd->bqhd', p, v)

    o = sweep_query_blocks(block, (q_nope, q_rope), q_pos)
    return o.reshape(b, t, MLA_HEADS * MLA_V) @ w_out, (c_kv, k_r)


def memory_kv(mem, g_mem, w_x_kv):
    m_l = rms_normalize(mem)[None] * g_mem[:, None, None, :]
    kv = jnp.einsum('lbnd,lde->lbne', m_l, w_x_kv)
    k, v = jnp.split(kv, 2, axis=-1)
    shp = (DEPTH, mem.shape[0], mem.shape[1], X_HEADS, X_HEAD_DIM)
    return k.reshape(shp), v.reshape(shp)


def cross_attn(h, mk, mv, w_q, w_o):
    b, t, _ = h.shape
    q = (h @ w_q).reshape(b, t, X_HEADS, X_HEAD_DIM)
    s = jnp.einsum('bqhd,bkhd->bhqk', q, mk).astype(jnp.float32) * X_SCALE
    p = jax.nn.softmax(s, axis=-1).astype(mv.dtype)
    o = jnp.einsum('bhqk,bkhd->bqhd', p, mv).reshape(b, t, X_WIDTH)
    return o @ w_o


def swiglu(h, w_gu, w_down):
    g, u = jnp.split(h @ w_gu, 2, axis=-1)
    return (jax.nn.silu(g) * u) @ w_down


def trunk(x, q_pos, fox_past, mla_past, mem_k, mem_v,
          g_mix, g_cross, g_ffn, g_final,
          w_fox_in, b_fox_f, w_fox_out,
          w_mla_a, g_mla_q, g_mla_kv, w_mla_qb, w_mla_kvb, w_mla_out,
          w_x_q, w_x_o, w_ffn_gu, w_ffn_down):
    fox_k, fox_v, fox_lf, mla_c, mla_r = [], [], [], [], []
    for i in range(DEPTH):
        j = i // 2
        h = rmsnorm(x, g_mix[i])
        if i % 2 == 0:
            past = None if fox_past is None else (fox_past[0][j], fox_past[1][j], fox_past[2][j])
            o, st = fox_mixer(h, q_pos, past, w_fox_in[j], b_fox_f[j], w_fox_out[j])
            fox_k.append(st[0]); fox_v.append(st[1]); fox_lf.append(st[2])
        else:
            past = None if mla_past is None else (mla_past[0][j], mla_past[1][j])
            o, st = mla_mixer(h, q_pos, past, w_mla_a[j], g_mla_q[j], g_mla_kv[j],
                              w_mla_qb[j], w_mla_kvb[j], w_mla_out[j])
            mla_c.append(st[0]); mla_r.append(st[1])
        x = x + o
        x = x + cross_attn(rmsnorm(x, g_cross[i]), mem_k[i], mem_v[i], w_x_q[i], w_x_o[i])
        x = x + swiglu(rmsnorm(x, g_ffn[i]), w_ffn_gu[i], w_ffn_down[i])
    y = rmsnorm(x, g_final)
    return y, jnp.stack(fox_k), jnp.stack(fox_v), jnp.stack(fox_lf), jnp.stack(mla_c), jnp.stack(mla_r)


def setup_inputs(seed: int = 0) -> dict:
    key = jax.random.key(seed)
    ks = iter(jax.random.split(key, 40))
    f32 = jnp.float32

    def nrm(shape, scale=1.0):
        return jax.random.normal(next(ks), shape, f32) * scale

    def gain(shape):
        return 1.0 + 0.05 * nrm(shape)

    d = D_MODEL
    return {
        "x_prompt": nrm((BATCH, SEQ, d)),
        "x_sample": nrm((DEC_BATCH, DEC_SEQ, d)),
        "mem_prompt": nrm((BATCH, N_MEM, d)),
        "cache_fox_k": nrm((N_FOX, DEC_BATCH, PAST_LEN, FOX_HEADS, FOX_HEAD_DIM)),
        "cache_fox_v": nrm((N_FOX, DEC_BATCH, PAST_LEN, FOX_HEADS, FOX_HEAD_DIM)),
        "cache_fox_logf": jax.nn.log_sigmoid(3.0 + 0.5 * nrm((N_FOX, DEC_BATCH, PAST_LEN, FOX_HEADS))),
        "cache_mla_ckv": nrm((N_MLA, DEC_BATCH, PAST_LEN, MLA_KV_LORA)),
        "cache_mla_krope": nrm((N_MLA, DEC_BATCH, PAST_LEN, MLA_ROPE)),
        "cache_mem_k": nrm((DEPTH, DEC_BATCH, N_MEM, X_HEADS, X_HEAD_DIM)),
        "cache_mem_v": nrm((DEPTH, DEC_BATCH, N_MEM, X_HEADS, X_HEAD_DIM)),
        "g_mix": gain((DEPTH, d)),
        "g_cross": gain((DEPTH, d)),
        "g_mem": gain((DEPTH, d)),
        "g_ffn": gain((DEPTH, d)),
        "g_final": gain((d,)),
        "w_fox_in": nrm((N_FOX, d, FOX_IN), d ** -0.5),
        "b_fox_f": 3.0 + 0.5 * nrm((N_FOX, FOX_HEADS)),
        "w_fox_out": nrm((N_FOX, FOX_WIDTH, d), FOX_WIDTH ** -0.5),
        "w_mla_a": nrm((N_MLA, d, MLA_DOWN), d ** -0.5),
        "g_mla_q": gain((N_MLA, MLA_Q_LORA)),
        "g_mla_kv": gain((N_MLA, MLA_KV_LORA)),
        "w_mla_qb": nrm((N_MLA, MLA_Q_LORA, MLA_HEADS * (MLA_NOPE + MLA_ROPE)), MLA_Q_LORA ** -0.5),
        "w_mla_kvb": nrm((N_MLA, MLA_KV_LORA, MLA_HEADS * (MLA_NOPE + MLA_V)), MLA_KV_LORA ** -0.5),
        "w_mla_out": nrm((N_MLA, MLA_HEADS * MLA_V, d), (MLA_HEADS * MLA_V) ** -0.5),
        "w_x_q": nrm((DEPTH, d, X_WIDTH), d ** -0.5),
        "w_x_kv": nrm((DEPTH, d, 2 * X_WIDTH), d ** -0.5),
        "w_x_o": nrm((DEPTH, X_WIDTH, d), X_WIDTH ** -0.5),
        "w_ffn_gu": nrm((DEPTH, d, 2 * D_FF), d ** -0.5),
        "w_ffn_down": nrm((DEPTH, D_FF, d), D_FF ** -0.5),
    }


def reference(x_prompt, x_sample, mem_prompt, cache_fox_k, cache_fox_v, cache_fox_logf,
              cache_mla_ckv, cache_mla_krope, cache_mem_k, cache_mem_v,
              g_mix, g_cross, g_mem, g_ffn, g_final,
              w_fox_in, b_fox_f, w_fox_out,
              w_mla_a, g_mla_q, g_mla_kv, w_mla_qb, w_mla_kvb, w_mla_out,
              w_x_q, w_x_kv, w_x_o, w_ffn_gu, w_ffn_down):
    mem_k_p, mem_v_p = memory_kv(mem_prompt, g_mem, w_x_kv)
    pos_p = jnp.arange(x_prompt.shape[1], dtype=jnp.int32)
    y_prompt, fk_p, fv_p, fl_p, mc_p, mr_p = trunk(
        x_prompt, pos_p, None, None, mem_k_p, mem_v_p,
        g_mix, g_cross, g_ffn, g_final, w_fox_in, b_fox_f, w_fox_out,
        w_mla_a, g_mla_q, g_mla_kv, w_mla_qb, w_mla_kvb, w_mla_out,
        w_x_q, w_x_o, w_ffn_gu, w_ffn_down)
    past_len = cache_fox_k.shape[2]
    pos_s = past_len + jnp.arange(x_sample.shape[1], dtype=jnp.int32)
    y_sample, fk_s, fv_s, fl_s, mc_s, mr_s = trunk(
        x_sample, pos_s, (cache_fox_k, cache_fox_v, cache_fox_logf), (cache_mla_ckv, cache_mla_krope),
        cache_mem_k, cache_mem_v,
        g_mix, g_cross, g_ffn, g_final, w_fox_in, b_fox_f, w_fox_out,
        w_mla_a, g_mla_q, g_mla_kv, w_mla_qb, w_mla_kvb, w_mla_out,
        w_x_q, w_x_o, w_ffn_gu, w_ffn_down)
    return (y_prompt, y_sample,
            fk_p, fv_p, fl_p, mc_p, mr_p, mem_k_p, mem_v_p,
            fk_s, fv_s, fl_s, mc_s, mr_s)
```

```python
import numpy as np
import ml_dtypes
from contextlib import ExitStack
import concourse.bass as bass
import concourse.mybir as mybir
from concourse.bass_utils import run_bass_kernel_spmd

F32 = mybir.dt.float32
BF16 = mybir.dt.bfloat16
AF = mybir.ActivationFunctionType
ALU = mybir.AluOpType

ENGINES = ("pe", "act", "dve", "pool", "sp")
N_DMA_SEMS = 20


class Op:
    __slots__ = ("eng", "fn", "deps", "needs_inc", "count", "idx", "dma", "dsem", "dval")

    def __init__(self, eng, fn):
        self.eng = eng
        self.fn = fn
        self.deps = []
        self.needs_inc = False
        self.count = None
        self.idx = None
        self.dma = False
        self.dsem = None
        self.dval = None


class Sched:
    def __init__(self):
        self.streams = {e: [] for e in ENGINES}
        self.state = {}
        self.seen = {e: {} for e in ENGINES}
        self.dma_rr = {e: 0 for e in ENGINES}
        self.dma_last = {e: [None] * N_DMA_SEMS for e in ENGINES}
        self.dma_cnt = {e: [0] * N_DMA_SEMS for e in ENGINES}

    def _add_dep(self, op, d):
        if d is None or d is op:
            return
        if d.dma:
            key = ("d", id(d))
            if key in self.seen[op.eng]:
                return
            self.seen[op.eng][key] = True
            op.deps.append(d)
            return
        if d.eng == op.eng and op.eng == "pe" and not op.dma:
            return
        prev = self.seen[op.eng].get(d.eng, -1)
        if d.idx <= prev:
            return
        self.seen[op.eng][d.eng] = d.idx
        d.needs_inc = True
        op.deps.append(d)

    @staticmethod
    def _is_psum(k):
        return k == "psM" or (isinstance(k, tuple) and k[0] in ("psA", "psS", "psO", "psT"))

    def op(self, eng, fn, reads=(), writes=(), dma=False):
        excl = [k for k in reads if self._is_psum(k)]
        if excl:
            reads = [k for k in reads if not self._is_psum(k)]
            writes = list(writes) + [k for k in excl if k not in writes]
        o = Op(eng, fn)
        o.dma = dma
        o.idx = len(self.streams[eng])
        if dma:
            s = self.dma_rr[eng]
            self.dma_rr[eng] = (s + 1) % N_DMA_SEMS
            prev = self.dma_last[eng][s]
            if prev is not None:
                key = ("d", id(prev))
                if key not in self.seen[eng]:
                    self.seen[eng][key] = True
                    o.deps.append(prev)
            self.dma_cnt[eng][s] += 1
            o.dsem = s
            o.dval = 16 * self.dma_cnt[eng][s]
            self.dma_last[eng][s] = o
        for k in reads:
            st = self.state.get(k)
            if st is not None:
                self._add_dep(o, st[0])
        for k in writes:
            st = self.state.get(k)
            if st is not None:
                self._add_dep(o, st[0])
                for r in st[1]:
                    self._add_dep(o, r)
        for k in reads:
            st = self.state.setdefault(k, [None, []])
            st[1].append(o)
        for k in writes:
            self.state[k] = [o, []]
        self.streams[eng].append(o)
        return o

    def emit(self, nc):
        with ExitStack() as es:
            prog = {e: es.enter_context(nc.semaphore(f"prog_{e}")) for e in ENGINES}
            dsem = {e: [es.enter_context(nc.semaphore(f"dma_{e}_{i}")) for i in range(N_DMA_SEMS)]
                    for e in ("sp", "act", "pool")}
            for e in ENGINES:
                c = 0
                for o in self.streams[e]:
                    if o.needs_inc and not o.dma:
                        c += 1
                        o.count = c
            block = es.enter_context(nc.Block())

            def run(ename, eng):
                for o in self.streams[ename]:
                    for d in o.deps:
                        if d.dma:
                            eng.wait_ge(dsem[d.eng][d.dsem], d.dval)
                        else:
                            eng.wait_ge(prog[d.eng], d.count)
                    ins = o.fn(eng)
                    if o.dma:
                        ins.then_inc(dsem[ename][o.dsem], 16)
                    elif o.needs_inc:
                        ins.then_inc(prog[ename], 1)
                if ename in dsem:
                    for s in range(N_DMA_SEMS):
                        last = self.dma_last[ename][s]
                        if last is not None:
                            eng.wait_ge(dsem[ename][s], last.dval)

            @block.sync
            def _(eng):
                run("sp", eng)

            @block.scalar
            def _(eng):
                run("act", eng)

            @block.vector
            def _(eng):
                run("dve", eng)

            @block.gpsimd
            def _(eng):
                run("pool", eng)

            @block.tensor
            def _(eng):
                run("pe", eng)


D = 1024
NH = 16
HD = 64
FOX_IN = 4112
MLA_QL, MLA_KVL, MLA_R = 384, 256, 32
MLA_DOWN = 672
MLA_SCALE = float(96 ** -0.5)
FOX_SCALE = 0.125
X_SCALE = 1.0 / 16.0
DFF = 2816
NMEM = 256
EPS = 1e-6
NEG = -30000.0
TS = 512
DS = 16
NSEQ = 4


class Cfg:
    def __init__(self, T=4096, P=2048, L=4, ncores=8):
        self.T, self.P, self.L, self.ncores = T, P, L, ncores
        self.skip = set()
        self.stop = None
        self.NF = (L + 1) // 2
        self.NM = L // 2
        assert T % TS == 0 and P % 1024 == 0


class StopBuild(Exception):
    pass


class Tile:
    def __init__(self, kind, j):
        self.kind = kind
        self.j = j
        if kind == "p":
            self.nsub, self.NT, self.pos0 = TS // 128, TS, j * TS
        else:
            self.nsub, self.NT, self.pos0 = 1, 128, 0


class Builder:
    def __init__(self, cfg):
        self.cfg = cfg
        self.nc = bass.Bass("TRN2", target_bir_lowering=False)
        self.S = Sched()
        self.es = ExitStack()
        self.dram = {}
        self.rrA = 0
        self.rrE = 0
        self.rrW = 0
        self.rrStage = 0
        self.rrPT = 0
        self.rrS = 0
        self.rrO = 0
        self.rrKT = 0
        self.rrT = 0
        self.rrAcc = 0
        self.rrKV = 0
        self.kv_w_loaded = {}

    def kq(self, c):
        return ("R", c)

    def kk(self, c):
        return ("R", 8 + c)

    def ko(self, c):
        return ("R", 16 + c)

    def ka(self, f):
        return ("R", f)

    def din(self, name, shape, dt=F32):
        t = self.nc.dram_tensor(name, list(shape), dt, kind="ExternalInput")
        self.dram[name] = t
        return t

    def dout(self, name, shape, dt=F32):
        t = self.nc.dram_tensor(name, list(shape), dt, kind="ExternalOutput")
        self.dram[name] = t
        return t

    def dscr(self, name, shape, dt):
        t = self.nc.dram_tensor(name, list(shape), dt)
        self.dram[name] = t
        return t

    def sb(self, name, shape, dt):
        return self.es.enter_context(self.nc.sbuf_tensor(name, list(shape), dt))

    def ps(self, name, shape, dt):
        return self.es.enter_context(self.nc.psum_tensor(name, list(shape), dt))

    def dma(self, q, out, in_, reads=(), writes=()):
        return self.S.op(q, lambda e: e.dma_start(out=out, in_=in_), reads, writes, dma=True)

    def mm(self, out, lhsT, rhs, start, stop, reads, writes):
        return self.S.op("pe", lambda e: e.matmul(out, lhsT=lhsT, rhs=rhs, start=start, stop=stop),
                         reads, writes)

    def tr(self, out, in_, reads, writes):
        idn = self.ident[0:in_.shape[0], 0:in_.shape[0]]
        return self.S.op("pe", lambda e: e.transpose(out, in_, idn), list(reads) + ["cst"], writes)

    def act(self, out, in_, func, reads, writes, bias=None, scale=None, accum_out=None):
        kw = {}
        if bias is not None:
            kw["bias"] = bias
        if scale is not None:
            kw["scale"] = scale
        if accum_out is not None:
            kw["accum_out"] = accum_out
        return self.S.op("act", lambda e: e.activation(out=out, in_=in_, func=func, **kw), reads, writes)

    def tt(self, out, in0, in1, op, reads, writes, eng="dve"):
        return self.S.op(eng, lambda e: e.tensor_tensor(out=out, in0=in0, in1=in1, op=op), reads, writes)

    def ts(self, out, in0, s1, s2, op0, op1, reads, writes, eng="dve"):
        if op1 is None:
            return self.S.op(eng, lambda e: e.tensor_scalar(out=out, in0=in0, scalar1=s1, scalar2=None, op0=op0),
                             reads, writes)
        return self.S.op(eng, lambda e: e.tensor_scalar(out=out, in0=in0, scalar1=s1, scalar2=s2, op0=op0, op1=op1),
                         reads, writes)

    def stt(self, out, in0, scalar, in1, op0, op1, reads, writes):
        return self.S.op("dve", lambda e: e.scalar_tensor_tensor(out=out, in0=in0, scalar=scalar, in1=in1,
                                                                  op0=op0, op1=op1), reads, writes)

    def cp(self, out, in_, reads, writes, eng="dve"):
        if eng == "act":
            return self.S.op("act", lambda e: e.copy(out=out, in_=in_), reads, writes)
        return self.S.op(eng, lambda e: e.tensor_copy(out=out, in_=in_), reads, writes)

    def evac(self, out, in_, reads, writes):
        self.rrE ^= 1
        return self.cp(out, in_, reads, writes, eng=("act" if self.rrE else "dve"))

    def recip(self, out, in_, reads, writes):
        return self.S.op("dve", lambda e: e.reciprocal(out=out, in_=in_), reads, writes)

    def memset(self, ap, val, writes, eng="pool"):
        return self.S.op(eng, lambda e: e.memset(ap, val), (), writes)

    def nextA(self):
        self.rrA ^= 1
        return self.psA[self.rrA], ("psA", self.rrA)

    def nextS(self):
        self.rrS ^= 1
        return self.psS[self.rrS], ("psS", self.rrS)

    def nextO(self):
        self.rrO ^= 1
        return self.psO[self.rrO], ("psO", self.rrO)

    def nextPT(self):
        self.rrPT = (self.rrPT + 1) % len(self.PT)
        return self.PT[self.rrPT], ("PT", self.rrPT)

    def nextT(self):
        return self.psT[:, 0:512], ("psT", 0)

    def nextStage(self):
        self.rrStage = (self.rrStage + 1) % len(self.stage)
        return self.stage[self.rrStage], ("stage", self.rrStage)

    def bcast_row(self, tname, row_off, n):
        return bass.AP(self.dram[tname], row_off, [[0, 128], [1, n]])

    def conv_weight(self, name, src_name, src_l, K, c0, c1, cb, src_cols):
        KC = K // 128
        M = c1 - c0
        nblk = (M + cb - 1) // cb
        scr = self.dscr(name, [nblk, 128, KC * cb], BF16)
        src = self.dram[src_name].ap()[src_l]
        for b in range(nblk):
            w = min(cb, M - b * cb)
            o = scr.ap()[b].rearrange("p (k c) -> p k c", k=KC)[:, :, 0:w]
            i = src[:, c0 + b * cb: c0 + b * cb + w].rearrange("(k p) c -> p k c", p=128)
            self.dma("pool", o, i, reads=(), writes=[("wb", name, b)])
        return dict(name=name, KC=KC, cb=cb, nblk=nblk, M=M, scr=scr)

    def wplan(self, items):
        self.plan = list(items)
        self.plan_i = 0
        self.plan_loaded = 0
        self.loaded = {}

    def _issue_load(self, i):
        W, b = self.plan[i]
        slot = self.rrW
        self.rrW = (self.rrW + 1) % len(self.wbuf)
        n = W["KC"] * W["cb"]
        w = min(W["cb"], W["M"] - b * W["cb"]) if W["name"][:2] != "wd" else W["cb"]
        kcv = W["KC"]
        if W["name"][:2] == "wd":
            nfb = W["nblk"] // 2
            kcv = min(4, DFF // 128 - (b % nfb) * 4)
        self.dma("sp", self.wbuf[slot][:, 0:n].rearrange("p (k c) -> p k c", k=W["KC"])[:, 0:kcv, 0:w],
                 W["scr"].ap()[b].rearrange("p (k c) -> p k c", k=W["KC"])[:, 0:kcv, 0:w],
                 reads=[("wb", W["name"], b)], writes=[("w", slot)])
        self.loaded[i] = slot

    def wnext(self, W, b):
        i = self.plan_i
        assert self.plan[i][0] is W and self.plan[i][1] == b, (self.plan[i][0]["name"], self.plan[i][1], W["name"], b)
        while self.plan_loaded < min(len(self.plan), i + 2):
            self._issue_load(self.plan_loaded)
            self.plan_loaded += 1
        slot = self.loaded.pop(i)
        self.plan_i += 1
        cb = W["cb"]
        buf = self.wbuf[slot]

        def view(kc, a, b2):
            return buf[:, kc * cb + a: kc * cb + b2]
        return view, ("w", slot)

    def rstd_of(self, src_fn, nsub, n, rkey, reads):
        for s in range(nsub):
            self.act(self.junk[:, 0:n], src_fn(s), AF.Square, reads, ["junk", (rkey, "ss", s)],
                     accum_out=self.ss[:, s:s + 1])
        self.ts(self.ss2[:, 0:nsub], self.ss[:, 0:nsub], 1.0 / n, EPS, ALU.mult, ALU.add,
                [(rkey, "ss", s) for s in range(nsub)], ["ss2"])
        self.S.op("act", lambda e: e.sqrt(out=self.ss2[:, 0:nsub], in_=self.ss2[:, 0:nsub]), ["ss2"], ["ss2"])
        self.recip(self.rstd[:, 0:nsub], self.ss2[:, 0:nsub], ["ss2"], ["rstd"])

    def norm_hT(self, tl, gname, l):
        gk = ("gbc", gname)
        self.dma("sp", self.gbc[:, :], self.bcast_row(gname, l * D, D), writes=[gk])
        self.rstd_of(lambda s: self.x_sb[:, s, :], tl.nsub, D, "x", ["x"])
        for s in range(tl.nsub):
            self.stt(self.h_bf[:, s, :], self.x_sb[:, s, :], self.rstd[:, s:s + 1], self.gbc[:, :],
                     ALU.mult, ALU.mult, ["x", "rstd", gk], [("h_bf", s)])
        self.transpose_to(self.h_bf, [("h_bf", s) for s in range(tl.nsub)], self.hT, "hT", 8, tl)

    def transpose_to(self, src, src_keys, dst, dkey, nchunks, tl, csz=128, dst_off=0, keyfn=None):
        for c in range(nchunks):
            pT, pk = self.nextT()
            for s in range(tl.nsub):
                self.tr(pT[0:csz, s * 128:(s + 1) * 128], src[:, s, c * csz:(c + 1) * csz], src_keys, [pk])
            self.evac(dst[0:csz, c, dst_off:dst_off + tl.NT], pT[0:csz, 0:tl.NT], [pk],
                      [keyfn(c) if keyfn else (dkey, c)])

    def rows_dma_in(self, tl, dst_fn, src_t, ncols, row_base=0, writes=()):
        if tl.kind == "p":
            for s in range(tl.nsub):
                r0 = row_base + tl.pos0 + s * 128
                self.dma("sp", dst_fn(s), src_t[r0:r0 + 128, :], writes=writes)
        else:
            for b in range(NSEQ):
                self.dma("sp", dst_fn(0)[32 * b:32 * b + DS, :], src_t[row_base + b * DS: row_base + (b + 1) * DS, :],
                         writes=writes)

    def rows_dma_out(self, tl, s, src_ap, dst_t, c0, c1, reads, row_base=0):
        if tl.kind == "p":
            r0 = row_base + tl.pos0 + s * 128
            self.dma("sp", dst_t[r0:r0 + 128, c0:c1], src_ap, reads=reads)
        else:
            for b in range(NSEQ):
                self.dma("sp", dst_t[row_base + b * DS: row_base + (b + 1) * DS, c0:c1],
                         src_ap[32 * b:32 * b + DS, :], reads=reads)

    def proj_fm(self, W, nchunks, tl, src, skey, KC, sink):
        per_blk = W["cb"] // 128
        for c in range(nchunks):
            blk, cc = divmod(c, per_blk)
            if cc == 0:
                wv, wk = self.wnext(W, blk)
            ps, pk = self.nextA()
            for kc in range(KC):
                self.mm(ps[:, 0:tl.NT], wv(kc, cc * 128, cc * 128 + 128), src[:, kc, 0:tl.NT],
                        kc == 0, kc == KC - 1, [wk] + [(skey, kc)], [pk])
            sink(c, ps, pk)

    def proj_tm(self, W, tl, src, skey, KC, sink, keyfn=None):
        for blk in range(W["nblk"]):
            wv, wk = self.wnext(W, blk)
            w = min(W["cb"], W["M"] - blk * W["cb"])
            for s in range(tl.nsub):
                ps, pk = self.nextA()
                for kc in range(KC):
                    self.mm(ps[:, 0:w], src[:, kc, s * 128:(s + 1) * 128], wv(kc, 0, w),
                            kc == 0, kc == KC - 1, [wk, keyfn(kc) if keyfn else (skey, kc)], [pk])
                sink(blk, s, ps, pk, w)

    def out_proj_residual(self, W, tl, src, skey, KC, keyfn=None):
        def sink(blk, s, ps, pk, w):
            self.tt(self.x_sb[:, s, blk * 512: blk * 512 + w], ps[:, 0:w], self.x_sb[:, s, blk * 512: blk * 512 + w],
                    ALU.add, [pk, "x"], ["x"])
        self.proj_tm(W, tl, src, skey, KC, sink, keyfn=keyfn)

    def next_acc_pair(self, allow_psA):
        if allow_psA:
            self.rrAcc ^= 1
        else:
            self.rrAcc = 0
        if self.rrAcc == 0:
            return [(self.psO[0], ("psO", 0)), (self.psO[1], ("psO", 1))]
        return [(self.psA[0], ("psA", 0)), (self.psA[1], ("psA", 1))]

    def next_kv(self):
        self.rrKV = (self.rrKV + 1) % len(self.KC)
        return self.rrKV

    def normalize_head(self, psO, ok, r, c, q0, q1, n, gate):
        self.recip(self.rd[64:65, 0:n], psO[64:65, 0:n], [ok], ["rd"])
        self.mm(self.psM[0:64, 0:n], self.ones_f[64:65, 0:64], self.rd[64:65, 0:n], True, True, ["rd", "cst"], ["psM"])
        if gate:
            self.tt(self.bcg[0:64, 0:n], self.psM[0:64, 0:n], self.gT[r * 64:(r + 1) * 64, c, q0:q1], ALU.mult,
                    ["psM", ("gT", c)], ["bcg"])
        else:
            self.cp(self.bcg[0:64, 0:n], self.psM[0:64, 0:n], ["psM"], ["bcg"], eng="act")
        self.tt(self.oT[r * 64:(r + 1) * 64, c, q0:q1], psO[0:64, 0:n], self.bcg[0:64, 0:n], ALU.mult,
                [ok, "bcg"], [self.ko(c)])

    def score_pv(self, c, r, kt, KCs, kcol, kkey, VCs, vt, vkey, psO, ok, q0, qn, n0, mla, mask, first, last):
        h = 2 * c + r
        scale = MLA_SCALE if mla else FOX_SCALE
        psS, sk = self.nextS()
        diag = mask is not None
        self.mm(psS[:, n0:qn], KCs[r * 64:(r + 1) * 64, kcol:kcol + 128], self.qT[r * 64:(r + 1) * 64, c, q0 + n0:q0 + qn],
                True, (not mla) and (not diag), [kkey, self.kq(c)], [sk])
        if mla:
            self.mm(psS[:, n0:qn], self.KR[r * 64:r * 64 + 32, kt * 128:(kt + 1) * 128],
                    self.qR[r * 64:r * 64 + 32, c, q0 + n0:q0 + qn], False, not diag, [("KR", kt // 4), ("gT", c)], [sk])
        if diag:
            self.mm(psS[:, n0:n0 + 128], self.ident[:, :], mask, False, True, ["cst"], [sk])
        PT, ptk = self.nextPT()
        if mla:
            self.act(PT[:, n0:qn], psS[:, n0:qn], AF.Exp, [sk], [ptk], scale=scale)
        else:
            self.act(PT[:, n0:qn], psS[:, n0:qn], AF.Exp, [sk, ("bias", kt)], [ptk],
                     bias=self.bias_all[:, kt, h:h + 1], scale=scale)
        self.mm(psO[0:65, n0:qn], VCs[:, vt, r, 0:65], PT[:, n0:qn], first, last, [vkey, ptk], [ok])

    def attn_prompt_pair(self, tl, c, mla):
        NT = tl.NT
        nkt = (tl.j + 1) * (TS // 128)
        accs = self.next_acc_pair(True)
        mask = self.maskM[:, :] if mla else self.maskF[:, :]
        for g in range((nkt + 7) // 8):
            nt = min(8, nkt - g * 8)
            slot = self.next_kv()
            kkey, vkey = ("KC", slot), ("VC", slot)
            jj = [("KTd", c, j2) for j2 in range(g * 2, min(g * 2 + 2, tl.j + 1))]
            vv = [("Vd", c, j2) for j2 in range(g * 2, min(g * 2 + 2, tl.j + 1))]
            self.dma("sp", self.KC[slot][:, 0:nt * 128], self.KTd.ap()[c][:, g * 1024:g * 1024 + nt * 128],
                     reads=jj, writes=[kkey])
            self.dma("sp", self.VC[slot][:, 0:nt, :, :].rearrange("p t h e -> p t (h e)"),
                     self.Vd.ap()[c][:, g * 8:g * 8 + nt, :], reads=vv, writes=[vkey])
            for r in range(2):
                psO, ok = accs[r]
                for t in range(nt):
                    kt = g * 8 + t
                    d = kt - tl.j * (TS // 128)
                    n0 = 0 if d < 0 else d * 128
                    self.score_pv(c, r, kt, self.KC[slot], t * 128, kkey, self.VC[slot], t, vkey, psO, ok,
                                  0, NT, n0, mla, mask if d >= 0 else None, kt == 0, kt == nkt - 1)
        for r in range(2):
            psO, ok = accs[r]
            self.normalize_head(psO, ok, r, c, 0, NT, NT, gate=not mla)

    def attn_sample_pair(self, l, b, c, mla):
        cfg = self.cfg
        lj = l // 2
        o = self.dram
        q0 = 32 * b
        npast = cfg.P // 128
        accs = self.next_acc_pair(False)
        for g in range(cfg.P // 1024):
            slot = self.next_kv()
            kkey, vkey = ("KC", slot), ("VC", slot)
            if not mla:
                src = o["cache_fox_k"].ap()[lj, b, g * 1024:(g + 1) * 1024, c * 128:(c + 1) * 128].rearrange("(t p) d -> p t d", p=128)
                self.dma("pool", self.kc_bf[:, :, :], src, writes=["kc_bf"])
                for g4 in range(2):
                    pT, pk = self.nextT()
                    for i in range(4):
                        self.tr(pT[:, i * 128:(i + 1) * 128], self.kc_bf[:, g4 * 4 + i, :], ["kc_bf"], [pk])
                    self.evac(self.KC[slot][:, g4 * 512:(g4 + 1) * 512], pT[:, 0:512], [pk], [kkey])
                for r2 in range(2):
                    srcv = o["cache_fox_v"].ap()[lj, b, g * 1024:(g + 1) * 1024, c * 128 + r2 * 64: c * 128 + (r2 + 1) * 64].rearrange(
                        "(t p) d -> p t d", p=128)
                    self.dma("pool", self.VC[slot][:, :, r2, 0:64], srcv, writes=[vkey])
            else:
                for half in range(2):
                    ps, pk = self.nextA()
                    for kc in range(2):
                        self.mm(ps[:, 0:512], self.wkvk[:, kc, c * 128:(c + 1) * 128],
                                self.ckvTs[:, kc, g * 1024 + half * 512: g * 1024 + (half + 1) * 512],
                                kc == 0, kc == 1, ["wkvk", ("hT", kc)], [pk])
                    self.evac(self.KC[slot][:, half * 512:(half + 1) * 512], ps[:, 0:512], [pk], [kkey])
                for t4 in range(2):
                    ps, pk = self.nextA()
                    for i in range(4):
                        t = t4 * 4 + i
                        for kc in range(2):
                            self.mm(ps[:, i * 128:(i + 1) * 128], self.ckvTs[:, kc, (g * 8 + t) * 128:(g * 8 + t + 1) * 128],
                                    self.wkvv[:, kc, c * 128:(c + 1) * 128], kc == 0, kc == 1, ["wkvv", ("hT", kc)], [pk])
                    self.evac(self.VC[slot][:, t4 * 4:(t4 + 1) * 4, :, 0:64],
                              ps[:, 0:512].rearrange("p (t h d) -> p t h d", t=4, d=64), [pk], [vkey])
            for r in range(2):
                psO, ok = accs[r]
                for t in range(8):
                    kt = g * 8 + t
                    self.score_pv(c, r, kt, self.KC[slot], t * 128, kkey, self.VC[slot], t, vkey, psO, ok,
                                  q0, DS, 0, mla, None, kt == 0, False)
        for r in range(2):
            h = 2 * c + r
            psO, ok = accs[r]
            scale = MLA_SCALE if mla else FOX_SCALE
            psS, sk = self.nextS()
            PT, ptk = self.nextPT()
            self.mm(psS[0:DS, 0:DS], self.kT[r * 64:(r + 1) * 64, c, q0:q0 + DS], self.qT[r * 64:(r + 1) * 64, c, q0:q0 + DS],
                    True, False, [self.kk(c), self.kq(c)], [sk])
            if mla:
                self.mm(psS[0:DS, 0:DS], self.krT[r * 64:r * 64 + 32, q0:q0 + DS], self.qR[r * 64:r * 64 + 32, c, q0:q0 + DS],
                        False, True, ["krT", ("gT", c)], [sk])
                self.act(PT[0:DS, 0:DS], psS[0:DS, 0:DS], AF.Exp, [sk], [ptk], scale=scale)
            else:
                self.mm(psS[0:DS, 0:DS], self.ident[0:DS, 0:DS], self.maskF[0:DS, 0:DS], False, True, ["cst"], [sk])
                self.act(PT[0:DS, 0:DS], psS[0:DS, 0:DS], AF.Exp, [sk, "bias_new"], [ptk],
                         bias=self.bias_new[0:DS, h:h + 1], scale=scale)
            self.mm(psO[0:65, 0:DS], self.vnew0[0:DS, h, 0:65], PT[0:DS, 0:DS], False, True, ["vnew0", ptk], [ok])
            self.normalize_head(psO, ok, r, c, q0, q0 + DS, DS, gate=not mla)

    def cumsum_tile(self, lf_ap, lf_key, dst_ap, dst_key):
        self.mm(self.psM[:, 0:16], self.Umat[:, :], lf_ap, True, False, [lf_key, "cst"], ["psM"])
        self.mm(self.psM[:, 0:16], self.ones_f[:, :], self.Sprev[:, :], False, True, ["Sprev", "cst"], ["psM"])
        self.cp(dst_ap, self.psM[:, 0:16], ["psM"], [dst_key])
        self.tt(self.Sprev[:, :], self.Sprev[:, :], lf_ap, ALU.add, ["Sprev", lf_key], ["Sprev"])

    def carry_bcast(self, dst_ap, dst_key):
        self.mm(self.psM[:, 0:16], self.ones_f[:, :], self.Sprev[:, :], True, True, ["Sprev", "cst"], ["psM"])
        self.cp(dst_ap, self.psM[:, 0:16], ["psM"], [dst_key])

    def store_v_tile(self, tl):
        for c in range(8):
            self.dma("sp", self.Vd.ap()[c][:, tl.j * 4:(tl.j + 1) * 4, :],
                     self.vt_bf[:, :, 2 * c:2 * c + 2, :].rearrange("p s h e -> p s (h e)"),
                     reads=[("vt_bf", s) for s in range(4)], writes=[("Vd", c, tl.j)])

    def fox_mixer(self, l, tl):
        cfg = self.cfg
        lj = l // 2
        W = self.W[l]
        NT, nsub = tl.NT, tl.nsub
        o = self.dram
        isp = tl.kind == "p"
        rb = lj * cfg.T if isp else lj * NSEQ * DS
        self.norm_hT(tl, "g_mix", l)
        self.proj_fm(W["q"], 8, tl, self.hT, "hT", 8,
                     lambda c, ps, pk: self.evac(self.qT[:, c, 0:NT], ps[:, 0:NT], [pk], [self.kq(c)]))

        self.phase("fox k")
        def k_sink(blk, s, ps, pk, w):
            st, sk = self.nextStage()
            self.cp(st[:, 0:w], ps[:, 0:w], [pk], [sk], eng="act")
            self.cp(self.h_bf[:, s, blk * 512: blk * 512 + w], ps[:, 0:w], [pk], [("h_bf", s)])
            self.rows_dma_out(tl, s, st[:, 0:w], o["fk_p" if isp else "fk_s"].ap(), blk * 512, blk * 512 + w, [sk], row_base=rb)
        self.proj_tm(W["k"], tl, self.hT, "hT", 8, k_sink)
        self.transpose_to(self.h_bf, [("h_bf", s) for s in range(nsub)], self.kT, None, 8, tl, keyfn=self.kk)
        if isp:
            for c in range(8):
                self.dma("sp", self.KTd.ap()[c][:, tl.pos0:tl.pos0 + NT], self.kT[:, c, 0:NT],
                         reads=[self.kk(c)], writes=[("KTd", c, tl.j)])

        self.phase("fox v")
        def v_sink(blk, s, ps, pk, w):
            st, sk = self.nextStage()
            self.cp(st[:, 0:w], ps[:, 0:w], [pk], [sk], eng="act")
            if isp:
                self.cp(self.vt_bf[:, s, blk * 8:(blk + 1) * 8, 0:64], ps[:, 0:w].rearrange("p (h d) -> p h d", d=64),
                        [pk], [("vt_bf", s)])
            else:
                self.cp(self.vnew[:, blk * 8:(blk + 1) * 8, 0:64], ps[:, 0:w].rearrange("p (h d) -> p h d", d=64),
                        [pk], ["vnew"])
            self.rows_dma_out(tl, s, st[:, 0:w], o["fv_p" if isp else "fv_s"].ap(), blk * 512, blk * 512 + w, [sk], row_base=rb)
        self.proj_tm(W["v"], tl, self.hT, "hT", 8, v_sink)
        if isp:
            self.store_v_tile(tl)

        self.phase("fox f")
        bk = "bfb"
        self.dma("sp", self.bfb[:, :], self.bcast_row("b_fox_f", lj * NH, NH), writes=[bk])
        wv, wk = self.wnext(W["f"], 0)
        for s in range(nsub):
            for kc in range(8):
                self.mm(self.psM[:, 0:16], self.hT[:, kc, s * 128:(s + 1) * 128], wv(kc, 0, 16), kc == 0, kc == 7,
                        [wk, ("hT", kc)], ["psM"])
            self.tt(self.lf[:, s, :], self.psM[:, 0:16], self.bfb[:, :], ALU.add, ["psM", bk], [("lf", s)])
            self.act(self.lf[:, s, :], self.lf[:, s, :], AF.Exp, [("lf", s)], [("lf", s)], scale=-1.0)
            self.act(self.lf[:, s, :], self.lf[:, s, :], AF.Ln, [("lf", s), "one_col"], [("lf", s)], bias=self.one_col[:, 0:1], scale=1.0)
            self.ts(self.lf[:, s, :], self.lf[:, s, :], -1.0, None, ALU.mult, None, [("lf", s)], [("lf", s)])
            self.rows_dma_out(tl, s, self.lf[:, s, :], o["fl_p" if isp else "fl_s"].ap(), 0, NH, [("lf", s)], row_base=rb)

        self.phase("fox g")
        self.proj_fm(W["g"], 8, tl, self.hT, "hT", 8,
                     lambda c, ps, pk: self.act(self.gT[:, c, 0:NT], ps[:, 0:NT], AF.Sigmoid, [pk], [("gT", c)]))

        self.phase("fox attn")
        if isp:
            nkt = (tl.j + 1) * 4
            for s in range(nsub):
                kt = tl.j * 4 + s
                if s == 2:
                    self.carry_bcast(self.cref[:, :], "cref")
                self.cumsum_tile(self.lf[:, s, :], ("lf", s), self.c_all[:, kt, :], ("c_all", kt))
            for kt in range(nkt):
                self.tt(self.bias_all[:, kt, :], self.cref[:, :], self.c_all[:, kt, :], ALU.subtract,
                        ["cref", ("c_all", kt)], [("bias", kt)])
            for c in range(8):
                self.attn_prompt_pair(tl, c, False)
        else:
            npast = cfg.P // 128
            for c in range(8):
                self.memset(self.oT[:, c, 0:128], 0.0, [self.ko(c)])
            for b in range(NSEQ):
                self.dma("sp", self.lfs[:, 0:npast, :],
                         o["cache_fox_logf"].ap()[lj, b].rearrange("(t p) h -> p t h", p=128), writes=["lfs"])
                self.memset(self.Sprev[:, :], 0.0, ["Sprev"])
                for kt in range(npast):
                    self.cumsum_tile(self.lfs[:, kt, :], "lfs", self.c_all[:, kt, :], ("c_all", kt))
                self.carry_bcast(self.cref[:, :], "cref")
                for kt in range(npast):
                    self.tt(self.bias_all[:, kt, :], self.cref[:, :], self.c_all[:, kt, :], ALU.subtract,
                            ["cref", ("c_all", kt)], [("bias", kt)])
                self.phase(f"fox s-attn b{b} newbias")
                self.cp(self.lfn0[0:DS, :], self.lf[32 * b:32 * b + DS, 0, :], [("lf", 0)], ["lfn0"])
                self.mm(self.psM[0:DS, 0:16], self.Umat[0:DS, 0:DS], self.lfn0[0:DS, :], True, True, ["lfn0", "cst"], ["psM"])
                self.ts(self.bias_new[0:DS, :], self.psM[0:DS, 0:16], -1.0, None, ALU.mult, None, ["psM"], ["bias_new"])
                self.cp(self.vnew0[0:DS, :, :], self.vnew[32 * b:32 * b + DS, :, :], ["vnew"], ["vnew0"])
                self.phase(f"fox s-attn b{b} pairs")
                for c in range(8):
                    self.attn_sample_pair(l, b, c, False)
        self.phase("fox out")
        self.out_proj_residual(W["o"], tl, self.oT, None, 8, keyfn=self.ko)

    def mla_mixer(self, l, tl):
        cfg = self.cfg
        lj = l // 2
        W = self.W[l]
        NT, nsub = tl.NT, tl.nsub
        o = self.dram
        isp = tl.kind == "p"
        rb = lj * cfg.T if isp else lj * NSEQ * DS
        self.norm_hT(tl, "g_mix", l)
        if not self.kv_w_loaded.get(l):
            self.kv_w_loaded[l] = True
            self.dma("sp", self.wkvk[:, :, :].rearrange("p k c -> p (k c)"), W["kvk"]["scr"].ap()[0],
                     reads=[("wb", W["kvk"]["name"], 0)], writes=["wkvk"])
            self.dma("sp", self.wkvv[:, :, :].rearrange("p k c -> p (k c)"), W["kvv"]["scr"].ap()[0],
                     reads=[("wb", W["kvv"]["name"], 0)], writes=["wkvv"])

        def a_sink(blk, s, ps, pk, w):
            self.evac(self.a_sb[:, s, blk * 512: blk * 512 + w], ps[:, 0:w], [pk], [("a_sb", s, blk)])
        self.proj_tm(W["a"], tl, self.hT, "hT", 8, a_sink)
        akeys = [("a_sb", s, blk) for s in range(nsub) for blk in range(2)]
        self.dma("sp", self.gq[:, :], self.bcast_row("g_mla_q", lj * MLA_QL, MLA_QL), writes=["gq"])
        self.dma("sp", self.gkv[:, :], self.bcast_row("g_mla_kv", lj * MLA_KVL, MLA_KVL), writes=["gkv"])
        self.rstd_of(lambda s: self.a_sb[:, s, 0:MLA_QL], nsub, MLA_QL, "cq", akeys)
        for s in range(nsub):
            self.stt(self.cq_bf[:, s, :], self.a_sb[:, s, 0:MLA_QL], self.rstd[:, s:s + 1], self.gq[:, :],
                     ALU.mult, ALU.mult, akeys + ["rstd", "gq"], [("cq_bf", s)])
        self.transpose_to(self.cq_bf, [("cq_bf", s) for s in range(nsub)], self.cqT, "cqT", 3, tl)
        self.rstd_of(lambda s: self.a_sb[:, s, MLA_QL:MLA_QL + MLA_KVL], nsub, MLA_KVL, "ckv", akeys)
        for s in range(nsub):
            st, sk = self.nextStage()
            self.stt(st[:, 0:MLA_KVL], self.a_sb[:, s, MLA_QL:MLA_QL + MLA_KVL], self.rstd[:, s:s + 1], self.gkv[:, :],
                     ALU.mult, ALU.mult, akeys + ["rstd", "gkv"], [sk])
            self.cp(self.ckv_bf[:, s, :], st[:, 0:MLA_KVL], [sk], [("ckv_bf", s)], eng="act")
            self.rows_dma_out(tl, s, st[:, 0:MLA_KVL], o["mc_p" if isp else "mc_s"].ap(), 0, MLA_KVL, [sk], row_base=rb)
        self.transpose_to(self.ckv_bf, [("ckv_bf", s) for s in range(nsub)], self.ckvT, "ckvT", 2, tl)
        if isp:
            self.dma("sp", self.rtm[:, 0:nsub, :],
                     o["rope_tm_p"].ap()[tl.pos0:tl.pos0 + NT, :].rearrange("(s p) c -> p s c", p=128), writes=["rtm"])
        else:
            self.dma("sp", self.rtm[:, 0, :], o["rope_tm_s"].ap(), writes=["rtm"])
        A0 = MLA_QL + MLA_KVL
        for s in range(nsub):
            x1 = self.a_sb[:, s, A0:A0 + 16]
            x2 = self.a_sb[:, s, A0 + 16:A0 + 32]
            cs = self.rtm[:, s, 0:16]
            sn = self.rtm[:, s, 16:32]
            st, sk = self.nextStage()
            rk = akeys + ["rtm"]
            self.tt(st[:, 0:16], x1, cs, ALU.mult, rk, [sk])
            self.tt(st[:, 32:48], x2, sn, ALU.mult, rk, [sk])
            self.tt(st[:, 0:16], st[:, 0:16], st[:, 32:48], ALU.subtract, [sk], [sk])
            self.tt(st[:, 16:32], x1, sn, ALU.mult, rk, [sk])
            self.tt(st[:, 32:48], x2, cs, ALU.mult, rk, [sk])
            self.tt(st[:, 16:32], st[:, 16:32], st[:, 32:48], ALU.add, [sk], [sk])
            self.cp(self.kr_bf[:, s, :], st[:, 0:32], [sk], [("kr_bf", s)], eng="act")
            self.rows_dma_out(tl, s, st[:, 0:32], o["mr_p" if isp else "mr_s"].ap(), 0, MLA_R, [sk], row_base=rb)
        pT, pk = self.nextT()
        for s in range(nsub):
            self.tr(pT[0:32, s * 128:(s + 1) * 128], self.kr_bf[:, s, :], [("kr_bf", s)], [pk])
        for r in range(2):
            if isp:
                self.evac(self.KR[r * 64:r * 64 + 32, tl.pos0:tl.pos0 + NT], pT[0:32, 0:NT], [pk], [("KR", tl.j)])
            else:
                self.evac(self.krT[r * 64:r * 64 + 32, 0:NT], pT[0:32, 0:NT], [pk], ["krT"])

        if isp:
            for i in range(2):
                self.dma("sp", self.rfm[64:96, i, 0:NT], o["rope_fm_p"].ap()[i, :, tl.pos0:tl.pos0 + NT], writes=["rfm"])
        else:
            for i in range(2):
                self.dma("sp", self.rfm[64:96, i, 0:NT], o["rope_fm_s"].ap()[i], writes=["rfm"])
        Wq = W["qb"]
        for h in range(NH):
            blk, hh = divmod(h, 4)
            if hh == 0:
                wv, wk = self.wnext(Wq, blk)
            c, r = divmod(h, 2)
            ps, pk = self.nextA()
            for kc in range(3):
                self.mm(ps[0:96, 0:NT], wv(kc, hh * 96, hh * 96 + 96), self.cqT[:, kc, 0:NT], kc == 0, kc == 2,
                        [wk, ("cqT", kc)], [pk])
            self.cp(self.qT[r * 64:(r + 1) * 64, c, 0:NT], ps[0:64, 0:NT], [pk], [self.kq(c)], eng="act")
            self.cp(self.xr[64:96, 0:NT], ps[64:96, 0:NT], [pk], ["xr"], eng="act")
            self.mm(self.psM[64:96, 0:NT], self.Rrot[64:96, 0:32], self.xr[64:96, 0:NT], True, True, ["xr", "cst"], ["psM"])
            self.tt(self.t1[64:96, 0:NT], ps[64:96, 0:NT], self.rfm[64:96, 0, 0:NT], ALU.mult, [pk, "rfm"], ["t1"])
            self.tt(self.t2[64:96, 0:NT], self.psM[64:96, 0:NT], self.rfm[64:96, 1, 0:NT], ALU.mult, ["psM", "rfm"], ["t2"])
            self.tt(self.qR[r * 64:r * 64 + 32, c, 0:NT], self.t1[64:96, 0:NT], self.t2[64:96, 0:NT], ALU.add,
                    ["t1", "t2"], [("gT", c)])

        for c in range(8):
            ps, pk = self.nextA()
            for kc in range(2):
                self.mm(ps[:, 0:NT], self.wkvk[:, kc, c * 128:(c + 1) * 128], self.ckvT[:, kc, 0:NT], kc == 0, kc == 1,
                        ["wkvk", ("ckvT", kc)], [pk])
            self.evac(self.kT[:, c, 0:NT], ps[:, 0:NT], [pk], [self.kk(c)])
            if isp:
                self.dma("sp", self.KTd.ap()[c][:, tl.pos0:tl.pos0 + NT], self.kT[:, c, 0:NT],
                         reads=[self.kk(c)], writes=[("KTd", c, tl.j)])
        for s in range(nsub):
            for half in range(2):
                ps, pk = self.nextA()
                for kc in range(2):
                    self.mm(ps[:, 0:512], self.ckvT[:, kc, s * 128:(s + 1) * 128], self.wkvv[:, kc, half * 512:(half + 1) * 512],
                            kc == 0, kc == 1, ["wkvv", ("ckvT", kc)], [pk])
                if isp:
                    self.evac(self.vt_bf[:, s, half * 8:(half + 1) * 8, 0:64],
                              ps[:, 0:512].rearrange("p (h d) -> p h d", d=64), [pk], [("vt_bf", s)])
                else:
                    self.evac(self.vnew[:, half * 8:(half + 1) * 8, 0:64],
                              ps[:, 0:512].rearrange("p (h d) -> p h d", d=64), [pk], ["vnew"])
        if isp:
            self.store_v_tile(tl)
            for c in range(8):
                self.attn_prompt_pair(tl, c, True)
        else:
            npast = cfg.P // 128
            for c in range(8):
                self.memset(self.oT[:, c, 0:128], 0.0, [self.ko(c)])
            for b in range(NSEQ):
                self.dma("pool", self.ckvc[:, 0:npast, :],
                         o["cache_mla_ckv"].ap()[lj, b].rearrange("(t p) d -> p t d", p=128),
                         writes=[("h_bf", s) for s in range(4)])
                for kc in range(2):
                    for g in range(0, npast, 4):
                        pT, pk = self.nextT()
                        for i in range(4):
                            self.tr(pT[:, i * 128:(i + 1) * 128], self.ckvc[:, g + i, kc * 128:(kc + 1) * 128],
                                    [("h_bf", s) for s in range(4)], [pk])
                        self.evac(self.ckvTs[:, kc, g * 128:(g + 4) * 128], pT[:, 0:512], [pk], [("hT", kk2) for kk2 in range(8)])
                self.dma("pool", self.krc[:, 0:npast, :],
                         o["cache_mla_krope"].ap()[lj, b].rearrange("(t p) d -> p t d", p=128), writes=["krc"])
                for g in range(0, npast, 4):
                    pT, pk = self.nextT()
                    for i in range(4):
                        self.tr(pT[0:32, i * 128:(i + 1) * 128], self.krc[:, g + i, :], ["krc"], [pk])
                    for r in range(2):
                        self.evac(self.KR[r * 64:r * 64 + 32, g * 128:(g + 4) * 128], pT[0:32, 0:512], [pk], [("KR", g // 4)])
                self.cp(self.vnew0[0:DS, :, :], self.vnew[32 * b:32 * b + DS, :, :], ["vnew"], ["vnew0"])
                for c in range(8):
                    self.attn_sample_pair(l, b, c, True)
        self.out_proj_residual(W["o"], tl, self.oT, None, 8, keyfn=self.ko)

    def load_mem_kv(self, ksrc, vsrc, reads):
        self.dma("pool", self.h_bf[:, 0:2, :], ksrc.rearrange("(t p) d -> p t d", p=128), reads=reads,
                 writes=[("h_bf", 0), ("h_bf", 1)])
        self.dma("pool", self.MV[:, :, :], vsrc.rearrange("(t p) d -> p t d", p=128), reads=reads, writes=["MV"])
        for g in range(0, 8, 2):
            pT, pk = self.nextT()
            for i in range(2):
                for t in range(2):
                    self.tr(pT[:, i * 256 + t * 128: i * 256 + (t + 1) * 128], self.h_bf[:, t, (g + i) * 128:(g + i + 1) * 128],
                            [("h_bf", 0), ("h_bf", 1)], [pk])
            self.evac(self.MKT[:, g:g + 2, :], pT[:, 0:512].rearrange("p (i k) -> p i k", i=2), [pk], ["MKT"])

    def cross_heads(self, n0, n1):
        for hx in range(4):
            pts = []
            for mkt in range(2):
                psS, sk = self.nextS()
                for dc in range(2):
                    self.mm(psS[:, n0:n1], self.MKT[:, hx * 2 + dc, mkt * 128:(mkt + 1) * 128], self.qT[:, hx * 2 + dc, n0:n1],
                            dc == 0, dc == 1, ["MKT", self.kq(hx * 2 + dc)], [sk])
                PT, ptk = self.nextPT()
                self.act(PT[:, n0:n1], psS[:, n0:n1], AF.Exp, [sk], [ptk], scale=X_SCALE)
                pts.append((PT, ptk))
            for mkt in range(2):
                self.mm(self.psM[:, n0:n1], self.ones_b[:, :], pts[mkt][0][:, n0:n1], mkt == 0, mkt == 1,
                        [pts[mkt][1], "cst"], ["psM"])
            self.recip(self.rden[:, n0:n1], self.psM[:, n0:n1], ["psM"], ["bcg"])
            for dc in range(2):
                psO, ok = self.nextO()
                for mkt in range(2):
                    self.mm(psO[:, n0:n1], self.MV[:, mkt, hx * 256 + dc * 128: hx * 256 + (dc + 1) * 128],
                            pts[mkt][0][:, n0:n1], mkt == 0, mkt == 1, ["MV", pts[mkt][1]], [ok])
                self.tt(self.oT[:, hx * 2 + dc, n0:n1], psO[:, n0:n1], self.rden[:, n0:n1], ALU.mult,
                        [ok, "bcg"], [self.ko(hx * 2 + dc)])

    def cross_attn(self, l, tl):
        o = self.dram
        W = self.W[l]
        NT = tl.NT
        self.norm_hT(tl, "g_cross", l)
        self.proj_fm(W["xq"], 8, tl, self.hT, "hT", 8,
                     lambda c, ps, pk: self.evac(self.qT[:, c, 0:NT], ps[:, 0:NT], [pk], [self.kq(c)]))
        if tl.kind == "p":
            if tl.j == 0:
                self.load_mem_kv(o["memk_p"].ap()[l], o["memv_p"].ap()[l], [("memkv_out", l)])
            self.cross_heads(0, NT)
        else:
            for c in range(8):
                self.memset(self.oT[:, c, 0:128], 0.0, [self.ko(c)])
            for b in range(NSEQ):
                self.load_mem_kv(o["cache_mem_k"].ap()[l, b], o["cache_mem_v"].ap()[l, b], [])
                self.cross_heads(32 * b, 32 * b + DS)
        self.out_proj_residual(W["xo"], tl, self.oT, None, 8, keyfn=self.ko)

    def ffn(self, l, tl):
        W = self.W[l]
        NT = tl.NT
        self.norm_hT(tl, "g_ffn", l)
        NFC = DFF // 128
        for grp in range(0, NFC, 4):
            n = min(4, NFC - grp)
            gv, gk = self.wnext(W["gu_g"], grp // 4)
            uv, uk = self.wnext(W["gu_u"], grp // 4)
            for i in range(n):
                f = grp + i
                psg, pgk = self.nextA()
                for kc in range(8):
                    self.mm(psg[:, 0:NT], gv(kc, i * 128, i * 128 + 128), self.hT[:, kc, 0:NT], kc == 0, kc == 7,
                            [gk, ("hT", kc)], [pgk])
                psu, puk = self.nextS()
                for kc in range(8):
                    self.mm(psu[:, 0:NT], uv(kc, i * 128, i * 128 + 128), self.hT[:, kc, 0:NT], kc == 0, kc == 7,
                            [uk, ("hT", kc)], [puk])
                sg, sgk = self.nextStage()
                self.act(sg[:, 0:NT], psg[:, 0:NT], AF.Silu, [pgk], [sgk])
                self.tt(self.aT[:, f, 0:NT], psu[:, 0:NT], sg[:, 0:NT], ALU.mult, [puk, sgk], [self.ka(f)])
        Wd = W["down"]
        for half in range(2):
            accs = [(self.psA[0], ("psA", 0)), (self.psA[1], ("psA", 1)), (self.psS[0], ("psS", 0)), (self.psS[1], ("psS", 1))]
            for fb in range(Wd["nblk"] // 2):
                dv, dk = self.wnext(Wd, half * (Wd["nblk"] // 2) + fb)
                nf = min(4, NFC - fb * 4)
                for s in range(tl.nsub):
                    ps, pk = accs[s]
                    for i in range(nf):
                        f = fb * 4 + i
                        self.mm(ps[:, 0:512], self.aT[:, f, s * 128:(s + 1) * 128], dv(i, 0, 512), f == 0, f == NFC - 1,
                                [dk, self.ka(f)], [pk])
            for s in range(tl.nsub):
                ps, pk = accs[s]
                self.tt(self.x_sb[:, s, half * 512:(half + 1) * 512], ps[:, 0:512], self.x_sb[:, s, half * 512:(half + 1) * 512],
                        ALU.add, [pk, "x"], ["x"])

    def memory_kv(self):
        cfg = self.cfg
        o = self.dram
        tl = Tile("p", 0)
        tl.nsub, tl.NT = 2, 256
        for s in range(2):
            self.dma("sp", self.x_sb[:, s, :], o["mem_prompt"].ap()[s * 128:(s + 1) * 128, :], writes=["x"])
        self.rstd_of(lambda s: self.x_sb[:, s, :], 2, D, "x", ["x"])
        for l in range(cfg.L):
            gk = ("gbc", "g_mem")
            self.dma("sp", self.gbc[:, :], self.bcast_row("g_mem", l * D, D), writes=[gk])
            for s in range(2):
                self.stt(self.h_bf[:, s, :], self.x_sb[:, s, :], self.rstd[:, s:s + 1], self.gbc[:, :],
                         ALU.mult, ALU.mult, ["x", "rstd", gk], [("h_bf", s)])
            self.transpose_to(self.h_bf, [("h_bf", s) for s in range(2)], self.hT, "hT", 8, tl)
            for blk in range(4):
                slot = self.rrW
                self.rrW = (self.rrW + 1) % len(self.wbuf)
                src = o["w_x_kv"].ap()[l][:, blk * 512:(blk + 1) * 512].rearrange("(k p) c -> p k c", p=128)
                self.dma("pool", self.wbuf[slot][:, 0:4096].rearrange("p (k c) -> p k c", k=8), src, writes=[("w", slot)])
                for s in range(2):
                    ps, pk = self.nextA()
                    for kc in range(8):
                        self.mm(ps[:, 0:512], self.hT[:, kc, s * 128:(s + 1) * 128],
                                self.wbuf[slot][:, kc * 512:(kc + 1) * 512], kc == 0, kc == 7, [("w", slot), ("hT", kc)], [pk])
                    st, sk = self.nextStage()
                    self.evac(st[:, 0:512], ps[:, 0:512], [pk], [sk])
                    dst = o["memk_p"] if blk < 2 else o["memv_p"]
                    cb = (blk % 2) * 512
                    self.dma("sp", dst.ap()[l][s * 128:(s + 1) * 128, cb:cb + 512], st[:, 0:512], reads=[sk],
                             writes=[("memkv_out", l)])

    def weight_plan(self, l):
        W = self.W[l]
        p = []
        if l % 2 == 0:
            p += [(W["q"], 0), (W["q"], 1), (W["k"], 0), (W["k"], 1), (W["v"], 0), (W["v"], 1), (W["f"], 0),
                  (W["g"], 0), (W["g"], 1), (W["o"], 0), (W["o"], 1)]
        else:
            p += [(W["a"], 0), (W["a"], 1)] + [(W["qb"], i) for i in range(4)] + [(W["o"], 0), (W["o"], 1)]
        if "cross" not in self.cfg.skip:
            p += [(W["xq"], 0), (W["xq"], 1), (W["xo"], 0), (W["xo"], 1)]
        if "ffn" not in self.cfg.skip:
            for g in range(6):
                p += [(W["gu_g"], g), (W["gu_u"], g)]
            nb = W["down"]["nblk"]
            p += [(W["down"], i) for i in range(nb)]
        return p

    def convert_layer(self, l):
        cw = self.conv_weight
        W = {}
        lj = l // 2
        if l % 2 == 0:
            W["q"] = cw(f"wq{l}", "w_fox_in", lj, D, 0, 1024, 512, FOX_IN)
            W["k"] = cw(f"wk{l}", "w_fox_in", lj, D, 1024, 2048, 512, FOX_IN)
            W["v"] = cw(f"wv{l}", "w_fox_in", lj, D, 2048, 3072, 512, FOX_IN)
            W["f"] = cw(f"wf{l}", "w_fox_in", lj, D, 3072, 3088, 16, FOX_IN)
            W["g"] = cw(f"wg{l}", "w_fox_in", lj, D, 3088, 4112, 512, FOX_IN)
            W["o"] = cw(f"wo{l}", "w_fox_out", lj, D, 0, 1024, 512, D)
        else:
            W["a"] = cw(f"wa{l}", "w_mla_a", lj, D, 0, MLA_DOWN, 512, MLA_DOWN)
            W["qb"] = cw(f"wqb{l}", "w_mla_qb", lj, MLA_QL, 0, 1536, 384, 1536)
            for nm, off in (("kvk", 0), ("kvv", 64)):
                scr = self.dscr(f"w{nm}{l}", [1, 128, 2 * 1024], BF16)
                srcw = self.dram["w_mla_kvb"].ap()[lj].rearrange("(k p) (h e) -> p k h e", p=128, e=128)
                dstw = scr.ap()[0].rearrange("p (k h d) -> p k h d", k=2, d=64)
                for kc in range(2):
                    self.dma("pool", dstw[:, kc, :, :], srcw[:, kc, :, off:off + 64], writes=[("wb", f"w{nm}{l}", 0)])
                W[nm] = dict(name=f"w{nm}{l}", KC=2, cb=1024, nblk=1, M=1024, scr=scr)
            W["o"] = cw(f"wo{l}", "w_mla_out", lj, D, 0, 1024, 512, D)
        W["xq"] = cw(f"wxq{l}", "w_x_q", l, D, 0, 1024, 512, D)
        W["xo"] = cw(f"wxo{l}", "w_x_o", l, D, 0, 1024, 512, D)
        W["gu_g"] = cw(f"wgg{l}", "w_ffn_gu", l, D, 0, DFF, 512, 2 * DFF)
        W["gu_u"] = cw(f"wgu{l}", "w_ffn_gu", l, D, DFF, 2 * DFF, 512, 2 * DFF)
        name = f"wd{l}"
        nfb = (DFF // 128 + 3) // 4
        scr = self.dscr(name, [2 * nfb, 128, 4 * 512], BF16)
        src = self.dram["w_ffn_down"].ap()[l]
        for half in range(2):
            for fb in range(nfb):
                nf = min(4, DFF // 128 - fb * 4)
                i = src[fb * 512: fb * 512 + nf * 128, half * 512:(half + 1) * 512].rearrange("(k p) c -> p k c", p=128)
                ov = scr.ap()[half * nfb + fb].rearrange("p (k c) -> p k c", k=4)[:, 0:nf, :]
                self.dma("pool", ov, i, writes=[("wb", name, half * nfb + fb)])
        W["down"] = dict(name=name, KC=4, cb=512, nblk=2 * nfb, M=1024, scr=scr)
        self.W[l] = W

    def phase(self, name):
        self.nphase = getattr(self, "nphase", 0) + 1
        if self.cfg.stop is not None and self.nphase > self.cfg.stop:
            print("STOP before phase", self.nphase, name)
            raise StopBuild()
        if self.cfg.stop is not None:
            print("phase", self.nphase, name)

    def layer_tile(self, l, tl):
        cfg = self.cfg
        o = self.dram
        last = l == cfg.L - 1
        self.phase(f"L{l} {tl.kind}{tl.j} load+mixer")
        if tl.kind == "s":
            self.memset(self.x_sb[:, 0, :], 0.0, ["x"])
            src = o["x_sample"] if l == 0 else o["xres_s"]
            self.rows_dma_in(tl, lambda s: self.x_sb[:, s, :], src.ap(), D, writes=["x"])
        else:
            src = o["x_prompt"] if l == 0 else o["xres_p"]
            rd = [] if l == 0 else [("xres_p", tl.j)]
            for s in range(tl.nsub):
                r0 = tl.pos0 + s * 128
                self.dma("sp", self.x_sb[:, s, :], src.ap()[r0:r0 + 128, :], reads=rd, writes=["x"])
        self.wplan(self.weight_plan(l))
        if l % 2 == 0:
            self.fox_mixer(l, tl)
        else:
            self.mla_mixer(l, tl)
        if "cross" not in cfg.skip:
            self.phase(f"L{l} {tl.kind}{tl.j} cross")
            self.cross_attn(l, tl)
        if "ffn" not in cfg.skip:
            self.phase(f"L{l} {tl.kind}{tl.j} ffn")
            self.ffn(l, tl)
        assert self.plan_i == len(self.plan)
        if not last:
            if tl.kind == "s":
                self.rows_dma_out(tl, 0, self.x_sb[:, 0, :], o["xres_s"].ap(), 0, D, ["x"])
            else:
                for s in range(tl.nsub):
                    r0 = tl.pos0 + s * 128
                    self.dma("sp", o["xres_p"].ap()[r0:r0 + 128, :], self.x_sb[:, s, :], reads=["x"],
                             writes=[("xres_p", tl.j)])
        else:
            gk = ("gbc", "g_final")
            self.dma("sp", self.gbc[:, :], self.bcast_row("g_final", 0, D), writes=[gk])
            self.rstd_of(lambda s: self.x_sb[:, s, :], tl.nsub, D, "x", ["x"])
            for s in range(tl.nsub):
                for half in range(2):
                    st, sk = self.nextStage()
                    self.stt(st[:, 0:512], self.x_sb[:, s, half * 512:(half + 1) * 512], self.rstd[:, s:s + 1],
                             self.gbc[:, half * 512:(half + 1) * 512], ALU.mult, ALU.mult, ["x", "rstd", gk], [sk])
                    self.rows_dma_out(tl, s, st[:, 0:512], o["y_p" if tl.kind == "p" else "y_s"].ap(),
                                      half * 512, (half + 1) * 512, [sk])

    def build(self):
        cfg = self.cfg
        T, P, L, NF, NM = cfg.T, cfg.P, cfg.L, cfg.NF, cfg.NM
        din, dout = self.din, self.dout
        din("x_prompt", [T, D]); din("x_sample", [NSEQ * DS, D]); din("mem_prompt", [NMEM, D])
        din("cache_fox_k", [NF, NSEQ, P, D]); din("cache_fox_v", [NF, NSEQ, P, D]); din("cache_fox_logf", [NF, NSEQ, P, NH])
        din("cache_mla_ckv", [max(NM, 1), NSEQ, P, MLA_KVL]); din("cache_mla_krope", [max(NM, 1), NSEQ, P, MLA_R])
        din("cache_mem_k", [L, NSEQ, NMEM, D]); din("cache_mem_v", [L, NSEQ, NMEM, D])
        din("g_mix", [L * D]); din("g_cross", [L * D]); din("g_mem", [L * D]); din("g_ffn", [L * D]); din("g_final", [D])
        din("w_fox_in", [NF, D, FOX_IN]); din("b_fox_f", [NF * NH]); din("w_fox_out", [NF, D, D])
        din("w_mla_a", [max(NM, 1), D, MLA_DOWN]); din("g_mla_q", [max(NM, 1) * MLA_QL]); din("g_mla_kv", [max(NM, 1) * MLA_KVL])
        din("w_mla_qb", [max(NM, 1), MLA_QL, 1536]); din("w_mla_kvb", [max(NM, 1), MLA_KVL, 2048]); din("w_mla_out", [max(NM, 1), D, D])
        din("w_x_q", [L, D, D]); din("w_x_kv", [L, D, 2 * D]); din("w_x_o", [L, D, D])
        din("w_ffn_gu", [L, D, 2 * DFF]); din("w_ffn_down", [L, DFF, D])
        din("cst_bf", [128, 640], BF16); din("cst_f", [128, 256])
        din("rope_tm_p", [T, 32]); din("rope_tm_s", [128, 32]); din("rope_fm_p", [2, 32, T]); din("rope_fm_s", [2, 32, 128])
        dout("y_p", [T, D]); dout("y_s", [NSEQ * DS, D])
        dout("fk_p", [NF * T, D]); dout("fv_p", [NF * T, D]); dout("fl_p", [NF * T, NH])
        dout("mc_p", [max(NM, 1) * T, MLA_KVL]); dout("mr_p", [max(NM, 1) * T, MLA_R])
        dout("memk_p", [L, NMEM, D]); dout("memv_p", [L, NMEM, D])
        dout("fk_s", [NF * NSEQ * DS, D]); dout("fv_s", [NF * NSEQ * DS, D]); dout("fl_s", [NF * NSEQ * DS, NH])
        dout("mc_s", [max(NM, 1) * NSEQ * DS, MLA_KVL]); dout("mr_s", [max(NM, 1) * NSEQ * DS, MLA_R])
        self.dscr("xres_p", [T, D], F32); self.dscr("xres_s", [NSEQ * DS, D], F32)
        self.KTd = self.dscr("KTd", [8, 128, T], BF16)

        sb, ps = self.sb, self.ps
        NKT = max(T, P) // 128
        self.Vd = self.dscr("Vd", [8, 128, T // 128, 130], BF16)
        self.x_sb = sb("x_sb", [128, 4, D], F32)
        self.h_bf = sb("h_bf", [128, 4, D], BF16)
        self.ckvc = self.h_bf[:, :, :].rearrange("p s d -> p (s d)")[:, 0:(P // 128) * MLA_KVL].rearrange("p (t d) -> p t d", d=MLA_KVL)
        self.hT = sb("hT", [128, 8, TS], BF16)
        self.ckvTs = self.hT[:, :, :].rearrange("p c n -> p (c n)")[:, 0:2 * P].rearrange("p (k n) -> p k n", k=2)
        assert (P // 128) * MLA_KVL <= 4 * D and 2 * P <= 8 * TS
        self.R = sb("R", [128, 24, TS], BF16)
        self.qT = self.R[:, 0:8, :]
        self.kT = self.R[:, 8:16, :]
        self.oT = self.R[:, 16:24, :]
        self.aT = self.R[:, 0:22, :]
        self.gT = sb("gT", [128, 8, TS], BF16)
        self.qR = self.gT
        self.vt_bf = sb("vt_bf", [128, 4, NH, 65], BF16)
        self.KC = [sb(f"KC{i}", [128, 1024], BF16) for i in range(3)]
        self.VC = [sb(f"VC{i}", [128, 8, 2, 65], BF16) for i in range(3)]
        self.KR = sb("KR", [128, max(T, P)], BF16)
        self.wbuf = [sb(f"wbuf{i}", [128, 4096], BF16) for i in range(3)]
        self.wkvk = sb("wkvk", [128, 2, 1024], BF16); self.wkvv = sb("wkvv", [128, 2, 1024], BF16)
        self.stage = [sb(f"stage{i}", [128, 512], F32) for i in range(4)]
        self.PT = [sb(f"PT{i}", [128, 512], BF16) for i in range(4)]
        self.gbc = sb("gbc", [128, D], F32)
        self.junk = sb("junk", [128, D], BF16)
        self.ss = sb("ss", [128, 4], F32); self.ss2 = sb("ss2", [128, 4], F32); self.rstd = sb("rstd", [128, 4], F32)
        self.one_col = sb("one_col", [128, 1], F32)
        self.cst_bf = sb("cst_bf_sb", [128, 640], BF16); self.cst_f = sb("cst_f_sb", [128, 256], F32)
        self.ident = self.cst_bf[:, 0:128]; self.maskF = self.cst_bf[:, 128:256]; self.maskM = self.cst_bf[:, 256:384]
        self.ones_b = self.cst_bf[:, 384:512]; self.Rrot = self.cst_bf[:, 512:544]
        self.Umat = self.cst_f[:, 0:128]; self.ones_f = self.cst_f[:, 128:256]
        self.rd = sb("rd", [128, 512], F32); self.bcg = sb("bcg", [128, 512], F32); self.rden = self.bcg
        self.lf = sb("lf", [128, 4, NH], F32); self.bfb = sb("bfb", [128, NH], F32)
        self.c_all = sb("c_all", [128, NKT, NH], F32); self.bias_all = sb("bias_all", [128, NKT, NH], F32)
        self.Sprev = sb("Sprev", [128, NH], F32); self.cref = sb("cref", [128, NH], F32)
        self.lfs = sb("lfs", [128, P // 128, NH], F32); self.lfn0 = sb("lfn0", [DS, NH], F32)
        self.bias_new = sb("bias_new", [DS, NH], F32)
        self.vnew = sb("vnew", [128, NH, 65], BF16); self.vnew0 = sb("vnew0", [DS, NH, 65], BF16)
        self.kc_bf = sb("kc_bf", [128, 8, 128], BF16)
        self.MKT = sb("MKT", [128, 8, NMEM], BF16); self.MV = sb("MV", [128, 2, D], BF16)
        self.a_sb = sb("a_sb", [128, 4, MLA_DOWN], F32)
        self.cq_bf = sb("cq_bf", [128, 4, MLA_QL], BF16); self.cqT = sb("cqT", [128, 3, TS], BF16)
        self.ckv_bf = sb("ckv_bf", [128, 4, MLA_KVL], BF16); self.ckvT = sb("ckvT", [128, 2, TS], BF16)
        self.kr_bf = sb("kr_bf", [128, 4, MLA_R], BF16); self.krT = sb("krT", [128, 128], BF16)
        self.gq = sb("gq", [128, MLA_QL], F32); self.gkv = sb("gkv", [128, MLA_KVL], F32)
        self.rtm = sb("rtm", [128, 4, 32], F32); self.rfm = sb("rfm", [128, 2, TS], F32)
        self.xr = sb("xr", [128, TS], BF16)
        self.t1 = sb("t1", [128, TS], F32); self.t2 = sb("t2", [128, TS], F32)
        self.krc = sb("krc", [128, P // 128, MLA_R], BF16)
        self.psA = [ps(f"psA{i}", [128, 512], F32) for i in range(2)]
        self.psS = [ps(f"psS{i}", [128, 512], F32) for i in range(2)]
        self.psO = [ps(f"psO{i}", [128, 512], F32) for i in range(2)]
        self.psM = ps("psM", [128, 512], F32)
        self.psT = ps("psT", [128, 1024], BF16)

        o = self.dram
        self.dma("sp", self.cst_bf[:, :], o["cst_bf"].ap(), writes=["cst"])
        self.dma("sp", self.cst_f[:, :], o["cst_f"].ap(), writes=["cst"])
        for i in range(3):
            self.memset(self.VC[i][:, :, :, 64:65], 1.0, [("VC", i)])
        self.memset(self.vt_bf[:, :, :, 64:65], 1.0, [("vt_bf", s) for s in range(4)])
        self.memset(self.vnew[:, :, 64:65], 1.0, ["vnew"])
        self.memset(self.one_col[:, :], 1.0, ["one_col"])
        self.W = {}
        try:
            self.phase("convert0")
            self.convert_layer(0)
            self.phase("memory_kv")
            self.memory_kv()
            tiles = [Tile("s", 0)] + [Tile("p", j) for j in range(T // TS)]
            for l in range(L):
                if l + 1 < L:
                    self.convert_layer(l + 1)
                if l % 2 == 0:
                    self.memset(self.Sprev[:, :], 0.0, ["Sprev"])
                for tl in tiles:
                    if tl.kind == "p" and tl.j == 0 and l % 2 == 0:
                        self.memset(self.Sprev[:, :], 0.0, ["Sprev"])
                    self.layer_tile(l, tl)
        except StopBuild:
            pass
        self.S.emit(self.nc)
        self.es.close()
        return self.nc


def _constants(cfg):
    bf = ml_dtypes.bfloat16
    cb = np.zeros((128, 640), np.float32)
    k = np.arange(128)[:, None]
    q = np.arange(128)[None, :]
    cb[:, 0:128] = np.eye(128)
    cb[:, 128:256] = np.where(k > q, NEG, 0.0)
    cb[:, 256:384] = np.where((k // 64) > (q // 64), NEG, 0.0)
    cb[:, 384:512] = 1.0
    R = np.zeros((32, 32), np.float32)
    for m in range(16):
        R[m + 16, m] = -1.0
        R[m, m + 16] = 1.0
    cb[64:96, 512:544] = R
    cf = np.zeros((128, 256), np.float32)
    cf[:, 0:128] = (k <= q)
    cf[:, 128:256] = 1.0
    half = 16
    inv = (np.float32(10000.0) ** (-(np.arange(half, dtype=np.float32) / np.float32(half)))).astype(np.float32)

    def tables(pos):
        ang = pos.astype(np.float32)[:, None] * inv[None, :]
        return np.cos(ang).astype(np.float32), np.sin(ang).astype(np.float32)
    cp, sp_ = tables(np.arange(cfg.T))
    rope_tm_p = np.concatenate([cp, sp_], axis=1)
    rope_fm_p = np.stack([np.concatenate([cp, cp], 1).T, np.concatenate([sp_, sp_], 1).T]).astype(np.float32)
    cs, ss = tables(cfg.P + np.arange(DS))
    rope_tm_s = np.zeros((128, 32), np.float32)
    rope_fm_s = np.zeros((2, 32, 128), np.float32)
    for b in range(NSEQ):
        rope_tm_s[32 * b:32 * b + DS] = np.concatenate([cs, ss], 1)
        rope_fm_s[0, :, 32 * b:32 * b + DS] = np.concatenate([cs, cs], 1).T
        rope_fm_s[1, :, 32 * b:32 * b + DS] = np.concatenate([ss, ss], 1).T
    return dict(cst_bf=cb.astype(bf), cst_f=cf, rope_tm_p=np.ascontiguousarray(rope_tm_p),
                rope_tm_s=rope_tm_s, rope_fm_p=np.ascontiguousarray(rope_fm_p), rope_fm_s=rope_fm_s)


def run(cfg, inputs):
    nc = Builder(cfg).build()
    T, P, L, NF, NM = cfg.T, cfg.P, cfg.L, cfg.NF, cfg.NM
    consts = _constants(cfg)
    c32 = lambda a: np.ascontiguousarray(a, dtype=np.float32)
    shared = {}
    for k in ("g_mix", "g_cross", "g_mem", "g_ffn", "g_final", "b_fox_f", "g_mla_q", "g_mla_kv"):
        shared[k] = c32(inputs[k]).reshape(-1)
    for k in ("w_fox_in", "w_fox_out", "w_mla_a", "w_mla_qb", "w_mla_kvb", "w_mla_out", "w_x_q", "w_x_kv", "w_x_o",
              "w_ffn_gu", "w_ffn_down"):
        shared[k] = c32(inputs[k])
    shared.update(consts)
    in_maps = []
    for c in range(cfg.ncores):
        m = dict(shared)
        m["x_prompt"] = c32(inputs["x_prompt"][c])
        m["x_sample"] = c32(inputs["x_sample"][NSEQ * c:NSEQ * (c + 1)]).reshape(NSEQ * DS, D)
        m["mem_prompt"] = c32(inputs["mem_prompt"][c])
        sl = slice(NSEQ * c, NSEQ * (c + 1))
        m["cache_fox_k"] = c32(inputs["cache_fox_k"][:, sl]).reshape(NF, NSEQ, P, D)
        m["cache_fox_v"] = c32(inputs["cache_fox_v"][:, sl]).reshape(NF, NSEQ, P, D)
        m["cache_fox_logf"] = c32(inputs["cache_fox_logf"][:, sl])
        m["cache_mla_ckv"] = c32(inputs["cache_mla_ckv"][:, sl])
        m["cache_mla_krope"] = c32(inputs["cache_mla_krope"][:, sl])
        m["cache_mem_k"] = c32(inputs["cache_mem_k"][:, sl]).reshape(L, NSEQ, NMEM, D)
        m["cache_mem_v"] = c32(inputs["cache_mem_v"][:, sl]).reshape(L, NSEQ, NMEM, D)
        in_maps.append(m)
    res = run_bass_kernel_spmd(nc, in_maps, core_ids=list(range(cfg.ncores)))
    R = res.results
    B = cfg.ncores

    def gat(name, shape_per_core, axis):
        return np.stack([np.asarray(R[c][name], dtype=np.float32).reshape(shape_per_core) for c in range(B)], axis=axis)

    y_p = gat("y_p", (T, D), 0)
    y_s = gat("y_s", (NSEQ, DS, D), 0).reshape(B * NSEQ, DS, D)
    fk_p = gat("fk_p", (NF, T, NH, HD), 1)
    fv_p = gat("fv_p", (NF, T, NH, HD), 1)
    fl_p = gat("fl_p", (NF, T, NH), 1)
    mc_p = gat("mc_p", (NM, T, MLA_KVL), 1)
    mr_p = gat("mr_p", (NM, T, MLA_R), 1)
    mk_p = gat("memk_p", (L, NMEM, 4, 256), 1)
    mv_p = gat("memv_p", (L, NMEM, 4, 256), 1)
    fk_s = gat("fk_s", (NF, NSEQ, DS, NH, HD), 1).reshape(NF, B * NSEQ, DS, NH, HD)
    fv_s = gat("fv_s", (NF, NSEQ, DS, NH, HD), 1).reshape(NF, B * NSEQ, DS, NH, HD)
    fl_s = gat("fl_s", (NF, NSEQ, DS, NH), 1).reshape(NF, B * NSEQ, DS, NH)
    mc_s = gat("mc_s", (NM, NSEQ, DS, MLA_KVL), 1).reshape(NM, B * NSEQ, DS, MLA_KVL)
    mr_s = gat("mr_s", (NM, NSEQ, DS, MLA_R), 1).reshape(NM, B * NSEQ, DS, MLA_R)
    return (y_p, y_s, fk_p, fv_p, fl_p, mc_p, mr_p, mk_p, mv_p, fk_s, fv_s, fl_s, mc_s, mr_s)


def kernel(**inputs):
    cfg = Cfg(T=4096, P=2048, L=4, ncores=8)
    return run(cfg, inputs)
```

```python
import numpy as np
import ml_dtypes
from contextlib import ExitStack
import concourse.bass as bass
import concourse.mybir as mybir
from concourse.bass_utils import run_bass_kernel_spmd

F32 = mybir.dt.float32
BF16 = mybir.dt.bfloat16
AF = mybir.ActivationFunctionType
ALU = mybir.AluOpType

ENGINES = ("pe", "act", "dve", "pool", "sp")
N_DMA_SEMS = 20


class Op:
    __slots__ = ("eng", "fn", "deps", "needs_inc", "count", "idx", "dma", "dsem", "dval")

    def __init__(self, eng, fn):
        self.eng = eng
        self.fn = fn
        self.deps = []
        self.needs_inc = False
        self.count = None
        self.idx = None
        self.dma = False
        self.dsem = None
        self.dval = None


class Sched:
    def __init__(self):
        self.streams = {e: [] for e in ENGINES}
        self.state = {}
        self.seen = {e: {} for e in ENGINES}
        self.dma_rr = {e: 0 for e in ENGINES}
        self.dma_last = {e: [None] * N_DMA_SEMS for e in ENGINES}
        self.dma_cnt = {e: [0] * N_DMA_SEMS for e in ENGINES}

    def _add_dep(self, op, d):
        if d is None or d is op:
            return
        if d.dma:
            key = ("d", id(d))
            if key in self.seen[op.eng]:
                return
            self.seen[op.eng][key] = True
            op.deps.append(d)
            return
        if d.eng == op.eng and op.eng == "pe" and not op.dma:
            return
        prev = self.seen[op.eng].get(d.eng, -1)
        if d.idx <= prev:
            return
        self.seen[op.eng][d.eng] = d.idx
        d.needs_inc = True
        op.deps.append(d)

    @staticmethod
    def _is_psum(k):
        return k == "psM" or (isinstance(k, tuple) and k[0] in ("psA", "psS", "psO", "psT"))

    def op(self, eng, fn, reads=(), writes=(), dma=False):
        excl = [k for k in reads if self._is_psum(k)]
        if excl:
            reads = [k for k in reads if not self._is_psum(k)]
            writes = list(writes) + [k for k in excl if k not in writes]
        o = Op(eng, fn)
        o.dma = dma
        o.idx = len(self.streams[eng])
        if dma:
            s = self.dma_rr[eng]
            self.dma_rr[eng] = (s + 1) % N_DMA_SEMS
            prev = self.dma_last[eng][s]
            if prev is not None:
                key = ("d", id(prev))
                if key not in self.seen[eng]:
                    self.seen[eng][key] = True
                    o.deps.append(prev)
            self.dma_cnt[eng][s] += 1
            o.dsem = s
            o.dval = 16 * self.dma_cnt[eng][s]
            self.dma_last[eng][s] = o
        for k in reads:
            st = self.state.get(k)
            if st is not None:
                self._add_dep(o, st[0])
        for k in writes:
            st = self.state.get(k)
            if st is not None:
                self._add_dep(o, st[0])
                for r in st[1]:
                    self._add_dep(o, r)
        for k in reads:
            st = self.state.setdefault(k, [None, []])
            st[1].append(o)
        for k in writes:
            self.state[k] = [o, []]
        self.streams[eng].append(o)
        return o

    def emit(self, nc):
        with ExitStack() as es:
            prog = {e: es.enter_context(nc.semaphore(f"prog_{e}")) for e in ENGINES}
            dsem = {e: [es.enter_context(nc.semaphore(f"dma_{e}_{i}")) for i in range(N_DMA_SEMS)]
                    for e in ("sp", "act", "pool")}
            for e in ENGINES:
                c = 0
                for o in self.streams[e]:
                    if o.needs_inc and not o.dma:
                        c += 1
                        o.count = c
            block = es.enter_context(nc.Block())

            def run(ename, eng):
                for o in self.streams[ename]:
                    for d in o.deps:
                        if d.dma:
                            eng.wait_ge(dsem[d.eng][d.dsem], d.dval)
                        else:
                            eng.wait_ge(prog[d.eng], d.count)
                    ins = o.fn(eng)
                    if o.dma:
                        ins.then_inc(dsem[ename][o.dsem], 16)
                    elif o.needs_inc:
                        ins.then_inc(prog[ename], 1)
                if ename in dsem:
                    for s in range(N_DMA_SEMS):
                        last = self.dma_last[ename][s]
                        if last is not None:
                            eng.wait_ge(dsem[ename][s], last.dval)

            @block.sync
            def _(eng):
                run("sp", eng)

            @block.scalar
            def _(eng):
                run("act", eng)

            @block.vector
            def _(eng):
                run("dve", eng)

            @block.gpsimd
            def _(eng):
                run("pool", eng)

            @block.tensor
            def _(eng):
                run("pe", eng)


D = 1024
NH = 16
HD = 64
FOX_IN = 4112
MLA_QL, MLA_KVL, MLA_R = 384, 256, 32
MLA_DOWN = 672
MLA_SCALE = float(96 ** -0.5)
FOX_SCALE = 0.125
X_SCALE = 1.0 / 16.0
DFF = 2816
NMEM = 256
EPS = 1e-6
NEG = -30000.0
TS = 512
DS = 16
NSEQ = 4


class Cfg:
    def __init__(self, T=4096, P=2048, L=4, ncores=8):
        self.T, self.P, self.L, self.ncores = T, P, L, ncores
        self.skip = set()
        self.stop = None
        self.NF = (L + 1) // 2
        self.NM = L // 2
        assert T % TS == 0 and P % 1024 == 0


class StopBuild(Exception):
    pass


class Tile:
    def __init__(self, kind, j):
        self.kind = kind
        self.j = j
        if kind == "p":
            self.nsub, self.NT, self.pos0 = TS // 128, TS, j * TS
        else:
            self.nsub, self.NT, self.pos0 = 1, 128, 0


class Builder:
    def __init__(self, cfg):
        self.cfg = cfg
        self.nc = bass.Bass("TRN2", target_bir_lowering=False)
        self.S = Sched()
        self.es = ExitStack()
        self.dram = {}
        self.rrA = 0
        self.rrE = 0
        self.rrW = 0
        self.rrStage = 0
        self.rrPT = 0
        self.rrS = 0
        self.rrO = 0
        self.rrKT = 0
        self.rrT = 0
        self.rrAcc = 0
        self.rrKV = 0
        self.kv_w_loaded = {}
        self.pend = None
        self.deferred = []

    def kq(self, c):
        return ("R", c)

    def kk(self, c):
        return ("R", 8 + c)

    def ko(self, c):
        return ("R", 16 + c)

    def ka(self, f):
        return ("R", f)

    def din(self, name, shape, dt=F32):
        t = self.nc.dram_tensor(name, list(shape), dt, kind="ExternalInput")
        self.dram[name] = t
        return t

    def dout(self, name, shape, dt=F32):
        t = self.nc.dram_tensor(name, list(shape), dt, kind="ExternalOutput")
        self.dram[name] = t
        return t

    def dscr(self, name, shape, dt):
        t = self.nc.dram_tensor(name, list(shape), dt)
        self.dram[name] = t
        return t

    def sb(self, name, shape, dt):
        return self.es.enter_context(self.nc.sbuf_tensor(name, list(shape), dt))

    def ps(self, name, shape, dt):
        return self.es.enter_context(self.nc.psum_tensor(name, list(shape), dt))

    def dma(self, q, out, in_, reads=(), writes=()):
        return self.S.op(q, lambda e: e.dma_start(out=out, in_=in_), reads, writes, dma=True)

    def mm(self, out, lhsT, rhs, start, stop, reads, writes):
        return self.S.op("pe", lambda e: e.matmul(out, lhsT=lhsT, rhs=rhs, start=start, stop=stop),
                         reads, writes)

    def tr(self, out, in_, reads, writes):
        idn = self.ident[0:in_.shape[0], 0:in_.shape[0]]
        return self.S.op("pe", lambda e: e.transpose(out, in_, idn), list(reads) + ["cst"], writes)

    def act(self, out, in_, func, reads, writes, bias=None, scale=None, accum_out=None):
        kw = {}
        if bias is not None:
            kw["bias"] = bias
        if scale is not None:
            kw["scale"] = scale
        if accum_out is not None:
            kw["accum_out"] = accum_out
        return self.S.op("act", lambda e: e.activation(out=out, in_=in_, func=func, **kw), reads, writes)

    def tt(self, out, in0, in1, op, reads, writes, eng="dve"):
        return self.S.op(eng, lambda e: e.tensor_tensor(out=out, in0=in0, in1=in1, op=op), reads, writes)

    def ts(self, out, in0, s1, s2, op0, op1, reads, writes, eng="dve"):
        if op1 is None:
            return self.S.op(eng, lambda e: e.tensor_scalar(out=out, in0=in0, scalar1=s1, scalar2=None, op0=op0),
                             reads, writes)
        return self.S.op(eng, lambda e: e.tensor_scalar(out=out, in0=in0, scalar1=s1, scalar2=s2, op0=op0, op1=op1),
                         reads, writes)

    def stt(self, out, in0, scalar, in1, op0, op1, reads, writes):
        return self.S.op("dve", lambda e: e.scalar_tensor_tensor(out=out, in0=in0, scalar=scalar, in1=in1,
                                                                  op0=op0, op1=op1), reads, writes)

    def cp(self, out, in_, reads, writes, eng="dve"):
        if eng == "act":
            return self.S.op("act", lambda e: e.copy(out=out, in_=in_), reads, writes)
        return self.S.op(eng, lambda e: e.tensor_copy(out=out, in_=in_), reads, writes)

    def evac(self, out, in_, reads, writes):
        self.rrE ^= 1
        return self.cp(out, in_, reads, writes, eng=("act" if self.rrE else "dve"))

    def recip(self, out, in_, reads, writes):
        return self.S.op("dve", lambda e: e.reciprocal(out=out, in_=in_), reads, writes)

    def memset(self, ap, val, writes, eng="pool"):
        return self.S.op(eng, lambda e: e.memset(ap, val), (), writes)

    def nextA(self):
        self.rrA ^= 1
        return self.psA[self.rrA], ("psA", self.rrA)

    def nextS(self):
        self.rrS ^= 1
        return self.psS[self.rrS], ("psS", self.rrS)

    def nextO(self):
        self.rrO ^= 1
        return self.psO[self.rrO], ("psO", self.rrO)

    def nextPT(self):
        self.rrPT = (self.rrPT + 1) % len(self.PT)
        return self.PT[self.rrPT], ("PT", self.rrPT)

    def nextT(self):
        return self.psT[:, 0:512], ("psT", 0)

    def nextStage(self):
        self.rrStage = (self.rrStage + 1) % len(self.stage)
        return self.stage[self.rrStage], ("stage", self.rrStage)

    def bcast_row(self, tname, row_off, n):
        return bass.AP(self.dram[tname], row_off, [[0, 128], [1, n]])

    def conv_weight(self, name, src_name, src_l, K, c0, c1, cb, src_cols):
        KC = K // 128
        M = c1 - c0
        nblk = (M + cb - 1) // cb
        scr = self.dscr(name, [nblk, 128, KC * cb], BF16)
        src = self.dram[src_name].ap()[src_l]
        for b in range(nblk):
            w = min(cb, M - b * cb)
            o = scr.ap()[b].rearrange("p (k c) -> p k c", k=KC)[:, :, 0:w]
            i = src[:, c0 + b * cb: c0 + b * cb + w].rearrange("(k p) c -> p k c", p=128)
            self.dma("pool", o, i, reads=(), writes=[("wb", name, b)])
        return dict(name=name, KC=KC, cb=cb, nblk=nblk, M=M, scr=scr)

    def wplan(self, items):
        self.plan = list(items)
        self.plan_i = 0
        self.plan_loaded = 0
        self.loaded = {}

    def _issue_load(self, i):
        W, b = self.plan[i]
        slot = self.rrW
        self.rrW = (self.rrW + 1) % len(self.wbuf)
        n = W["KC"] * W["cb"]
        w = min(W["cb"], W["M"] - b * W["cb"]) if W["name"][:2] != "wd" else W["cb"]
        kcv = W["KC"]
        if W["name"][:2] == "wd":
            nfb = W["nblk"] // 2
            kcv = min(4, DFF // 128 - (b % nfb) * 4)
        self.dma("sp", self.wbuf[slot][:, 0:n].rearrange("p (k c) -> p k c", k=W["KC"])[:, 0:kcv, 0:w],
                 W["scr"].ap()[b].rearrange("p (k c) -> p k c", k=W["KC"])[:, 0:kcv, 0:w],
                 reads=[("wb", W["name"], b)], writes=[("w", slot)])
        self.loaded[i] = slot

    def wnext(self, W, b):
        i = self.plan_i
        assert self.plan[i][0] is W and self.plan[i][1] == b, (self.plan[i][0]["name"], self.plan[i][1], W["name"], b)
        while self.plan_loaded < min(len(self.plan), i + 2):
            self._issue_load(self.plan_loaded)
            self.plan_loaded += 1
        slot = self.loaded.pop(i)
        self.plan_i += 1
        cb = W["cb"]
        buf = self.wbuf[slot]

        def view(kc, a, b2):
            return buf[:, kc * cb + a: kc * cb + b2]
        return view, ("w", slot)

    def rstd_of(self, src_fn, nsub, n, rkey, reads):
        for s in range(nsub):
            self.act(self.junk[:, 0:n], src_fn(s), AF.Square, reads, ["junk", (rkey, "ss", s)],
                     accum_out=self.ss[:, s:s + 1])
        self.ts(self.ss2[:, 0:nsub], self.ss[:, 0:nsub], 1.0 / n, EPS, ALU.mult, ALU.add,
                [(rkey, "ss", s) for s in range(nsub)], ["ss2"])
        self.S.op("act", lambda e: e.sqrt(out=self.ss2[:, 0:nsub], in_=self.ss2[:, 0:nsub]), ["ss2"], ["ss2"])
        self.recip(self.rstd[:, 0:nsub], self.ss2[:, 0:nsub], ["ss2"], ["rstd"])

    def norm_hT(self, tl, gname, l):
        gk = ("gbc", gname)
        self.dma("sp", self.gbc[:, :], self.bcast_row(gname, l * D, D), writes=[gk])
        self.rstd_of(lambda s: self.x_sb[:, s, :], tl.nsub, D, "x", ["x"])
        for s in range(tl.nsub):
            self.stt(self.h_bf[:, s, :], self.x_sb[:, s, :], self.rstd[:, s:s + 1], self.gbc[:, :],
                     ALU.mult, ALU.mult, ["x", "rstd", gk], [("h_bf", s)])
        self.transpose_to(self.h_bf, [("h_bf", s) for s in range(tl.nsub)], self.hT, "hT", 8, tl)

    def transpose_to(self, src, src_keys, dst, dkey, nchunks, tl, csz=128, dst_off=0, keyfn=None):
        for c in range(nchunks):
            pT, pk = self.nextT()
            for s in range(tl.nsub):
                self.tr(pT[0:csz, s * 128:(s + 1) * 128], src[:, s, c * csz:(c + 1) * csz], src_keys, [pk])
            self.evac(dst[0:csz, c, dst_off:dst_off + tl.NT], pT[0:csz, 0:tl.NT], [pk],
                      [keyfn(c) if keyfn else (dkey, c)])

    def rows_dma_in(self, tl, dst_fn, src_t, ncols, row_base=0, writes=()):
        if tl.kind == "p":
            for s in range(tl.nsub):
                r0 = row_base + tl.pos0 + s * 128
                self.dma("sp", dst_fn(s), src_t[r0:r0 + 128, :], writes=writes)
        else:
            for b in range(NSEQ):
                self.dma("sp", dst_fn(0)[32 * b:32 * b + DS, :], src_t[row_base + b * DS: row_base + (b + 1) * DS, :],
                         writes=writes)

    def rows_dma_out(self, tl, s, src_ap, dst_t, c0, c1, reads, row_base=0):
        if tl.kind == "p":
            r0 = row_base + tl.pos0 + s * 128
            self.dma("sp", dst_t[r0:r0 + 128, c0:c1], src_ap, reads=reads)
        else:
            for b in range(NSEQ):
                self.dma("sp", dst_t[row_base + b * DS: row_base + (b + 1) * DS, c0:c1],
                         src_ap[32 * b:32 * b + DS, :], reads=reads)

    def proj_fm(self, W, nchunks, tl, src, skey, KC, sink):
        per_blk = W["cb"] // 128
        for c in range(nchunks):
            blk, cc = divmod(c, per_blk)
            if cc == 0:
                wv, wk = self.wnext(W, blk)
            ps, pk = self.nextA()
            for kc in range(KC):
                self.mm(ps[:, 0:tl.NT], wv(kc, cc * 128, cc * 128 + 128), src[:, kc, 0:tl.NT],
                        kc == 0, kc == KC - 1, [wk] + [(skey, kc)], [pk])
            sink(c, ps, pk)

    def proj_tm(self, W, tl, src, skey, KC, sink, keyfn=None):
        for blk in range(W["nblk"]):
            wv, wk = self.wnext(W, blk)
            w = min(W["cb"], W["M"] - blk * W["cb"])
            for s in range(tl.nsub):
                ps, pk = self.nextA()
                for kc in range(KC):
                    self.mm(ps[:, 0:w], src[:, kc, s * 128:(s + 1) * 128], wv(kc, 0, w),
                            kc == 0, kc == KC - 1, [wk, keyfn(kc) if keyfn else (skey, kc)], [pk])
                sink(blk, s, ps, pk, w)

    def out_proj_residual(self, W, tl, src, skey, KC, keyfn=None):
        def sink(blk, s, ps, pk, w):
            self.tt(self.x_sb[:, s, blk * 512: blk * 512 + w], ps[:, 0:w], self.x_sb[:, s, blk * 512: blk * 512 + w],
                    ALU.add, [pk, "x"], ["x"])
        self.proj_tm(W, tl, src, skey, KC, sink, keyfn=keyfn)

    def next_acc_pair(self, allow_psA):
        if allow_psA:
            self.rrAcc ^= 1
        else:
            self.rrAcc = 0
        if self.rrAcc == 0:
            return [(self.psO[0], ("psO", 0)), (self.psO[1], ("psO", 1))]
        return [(self.psA[0], ("psA", 0)), (self.psA[1], ("psA", 1))]

    def next_kv(self):
        self.rrKV = (self.rrKV + 1) % len(self.KC)
        return self.rrKV

    def normalize_head(self, psO, ok, r, c, q0, q1, n, gate):
        self.recip(self.rd[64:65, 0:n], psO[64:65, 0:n], [ok], ["rd"])
        self.mm(self.psM[0:64, 0:n], self.ones_f[64:65, 0:64], self.rd[64:65, 0:n], True, True, ["rd", "cst"], ["psM"])
        if gate:
            self.tt(self.bcg[0:64, 0:n], self.psM[0:64, 0:n], self.gT[r * 64:(r + 1) * 64, c, q0:q1], ALU.mult,
                    ["psM", ("gT", c)], ["bcg"])
        else:
            self.cp(self.bcg[0:64, 0:n], self.psM[0:64, 0:n], ["psM"], ["bcg"], eng="act")
        self.tt(self.oT[r * 64:(r + 1) * 64, c, q0:q1], psO[0:64, 0:n], self.bcg[0:64, 0:n], ALU.mult,
                [ok, "bcg"], [self.ko(c)])

    def pipe_push(self, pv):
        prev, self.pend = self.pend, pv
        if prev is not None:
            prev()
        d, self.deferred = self.deferred, []
        for fn in d:
            fn()

    def pipe_flush(self):
        prev, self.pend = self.pend, None
        if prev is not None:
            prev()
        d, self.deferred = self.deferred, []
        for fn in d:
            fn()

    def score_pv(self, c, r, kt, KCs, kcol, kkey, VCs, vt, vkey, psO, ok, q0, qn, n0, mla, mask, first, last):
        h = 2 * c + r
        scale = MLA_SCALE if mla else FOX_SCALE
        psS, sk = self.nextS()
        diag = mask is not None
        self.mm(psS[:, n0:qn], KCs[r * 64:(r + 1) * 64, kcol:kcol + 128], self.qT[r * 64:(r + 1) * 64, c, q0 + n0:q0 + qn],
                True, (not mla) and (not diag), [kkey, self.kq(c)], [sk])
        if mla:
            self.mm(psS[:, n0:qn], self.KR[r * 64:r * 64 + 32, kt * 128:(kt + 1) * 128],
                    self.qR[r * 64:r * 64 + 32, c, q0 + n0:q0 + qn], False, not diag, [("KR", kt // 4), ("gT", c)], [sk])
        if diag:
            self.mm(psS[:, n0:n0 + 128], self.ident[:, :], mask, False, True, ["cst"], [sk])
        PT, ptk = self.nextPT()
        if mla:
            self.act(PT[:, n0:qn], psS[:, n0:qn], AF.Exp, [sk], [ptk], scale=scale)
        else:
            self.act(PT[:, n0:qn], psS[:, n0:qn], AF.Exp, [sk, ("bias", kt)], [ptk],
                     bias=self.bias_all[:, kt, h:h + 1], scale=scale)
        self.pipe_push(lambda: self.mm(psO[0:65, n0:qn], VCs[:, vt, r, 0:65], PT[:, n0:qn], first, last,
                                       [vkey, ptk], [ok]))

    def attn_prompt_pair(self, tl, c, mla):
        NT = tl.NT
        nkt = (tl.j + 1) * (TS // 128)
        accs = self.next_acc_pair(True)
        mask = self.maskM[:, :] if mla else self.maskF[:, :]
        for g in range((nkt + 7) // 8):
            nt = min(8, nkt - g * 8)
            slot = self.next_kv()
            kkey, vkey = ("KC", slot), ("VC", slot)
            jj = [("KTd", c, j2) for j2 in range(g * 2, min(g * 2 + 2, tl.j + 1))]
            vv = [("Vd", c, j2) for j2 in range(g * 2, min(g * 2 + 2, tl.j + 1))]
            self.dma("sp", self.KC[slot][:, 0:nt * 128], self.KTd.ap()[c][:, g * 1024:g * 1024 + nt * 128],
                     reads=jj, writes=[kkey])
            self.dma("sp", self.VC[slot][:, 0:nt, :, :].rearrange("p t h e -> p t (h e)"),
                     self.Vd.ap()[c][:, g * 8:g * 8 + nt, :], reads=vv, writes=[vkey])
            for t in range(nt):
                for r in range(2):
                    psO, ok = accs[r]
                    kt = g * 8 + t
                    d = kt - tl.j * (TS // 128)
                    n0 = 0 if d < 0 else d * 128
                    self.score_pv(c, r, kt, self.KC[slot], t * 128, kkey, self.VC[slot], t, vkey, psO, ok,
                                  0, NT, n0, mla, mask if d >= 0 else None, kt == 0, kt == nkt - 1)

        def norm():
            for r in range(2):
                psO, ok = accs[r]
                self.normalize_head(psO, ok, r, c, 0, NT, NT, gate=not mla)
        self.deferred.append(norm)

    def attn_sample_pair(self, l, b, c, mla):
        cfg = self.cfg
        lj = l // 2
        o = self.dram
        q0 = 32 * b
        npast = cfg.P // 128
        accs = self.next_acc_pair(not mla)
        for g in range(cfg.P // 1024):
            slot = self.next_kv()
            kkey, vkey = ("KC", slot), ("VC", slot)
            if not mla:
                src = o["cache_fox_k"].ap()[lj, b, g * 1024:(g + 1) * 1024, c * 128:(c + 1) * 128].rearrange("(t p) d -> p t d", p=128)
                self.dma("pool", self.kc_bf[:, :, :], src, writes=["kc_bf"])
                for g4 in range(2):
                    pT, pk = self.nextT()
                    for i in range(4):
                        self.tr(pT[:, i * 128:(i + 1) * 128], self.kc_bf[:, g4 * 4 + i, :], ["kc_bf"], [pk])
                    self.evac(self.KC[slot][:, g4 * 512:(g4 + 1) * 512], pT[:, 0:512], [pk], [kkey])
                for r2 in range(2):
                    srcv = o["cache_fox_v"].ap()[lj, b, g * 1024:(g + 1) * 1024, c * 128 + r2 * 64: c * 128 + (r2 + 1) * 64].rearrange(
                        "(t p) d -> p t d", p=128)
                    self.dma("pool", self.VC[slot][:, :, r2, 0:64], srcv, writes=[vkey])
            else:
                for half in range(2):
                    ps, pk = self.nextA()
                    for kc in range(2):
                        self.mm(ps[:, 0:512], self.wkvk[:, kc, c * 128:(c + 1) * 128],
                                self.ckvTs[:, kc, g * 1024 + half * 512: g * 1024 + (half + 1) * 512],
                                kc == 0, kc == 1, ["wkvk", ("hT", kc)], [pk])
                    self.evac(self.KC[slot][:, half * 512:(half + 1) * 512], ps[:, 0:512], [pk], [kkey])
                for t4 in range(2):
                    ps, pk = self.nextA()
                    for i in range(4):
                        t = t4 * 4 + i
                        for kc in range(2):
                            self.mm(ps[:, i * 128:(i + 1) * 128], self.ckvTs[:, kc, (g * 8 + t) * 128:(g * 8 + t + 1) * 128],
                                    self.wkvv[:, kc, c * 128:(c + 1) * 128], kc == 0, kc == 1, ["wkvv", ("hT", kc)], [pk])
                    self.evac(self.VC[slot][:, t4 * 4:(t4 + 1) * 4, :, 0:64],
                              ps[:, 0:512].rearrange("p (t h d) -> p t h d", t=4, d=64), [pk], [vkey])
            for t in range(8):
                for r in range(2):
                    psO, ok = accs[r]
                    kt = g * 8 + t
                    self.score_pv(c, r, kt, self.KC[slot], t * 128, kkey, self.VC[slot], t, vkey, psO, ok,
                                  q0, DS, 0, mla, None, kt == 0, False)
        for r in range(2):
            h = 2 * c + r
            psO, ok = accs[r]
            scale = MLA_SCALE if mla else FOX_SCALE
            psS, sk = self.nextS()
            PT, ptk = self.nextPT()
            self.mm(psS[0:DS, 0:DS], self.kT[r * 64:(r + 1) * 64, c, q0:q0 + DS], self.qT[r * 64:(r + 1) * 64, c, q0:q0 + DS],
                    True, False, [self.kk(c), self.kq(c)], [sk])
            if mla:
                self.mm(psS[0:DS, 0:DS], self.krT[r * 64:r * 64 + 32, q0:q0 + DS], self.qR[r * 64:r * 64 + 32, c, q0:q0 + DS],
                        False, True, ["krT", ("gT", c)], [sk])
                self.act(PT[0:DS, 0:DS], psS[0:DS, 0:DS], AF.Exp, [sk], [ptk], scale=scale)
            else:
                self.mm(psS[0:DS, 0:DS], self.ident[0:DS, 0:DS], self.maskF[0:DS, 0:DS], False, True, ["cst"], [sk])
                self.act(PT[0:DS, 0:DS], psS[0:DS, 0:DS], AF.Exp, [sk, "bias_new"], [ptk],
                         bias=self.bias_new[0:DS, h:h + 1], scale=scale)
            self.pipe_push(lambda psO=psO, ok=ok, h=h, PT=PT, ptk=ptk: self.mm(
                psO[0:65, 0:DS], self.vnew0[0:DS, h, 0:65], PT[0:DS, 0:DS], False, True, ["vnew0", ptk], [ok]))

        def norm():
            for r in range(2):
                psO, ok = accs[r]
                self.normalize_head(psO, ok, r, c, q0, q0 + DS, DS, gate=not mla)
        self.deferred.append(norm)
        if mla:
            self.pipe_flush()

    def cumsum_tile(self, lf_ap, lf_key, dst_ap, dst_key):
        self.mm(self.psM[:, 0:16], self.Umat[:, :], lf_ap, True, False, [lf_key, "cst"], ["psM"])
        self.mm(self.psM[:, 0:16], self.ones_f[:, :], self.Sprev[:, :], False, True, ["Sprev", "cst"], ["psM"])
        self.cp(dst_ap, self.psM[:, 0:16], ["psM"], [dst_key])
        self.tt(self.Sprev[:, :], self.Sprev[:, :], lf_ap, ALU.add, ["Sprev", lf_key], ["Sprev"])

    def carry_bcast(self, dst_ap, dst_key):
        self.mm(self.psM[:, 0:16], self.ones_f[:, :], self.Sprev[:, :], True, True, ["Sprev", "cst"], ["psM"])
        self.cp(dst_ap, self.psM[:, 0:16], ["psM"], [dst_key])

    def store_v_tile(self, tl):
        for c in range(8):
            self.dma("sp", self.Vd.ap()[c][:, tl.j * 4:(tl.j + 1) * 4, :],
                     self.vt_bf[:, :, 2 * c:2 * c + 2, :].rearrange("p s h e -> p s (h e)"),
                     reads=[("vt_bf", s) for s in range(4)], writes=[("Vd", c, tl.j)])

    def fox_mixer(self, l, tl):
        cfg = self.cfg
        lj = l // 2
        W = self.W[l]
        NT, nsub = tl.NT, tl.nsub
        o = self.dram
        isp = tl.kind == "p"
        rb = lj * cfg.T if isp else lj * NSEQ * DS
        self.norm_hT(tl, "g_mix", l)
        self.proj_fm(W["q"], 8, tl, self.hT, "hT", 8,
                     lambda c, ps, pk: self.evac(self.qT[:, c, 0:NT], ps[:, 0:NT], [pk], [self.kq(c)]))

        self.phase("fox k")
        def k_sink(blk, s, ps, pk, w):
            st, sk = self.nextStage()
            self.cp(st[:, 0:w], ps[:, 0:w], [pk], [sk], eng="act")
            self.cp(self.h_bf[:, s, blk * 512: blk * 512 + w], st[:, 0:w], [sk], [("h_bf", s)], eng="pool")
            self.rows_dma_out(tl, s, st[:, 0:w], o["fk_p" if isp else "fk_s"].ap(), blk * 512, blk * 512 + w, [sk], row_base=rb)
        self.proj_tm(W["k"], tl, self.hT, "hT", 8, k_sink)
        self.transpose_to(self.h_bf, [("h_bf", s) for s in range(nsub)], self.kT, None, 8, tl, keyfn=self.kk)
        if isp:
            for c in range(8):
                self.dma("sp", self.KTd.ap()[c][:, tl.pos0:tl.pos0 + NT], self.kT[:, c, 0:NT],
                         reads=[self.kk(c)], writes=[("KTd", c, tl.j)])

        self.phase("fox v")
        def v_sink(blk, s, ps, pk, w):
            st, sk = self.nextStage()
            self.cp(st[:, 0:w], ps[:, 0:w], [pk], [sk], eng="act")
            if isp:
                self.cp(self.vt_bf[:, s, blk * 8:(blk + 1) * 8, 0:64], st[:, 0:w].rearrange("p (h d) -> p h d", d=64),
                        [sk], [("vt_bf", s)], eng="pool")
            else:
                self.cp(self.vnew[:, blk * 8:(blk + 1) * 8, 0:64], st[:, 0:w].rearrange("p (h d) -> p h d", d=64),
                        [sk], ["vnew"], eng="pool")
            self.rows_dma_out(tl, s, st[:, 0:w], o["fv_p" if isp else "fv_s"].ap(), blk * 512, blk * 512 + w, [sk], row_base=rb)
        self.proj_tm(W["v"], tl, self.hT, "hT", 8, v_sink)
        if isp:
            self.store_v_tile(tl)

        self.phase("fox f")
        bk = "bfb"
        self.dma("sp", self.bfb[:, :], self.bcast_row("b_fox_f", lj * NH, NH), writes=[bk])
        wv, wk = self.wnext(W["f"], 0)
        for s in range(nsub):
            for kc in range(8):
                self.mm(self.psM[:, 0:16], self.hT[:, kc, s * 128:(s + 1) * 128], wv(kc, 0, 16), kc == 0, kc == 7,
                        [wk, ("hT", kc)], ["psM"])
            self.tt(self.lf[:, s, :], self.psM[:, 0:16], self.bfb[:, :], ALU.add, ["psM", bk], [("lf", s)])
            self.act(self.lf[:, s, :], self.lf[:, s, :], AF.Exp, [("lf", s)], [("lf", s)], scale=-1.0)
            self.act(self.lf[:, s, :], self.lf[:, s, :], AF.Ln, [("lf", s), "one_col"], [("lf", s)], bias=self.one_col[:, 0:1], scale=1.0)
            self.ts(self.lf[:, s, :], self.lf[:, s, :], -1.0, None, ALU.mult, None, [("lf", s)], [("lf", s)])
            self.rows_dma_out(tl, s, self.lf[:, s, :], o["fl_p" if isp else "fl_s"].ap(), 0, NH, [("lf", s)], row_base=rb)

        self.phase("fox g")
        self.proj_fm(W["g"], 8, tl, self.hT, "hT", 8,
                     lambda c, ps, pk: self.act(self.gT[:, c, 0:NT], ps[:, 0:NT], AF.Sigmoid, [pk], [("gT", c)]))

        self.phase("fox attn")
        if isp:
            nkt = (tl.j + 1) * 4
            for s in range(nsub):
                kt = tl.j * 4 + s
                if s == 2:
                    self.carry_bcast(self.cref[:, :], "cref")
                self.cumsum_tile(self.lf[:, s, :], ("lf", s), self.c_all[:, kt, :], ("c_all", kt))
            for kt in range(nkt):
                self.tt(self.bias_all[:, kt, :], self.cref[:, :], self.c_all[:, kt, :], ALU.subtract,
                        ["cref", ("c_all", kt)], [("bias", kt)])
            for c in range(8):
                self.attn_prompt_pair(tl, c, False)
        else:
            npast = cfg.P // 128
            for c in range(8):
                self.memset(self.oT[:, c, 0:128], 0.0, [self.ko(c)])
            for b in range(NSEQ):
                self.dma("sp", self.lfs[:, 0:npast, :],
                         o["cache_fox_logf"].ap()[lj, b].rearrange("(t p) h -> p t h", p=128), writes=["lfs"])
                self.memset(self.Sprev[:, :], 0.0, ["Sprev"])
                for kt in range(npast):
                    self.cumsum_tile(self.lfs[:, kt, :], "lfs", self.c_all[:, kt, :], ("c_all", kt))
                self.carry_bcast(self.cref[:, :], "cref")
                for kt in range(npast):
                    self.tt(self.bias_all[:, kt, :], self.cref[:, :], self.c_all[:, kt, :], ALU.subtract,
                            ["cref", ("c_all", kt)], [("bias", kt)])
                self.phase(f"fox s-attn b{b} newbias")
                self.cp(self.lfn0[0:DS, :], self.lf[32 * b:32 * b + DS, 0, :], [("lf", 0)], ["lfn0"])
                self.mm(self.psM[0:DS, 0:16], self.Umat[0:DS, 0:DS], self.lfn0[0:DS, :], True, True, ["lfn0", "cst"], ["psM"])
                self.ts(self.bias_new[0:DS, :], self.psM[0:DS, 0:16], -1.0, None, ALU.mult, None, ["psM"], ["bias_new"])
                self.cp(self.vnew0[0:DS, :, :], self.vnew[32 * b:32 * b + DS, :, :], ["vnew"], ["vnew0"])
                self.phase(f"fox s-attn b{b} pairs")
                for c in range(8):
                    self.attn_sample_pair(l, b, c, False)
                self.pipe_flush()
        self.pipe_flush()
        self.phase("fox out")
        self.out_proj_residual(W["o"], tl, self.oT, None, 8, keyfn=self.ko)

    def mla_mixer(self, l, tl):
        cfg = self.cfg
        lj = l // 2
        W = self.W[l]
        NT, nsub = tl.NT, tl.nsub
        o = self.dram
        isp = tl.kind == "p"
        rb = lj * cfg.T if isp else lj * NSEQ * DS
        self.norm_hT(tl, "g_mix", l)
        if not self.kv_w_loaded.get(l):
            self.kv_w_loaded[l] = True
            self.dma("sp", self.wkvk[:, :, :].rearrange("p k c -> p (k c)"), W["kvk"]["scr"].ap()[0],
                     reads=[("wb", W["kvk"]["name"], 0)], writes=["wkvk"])
            self.dma("sp", self.wkvv[:, :, :].rearrange("p k c -> p (k c)"), W["kvv"]["scr"].ap()[0],
                     reads=[("wb", W["kvv"]["name"], 0)], writes=["wkvv"])

        def a_sink(blk, s, ps, pk, w):
            self.evac(self.a_sb[:, s, blk * 512: blk * 512 + w], ps[:, 0:w], [pk], [("a_sb", s, blk)])
        self.proj_tm(W["a"], tl, self.hT, "hT", 8, a_sink)
        akeys = [("a_sb", s, blk) for s in range(nsub) for blk in range(2)]
        self.dma("sp", self.gq[:, :], self.bcast_row("g_mla_q", lj * MLA_QL, MLA_QL), writes=["gq"])
        self.dma("sp", self.gkv[:, :], self.bcast_row("g_mla_kv", lj * MLA_KVL, MLA_KVL), writes=["gkv"])
        self.rstd_of(lambda s: self.a_sb[:, s, 0:MLA_QL], nsub, MLA_QL, "cq", akeys)
        for s in range(nsub):
            self.stt(self.cq_bf[:, s, :], self.a_sb[:, s, 0:MLA_QL], self.rstd[:, s:s + 1], self.gq[:, :],
                     ALU.mult, ALU.mult, akeys + ["rstd", "gq"], [("cq_bf", s)])
        self.transpose_to(self.cq_bf, [("cq_bf", s) for s in range(nsub)], self.cqT, "cqT", 3, tl)
        self.rstd_of(lambda s: self.a_sb[:, s, MLA_QL:MLA_QL + MLA_KVL], nsub, MLA_KVL, "ckv", akeys)
        for s in range(nsub):
            st, sk = self.nextStage()
            self.stt(st[:, 0:MLA_KVL], self.a_sb[:, s, MLA_QL:MLA_QL + MLA_KVL], self.rstd[:, s:s + 1], self.gkv[:, :],
                     ALU.mult, ALU.mult, akeys + ["rstd", "gkv"], [sk])
            self.cp(self.ckv_bf[:, s, :], st[:, 0:MLA_KVL], [sk], [("ckv_bf", s)], eng="act")
            self.rows_dma_out(tl, s, st[:, 0:MLA_KVL], o["mc_p" if isp else "mc_s"].ap(), 0, MLA_KVL, [sk], row_base=rb)
        self.transpose_to(self.ckv_bf, [("ckv_bf", s) for s in range(nsub)], self.ckvT, "ckvT", 2, tl)
        if isp:
            self.dma("sp", self.rtm[:, 0:nsub, :],
                     o["rope_tm_p"].ap()[tl.pos0:tl.pos0 + NT, :].rearrange("(s p) c -> p s c", p=128), writes=["rtm"])
        else:
            self.dma("sp", self.rtm[:, 0, :], o["rope_tm_s"].ap(), writes=["rtm"])
        A0 = MLA_QL + MLA_KVL
        for s in range(nsub):
            x1 = self.a_sb[:, s, A0:A0 + 16]
            x2 = self.a_sb[:, s, A0 + 16:A0 + 32]
            cs = self.rtm[:, s, 0:16]
            sn = self.rtm[:, s, 16:32]
            st, sk = self.nextStage()
            rk = akeys + ["rtm"]
            self.tt(st[:, 0:16], x1, cs, ALU.mult, rk, [sk])
            self.tt(st[:, 32:48], x2, sn, ALU.mult, rk, [sk])
            self.tt(st[:, 0:16], st[:, 0:16], st[:, 32:48], ALU.subtract, [sk], [sk])
            self.tt(st[:, 16:32], x1, sn, ALU.mult, rk, [sk])
            self.tt(st[:, 32:48], x2, cs, ALU.mult, rk, [sk])
            self.tt(st[:, 16:32], st[:, 16:32], st[:, 32:48], ALU.add, [sk], [sk])
            self.cp(self.kr_bf[:, s, :], st[:, 0:32], [sk], [("kr_bf", s)], eng="act")
            self.rows_dma_out(tl, s, st[:, 0:32], o["mr_p" if isp else "mr_s"].ap(), 0, MLA_R, [sk], row_base=rb)
        pT, pk = self.nextT()
        for s in range(nsub):
            self.tr(pT[0:32, s * 128:(s + 1) * 128], self.kr_bf[:, s, :], [("kr_bf", s)], [pk])
        for r in range(2):
            if isp:
                self.evac(self.KR[r * 64:r * 64 + 32, tl.pos0:tl.pos0 + NT], pT[0:32, 0:NT], [pk], [("KR", tl.j)])
            else:
                self.evac(self.krT[r * 64:r * 64 + 32, 0:NT], pT[0:32, 0:NT], [pk], ["krT"])

        if isp:
            for i in range(2):
                self.dma("sp", self.rfm[64:96, i, 0:NT], o["rope_fm_p"].ap()[i, :, tl.pos0:tl.pos0 + NT], writes=["rfm"])
        else:
            for i in range(2):
                self.dma("sp", self.rfm[64:96, i, 0:NT], o["rope_fm_s"].ap()[i], writes=["rfm"])
        Wq = W["qb"]
        for h in range(NH):
            blk, hh = divmod(h, 4)
            if hh == 0:
                wv, wk = self.wnext(Wq, blk)
            c, r = divmod(h, 2)
            ps, pk = self.nextA()
            for kc in range(3):
                self.mm(ps[0:96, 0:NT], wv(kc, hh * 96, hh * 96 + 96), self.cqT[:, kc, 0:NT], kc == 0, kc == 2,
                        [wk, ("cqT", kc)], [pk])
            self.cp(self.qT[r * 64:(r + 1) * 64, c, 0:NT], ps[0:64, 0:NT], [pk], [self.kq(c)], eng="act")
            self.cp(self.xr[64:96, 0:NT], ps[64:96, 0:NT], [pk], ["xr"], eng="act")
            self.mm(self.psM[64:96, 0:NT], self.Rrot[64:96, 0:32], self.xr[64:96, 0:NT], True, True, ["xr", "cst"], ["psM"])
            self.tt(self.t1[64:96, 0:NT], ps[64:96, 0:NT], self.rfm[64:96, 0, 0:NT], ALU.mult, [pk, "rfm"], ["t1"])
            self.tt(self.t2[64:96, 0:NT], self.psM[64:96, 0:NT], self.rfm[64:96, 1, 0:NT], ALU.mult, ["psM", "rfm"], ["t2"])
            self.tt(self.qR[r * 64:r * 64 + 32, c, 0:NT], self.t1[64:96, 0:NT], self.t2[64:96, 0:NT], ALU.add,
                    ["t1", "t2"], [("gT", c)])

        for c in range(8):
            ps, pk = self.nextA()
            for kc in range(2):
                self.mm(ps[:, 0:NT], self.wkvk[:, kc, c * 128:(c + 1) * 128], self.ckvT[:, kc, 0:NT], kc == 0, kc == 1,
                        ["wkvk", ("ckvT", kc)], [pk])
            self.evac(self.kT[:, c, 0:NT], ps[:, 0:NT], [pk], [self.kk(c)])
            if isp:
                self.dma("sp", self.KTd.ap()[c][:, tl.pos0:tl.pos0 + NT], self.kT[:, c, 0:NT],
                         reads=[self.kk(c)], writes=[("KTd", c, tl.j)])
        for s in range(nsub):
            for half in range(2):
                ps, pk = self.nextA()
                for kc in range(2):
                    self.mm(ps[:, 0:512], self.ckvT[:, kc, s * 128:(s + 1) * 128], self.wkvv[:, kc, half * 512:(half + 1) * 512],
                            kc == 0, kc == 1, ["wkvv", ("ckvT", kc)], [pk])
                if isp:
                    self.evac(self.vt_bf[:, s, half * 8:(half + 1) * 8, 0:64],
                              ps[:, 0:512].rearrange("p (h d) -> p h d", d=64), [pk], [("vt_bf", s)])
                else:
                    self.evac(self.vnew[:, half * 8:(half + 1) * 8, 0:64],
                              ps[:, 0:512].rearrange("p (h d) -> p h d", d=64), [pk], ["vnew"])
        self.phase("mla attn")
        if isp:
            self.store_v_tile(tl)
            for c in range(8):
                self.attn_prompt_pair(tl, c, True)
        else:
            npast = cfg.P // 128
            for c in range(8):
                self.memset(self.oT[:, c, 0:128], 0.0, [self.ko(c)])
            for b in range(NSEQ):
                self.dma("pool", self.ckvc[:, 0:npast, :],
                         o["cache_mla_ckv"].ap()[lj, b].rearrange("(t p) d -> p t d", p=128),
                         writes=[("h_bf", s) for s in range(4)])
                for kc in range(2):
                    for g in range(0, npast, 4):
                        pT, pk = self.nextT()
                        for i in range(4):
                            self.tr(pT[:, i * 128:(i + 1) * 128], self.ckvc[:, g + i, kc * 128:(kc + 1) * 128],
                                    [("h_bf", s) for s in range(4)], [pk])
                        self.evac(self.ckvTs[:, kc, g * 128:(g + 4) * 128], pT[:, 0:512], [pk], [("hT", kk2) for kk2 in range(8)])
                self.dma("pool", self.krc[:, 0:npast, :],
                         o["cache_mla_krope"].ap()[lj, b].rearrange("(t p) d -> p t d", p=128), writes=["krc"])
                for g in range(0, npast, 4):
                    pT, pk = self.nextT()
                    for i in range(4):
                        self.tr(pT[0:32, i * 128:(i + 1) * 128], self.krc[:, g + i, :], ["krc"], [pk])
                    for r in range(2):
                        self.evac(self.KR[r * 64:r * 64 + 32, g * 128:(g + 4) * 128], pT[0:32, 0:512], [pk], [("KR", g // 4)])
                self.cp(self.vnew0[0:DS, :, :], self.vnew[32 * b:32 * b + DS, :, :], ["vnew"], ["vnew0"])
                for c in range(8):
                    self.attn_sample_pair(l, b, c, True)
                self.pipe_flush()
        self.pipe_flush()
        self.phase("mla out")
        self.out_proj_residual(W["o"], tl, self.oT, None, 8, keyfn=self.ko)

    def load_mem_kv(self, ksrc, vsrc, reads):
        self.dma("pool", self.h_bf[:, 0:2, :], ksrc.rearrange("(t p) d -> p t d", p=128), reads=reads,
                 writes=[("h_bf", 0), ("h_bf", 1)])
        self.dma("pool", self.MV[:, :, :], vsrc.rearrange("(t p) d -> p t d", p=128), reads=reads, writes=["MV"])
        for g in range(0, 8, 2):
            pT, pk = self.nextT()
            for i in range(2):
                for t in range(2):
                    self.tr(pT[:, i * 256 + t * 128: i * 256 + (t + 1) * 128], self.h_bf[:, t, (g + i) * 128:(g + i + 1) * 128],
                            [("h_bf", 0), ("h_bf", 1)], [pk])
            self.evac(self.MKT[:, g:g + 2, :], pT[:, 0:512].rearrange("p (i k) -> p i k", i=2), [pk], ["MKT"])

    def cross_heads(self, n0, n1):
        for hx in range(4):
            pts = []
            for mkt in range(2):
                psS, sk = self.nextS()
                for dc in range(2):
                    self.mm(psS[:, n0:n1], self.MKT[:, hx * 2 + dc, mkt * 128:(mkt + 1) * 128], self.qT[:, hx * 2 + dc, n0:n1],
                            dc == 0, dc == 1, ["MKT", self.kq(hx * 2 + dc)], [sk])
                PT, ptk = self.nextPT()
                self.act(PT[:, n0:n1], psS[:, n0:n1], AF.Exp, [sk], [ptk], scale=X_SCALE)
                pts.append((PT, ptk))
            for mkt in range(2):
                self.mm(self.psM[:, n0:n1], self.ones_b[:, :], pts[mkt][0][:, n0:n1], mkt == 0, mkt == 1,
                        [pts[mkt][1], "cst"], ["psM"])
            self.recip(self.rden[:, n0:n1], self.psM[:, n0:n1], ["psM"], ["bcg"])
            for dc in range(2):
                psO, ok = self.nextO()
                for mkt in range(2):
                    self.mm(psO[:, n0:n1], self.MV[:, mkt, hx * 256 + dc * 128: hx * 256 + (dc + 1) * 128],
                            pts[mkt][0][:, n0:n1], mkt == 0, mkt == 1, ["MV", pts[mkt][1]], [ok])
                self.tt(self.oT[:, hx * 2 + dc, n0:n1], psO[:, n0:n1], self.rden[:, n0:n1], ALU.mult,
                        [ok, "bcg"], [self.ko(hx * 2 + dc)])

    def cross_attn(self, l, tl):
        o = self.dram
        W = self.W[l]
        NT = tl.NT
        self.norm_hT(tl, "g_cross", l)
        self.proj_fm(W["xq"], 8, tl, self.hT, "hT", 8,
                     lambda c, ps, pk: self.evac(self.qT[:, c, 0:NT], ps[:, 0:NT], [pk], [self.kq(c)]))
        if tl.kind == "p":
            if tl.j == 0:
                self.load_mem_kv(o["memk_p"].ap()[l], o["memv_p"].ap()[l], [("memkv_out", l)])
            self.cross_heads(0, NT)
        else:
            for c in range(8):
                self.memset(self.oT[:, c, 0:128], 0.0, [self.ko(c)])
            for b in range(NSEQ):
                self.load_mem_kv(o["cache_mem_k"].ap()[l, b], o["cache_mem_v"].ap()[l, b], [])
                self.cross_heads(32 * b, 32 * b + DS)
        self.out_proj_residual(W["xo"], tl, self.oT, None, 8, keyfn=self.ko)

    def ffn(self, l, tl):
        W = self.W[l]
        NT = tl.NT
        self.norm_hT(tl, "g_ffn", l)
        NFC = DFF // 128
        for grp in range(0, NFC, 4):
            n = min(4, NFC - grp)
            gv, gk = self.wnext(W["gu_g"], grp // 4)
            uv, uk = self.wnext(W["gu_u"], grp // 4)
            for i in range(n):
                f = grp + i
                psg, pgk = self.nextA()
                for kc in range(8):
                    self.mm(psg[:, 0:NT], gv(kc, i * 128, i * 128 + 128), self.hT[:, kc, 0:NT], kc == 0, kc == 7,
                            [gk, ("hT", kc)], [pgk])
                psu, puk = self.nextS()
                for kc in range(8):
                    self.mm(psu[:, 0:NT], uv(kc, i * 128, i * 128 + 128), self.hT[:, kc, 0:NT], kc == 0, kc == 7,
                            [uk, ("hT", kc)], [puk])
                sg, sgk = self.nextStage()
                self.act(sg[:, 0:NT], psg[:, 0:NT], AF.Silu, [pgk], [sgk])
                self.tt(self.aT[:, f, 0:NT], psu[:, 0:NT], sg[:, 0:NT], ALU.mult, [puk, sgk], [self.ka(f)])
        Wd = W["down"]
        for half in range(2):
            accs = [(self.psA[0], ("psA", 0)), (self.psA[1], ("psA", 1)), (self.psS[0], ("psS", 0)), (self.psS[1], ("psS", 1))]
            for fb in range(Wd["nblk"] // 2):
                dv, dk = self.wnext(Wd, half * (Wd["nblk"] // 2) + fb)
                nf = min(4, NFC - fb * 4)
                for s in range(tl.nsub):
                    ps, pk = accs[s]
                    for i in range(nf):
                        f = fb * 4 + i
                        self.mm(ps[:, 0:512], self.aT[:, f, s * 128:(s + 1) * 128], dv(i, 0, 512), f == 0, f == NFC - 1,
                                [dk, self.ka(f)], [pk])
            for s in range(tl.nsub):
                ps, pk = accs[s]
                self.tt(self.x_sb[:, s, half * 512:(half + 1) * 512], ps[:, 0:512], self.x_sb[:, s, half * 512:(half + 1) * 512],
                        ALU.add, [pk, "x"], ["x"])

    def memory_kv(self):
        cfg = self.cfg
        o = self.dram
        tl = Tile("p", 0)
        tl.nsub, tl.NT = 2, 256
        for s in range(2):
            self.dma("sp", self.x_sb[:, s, :], o["mem_prompt"].ap()[s * 128:(s + 1) * 128, :], writes=["x"])
        self.rstd_of(lambda s: self.x_sb[:, s, :], 2, D, "x", ["x"])
        for l in range(cfg.L):
            gk = ("gbc", "g_mem")
            self.dma("sp", self.gbc[:, :], self.bcast_row("g_mem", l * D, D), writes=[gk])
            for s in range(2):
                self.stt(self.h_bf[:, s, :], self.x_sb[:, s, :], self.rstd[:, s:s + 1], self.gbc[:, :],
                         ALU.mult, ALU.mult, ["x", "rstd", gk], [("h_bf", s)])
            self.transpose_to(self.h_bf, [("h_bf", s) for s in range(2)], self.hT, "hT", 8, tl)
            for blk in range(4):
                slot = self.rrW
                self.rrW = (self.rrW + 1) % len(self.wbuf)
                src = o["w_x_kv"].ap()[l][:, blk * 512:(blk + 1) * 512].rearrange("(k p) c -> p k c", p=128)
                self.dma("pool", self.wbuf[slot][:, 0:4096].rearrange("p (k c) -> p k c", k=8), src, writes=[("w", slot)])
                for s in range(2):
                    ps, pk = self.nextA()
                    for kc in range(8):
                        self.mm(ps[:, 0:512], self.hT[:, kc, s * 128:(s + 1) * 128],
                                self.wbuf[slot][:, kc * 512:(kc + 1) * 512], kc == 0, kc == 7, [("w", slot), ("hT", kc)], [pk])
                    st, sk = self.nextStage()
                    self.evac(st[:, 0:512], ps[:, 0:512], [pk], [sk])
                    dst = o["memk_p"] if blk < 2 else o["memv_p"]
                    cb = (blk % 2) * 512
                    self.dma("sp", dst.ap()[l][s * 128:(s + 1) * 128, cb:cb + 512], st[:, 0:512], reads=[sk],
                             writes=[("memkv_out", l)])

    def weight_plan(self, l):
        W = self.W[l]
        p = []
        if l % 2 == 0:
            p += [(W["q"], 0), (W["q"], 1), (W["k"], 0), (W["k"], 1), (W["v"], 0), (W["v"], 1), (W["f"], 0),
                  (W["g"], 0), (W["g"], 1), (W["o"], 0), (W["o"], 1)]
        else:
            p += [(W["a"], 0), (W["a"], 1)] + [(W["qb"], i) for i in range(4)] + [(W["o"], 0), (W["o"], 1)]
        if "cross" not in self.cfg.skip:
            p += [(W["xq"], 0), (W["xq"], 1), (W["xo"], 0), (W["xo"], 1)]
        if "ffn" not in self.cfg.skip:
            for g in range(6):
                p += [(W["gu_g"], g), (W["gu_u"], g)]
            nb = W["down"]["nblk"]
            p += [(W["down"], i) for i in range(nb)]
        return p

    def convert_layer(self, l):
        cw = self.conv_weight
        W = {}
        lj = l // 2
        if l % 2 == 0:
            W["q"] = cw(f"wq{l}", "w_fox_in", lj, D, 0, 1024, 512, FOX_IN)
            W["k"] = cw(f"wk{l}", "w_fox_in", lj, D, 1024, 2048, 512, FOX_IN)
            W["v"] = cw(f"wv{l}", "w_fox_in", lj, D, 2048, 3072, 512, FOX_IN)
            W["f"] = cw(f"wf{l}", "w_fox_in", lj, D, 3072, 3088, 16, FOX_IN)
            W["g"] = cw(f"wg{l}", "w_fox_in", lj, D, 3088, 4112, 512, FOX_IN)
            W["o"] = cw(f"wo{l}", "w_fox_out", lj, D, 0, 1024, 512, D)
        else:
            W["a"] = cw(f"wa{l}", "w_mla_a", lj, D, 0, MLA_DOWN, 512, MLA_DOWN)
            W["qb"] = cw(f"wqb{l}", "w_mla_qb", lj, MLA_QL, 0, 1536, 384, 1536)
            for nm, off in (("kvk", 0), ("kvv", 64)):
                scr = self.dscr(f"w{nm}{l}", [1, 128, 2 * 1024], BF16)
                srcw = self.dram["w_mla_kvb"].ap()[lj].rearrange("(k p) (h e) -> p k h e", p=128, e=128)
                dstw = scr.ap()[0].rearrange("p (k h d) -> p k h d", k=2, d=64)
                for kc in range(2):
                    self.dma("pool", dstw[:, kc, :, :], srcw[:, kc, :, off:off + 64], writes=[("wb", f"w{nm}{l}", 0)])
                W[nm] = dict(name=f"w{nm}{l}", KC=2, cb=1024, nblk=1, M=1024, scr=scr)
            W["o"] = cw(f"wo{l}", "w_mla_out", lj, D, 0, 1024, 512, D)
        W["xq"] = cw(f"wxq{l}", "w_x_q", l, D, 0, 1024, 512, D)
        W["xo"] = cw(f"wxo{l}", "w_x_o", l, D, 0, 1024, 512, D)
        W["gu_g"] = cw(f"wgg{l}", "w_ffn_gu", l, D, 0, DFF, 512, 2 * DFF)
        W["gu_u"] = cw(f"wgu{l}", "w_ffn_gu", l, D, DFF, 2 * DFF, 512, 2 * DFF)
        name = f"wd{l}"
        nfb = (DFF // 128 + 3) // 4
        scr = self.dscr(name, [2 * nfb, 128, 4 * 512], BF16)
        src = self.dram["w_ffn_down"].ap()[l]
        for half in range(2):
            for fb in range(nfb):
                nf = min(4, DFF // 128 - fb * 4)
                i = src[fb * 512: fb * 512 + nf * 128, half * 512:(half + 1) * 512].rearrange("(k p) c -> p k c", p=128)
                ov = scr.ap()[half * nfb + fb].rearrange("p (k c) -> p k c", k=4)[:, 0:nf, :]
                self.dma("pool", ov, i, writes=[("wb", name, half * nfb + fb)])
        W["down"] = dict(name=name, KC=4, cb=512, nblk=2 * nfb, M=1024, scr=scr)
        self.W[l] = W

    def phase(self, name):
        self.nphase = getattr(self, "nphase", 0) + 1
        if self.cfg.stop is not None and self.nphase > self.cfg.stop:
            print("STOP before phase", self.nphase, name)
            raise StopBuild()
        if self.cfg.stop is not None:
            print("phase", self.nphase, name)

    def layer_tile(self, l, tl):
        cfg = self.cfg
        o = self.dram
        last = l == cfg.L - 1
        self.phase(f"L{l} {tl.kind}{tl.j} load+mixer")
        if tl.kind == "s":
            self.memset(self.x_sb[:, 0, :], 0.0, ["x"])
            src = o["x_sample"] if l == 0 else o["xres_s"]
            self.rows_dma_in(tl, lambda s: self.x_sb[:, s, :], src.ap(), D, writes=["x"])
        else:
            src = o["x_prompt"] if l == 0 else o["xres_p"]
            rd = [] if l == 0 else [("xres_p", tl.j)]
            for s in range(tl.nsub):
                r0 = tl.pos0 + s * 128
                self.dma("sp", self.x_sb[:, s, :], src.ap()[r0:r0 + 128, :], reads=rd, writes=["x"])
        self.wplan(self.weight_plan(l))
        if l % 2 == 0:
            self.fox_mixer(l, tl)
        else:
            self.mla_mixer(l, tl)
        if "cross" not in cfg.skip:
            self.phase(f"L{l} {tl.kind}{tl.j} cross")
            self.cross_attn(l, tl)
        if "ffn" not in cfg.skip:
            self.phase(f"L{l} {tl.kind}{tl.j} ffn")
            self.ffn(l, tl)
        assert self.plan_i == len(self.plan)
        if not last:
            if tl.kind == "s":
                self.rows_dma_out(tl, 0, self.x_sb[:, 0, :], o["xres_s"].ap(), 0, D, ["x"])
            else:
                for s in range(tl.nsub):
                    r0 = tl.pos0 + s * 128
                    self.dma("sp", o["xres_p"].ap()[r0:r0 + 128, :], self.x_sb[:, s, :], reads=["x"],
                             writes=[("xres_p", tl.j)])
        else:
            gk = ("gbc", "g_final")
            self.dma("sp", self.gbc[:, :], self.bcast_row("g_final", 0, D), writes=[gk])
            self.rstd_of(lambda s: self.x_sb[:, s, :], tl.nsub, D, "x", ["x"])
            for s in range(tl.nsub):
                for half in range(2):
                    st, sk = self.nextStage()
                    self.stt(st[:, 0:512], self.x_sb[:, s, half * 512:(half + 1) * 512], self.rstd[:, s:s + 1],
                             self.gbc[:, half * 512:(half + 1) * 512], ALU.mult, ALU.mult, ["x", "rstd", gk], [sk])
                    self.rows_dma_out(tl, s, st[:, 0:512], o["y_p" if tl.kind == "p" else "y_s"].ap(),
                                      half * 512, (half + 1) * 512, [sk])

    def build(self):
        cfg = self.cfg
        T, P, L, NF, NM = cfg.T, cfg.P, cfg.L, cfg.NF, cfg.NM
        din, dout = self.din, self.dout
        din("x_prompt", [T, D]); din("x_sample", [NSEQ * DS, D]); din("mem_prompt", [NMEM, D])
        din("cache_fox_k", [NF, NSEQ, P, D]); din("cache_fox_v", [NF, NSEQ, P, D]); din("cache_fox_logf", [NF, NSEQ, P, NH])
        din("cache_mla_ckv", [max(NM, 1), NSEQ, P, MLA_KVL]); din("cache_mla_krope", [max(NM, 1), NSEQ, P, MLA_R])
        din("cache_mem_k", [L, NSEQ, NMEM, D]); din("cache_mem_v", [L, NSEQ, NMEM, D])
        din("g_mix", [L * D]); din("g_cross", [L * D]); din("g_mem", [L * D]); din("g_ffn", [L * D]); din("g_final", [D])
        din("w_fox_in", [NF, D, FOX_IN]); din("b_fox_f", [NF * NH]); din("w_fox_out", [NF, D, D])
        din("w_mla_a", [max(NM, 1), D, MLA_DOWN]); din("g_mla_q", [max(NM, 1) * MLA_QL]); din("g_mla_kv", [max(NM, 1) * MLA_KVL])
        din("w_mla_qb", [max(NM, 1), MLA_QL, 1536]); din("w_mla_kvb", [max(NM, 1), MLA_KVL, 2048]); din("w_mla_out", [max(NM, 1), D, D])
        din("w_x_q", [L, D, D]); din("w_x_kv", [L, D, 2 * D]); din("w_x_o", [L, D, D])
        din("w_ffn_gu", [L, D, 2 * DFF]); din("w_ffn_down", [L, DFF, D])
        din("cst_bf", [128, 640], BF16); din("cst_f", [128, 256])
        din("rope_tm_p", [T, 32]); din("rope_tm_s", [128, 32]); din("rope_fm_p", [2, 32, T]); din("rope_fm_s", [2, 32, 128])
        dout("y_p", [T, D]); dout("y_s", [NSEQ * DS, D])
        dout("fk_p", [NF * T, D]); dout("fv_p", [NF * T, D]); dout("fl_p", [NF * T, NH])
        dout("mc_p", [max(NM, 1) * T, MLA_KVL]); dout("mr_p", [max(NM, 1) * T, MLA_R])
        dout("memk_p", [L, NMEM, D]); dout("memv_p", [L, NMEM, D])
        dout("fk_s", [NF * NSEQ * DS, D]); dout("fv_s", [NF * NSEQ * DS, D]); dout("fl_s", [NF * NSEQ * DS, NH])
        dout("mc_s", [max(NM, 1) * NSEQ * DS, MLA_KVL]); dout("mr_s", [max(NM, 1) * NSEQ * DS, MLA_R])
        self.dscr("xres_p", [T, D], F32); self.dscr("xres_s", [NSEQ * DS, D], F32)
        self.KTd = self.dscr("KTd", [8, 128, T], BF16)

        sb, ps = self.sb, self.ps
        NKT = max(T, P) // 128
        self.Vd = self.dscr("Vd", [8, 128, T // 128, 130], BF16)
        self.x_sb = sb("x_sb", [128, 4, D], F32)
        self.h_bf = sb("h_bf", [128, 4, D], BF16)
        self.ckvc = self.h_bf[:, :, :].rearrange("p s d -> p (s d)")[:, 0:(P // 128) * MLA_KVL].rearrange("p (t d) -> p t d", d=MLA_KVL)
        self.hT = sb("hT", [128, 8, TS], BF16)
        self.ckvTs = self.hT[:, :, :].rearrange("p c n -> p (c n)")[:, 0:2 * P].rearrange("p (k n) -> p k n", k=2)
        assert (P // 128) * MLA_KVL <= 4 * D and 2 * P <= 8 * TS
        self.R = sb("R", [128, 24, TS], BF16)
        self.qT = self.R[:, 0:8, :]
        self.kT = self.R[:, 8:16, :]
        self.oT = self.R[:, 16:24, :]
        self.aT = self.R[:, 0:22, :]
        self.gT = sb("gT", [128, 8, TS], BF16)
        self.qR = self.gT
        self.vt_bf = sb("vt_bf", [128, 4, NH, 65], BF16)
        self.KC = [sb(f"KC{i}", [128, 1024], BF16) for i in range(3)]
        self.VC = [sb(f"VC{i}", [128, 8, 2, 65], BF16) for i in range(3)]
        self.KR = sb("KR", [128, max(T, P)], BF16)
        self.wbuf = [sb(f"wbuf{i}", [128, 4096], BF16) for i in range(3)]
        self.wkvk = sb("wkvk", [128, 2, 1024], BF16); self.wkvv = sb("wkvv", [128, 2, 1024], BF16)
        self.stage = [sb(f"stage{i}", [128, 512], F32) for i in range(4)]
        self.PT = [sb(f"PT{i}", [128, 512], BF16) for i in range(4)]
        self.gbc = sb("gbc", [128, D], F32)
        self.junk = sb("junk", [128, D], BF16)
        self.ss = sb("ss", [128, 4], F32); self.ss2 = sb("ss2", [128, 4], F32); self.rstd = sb("rstd", [128, 4], F32)
        self.one_col = sb("one_col", [128, 1], F32)
        self.cst_bf = sb("cst_bf_sb", [128, 640], BF16); self.cst_f = sb("cst_f_sb", [128, 256], F32)
        self.ident = self.cst_bf[:, 0:128]; self.maskF = self.cst_bf[:, 128:256]; self.maskM = self.cst_bf[:, 256:384]
        self.ones_b = self.cst_bf[:, 384:512]; self.Rrot = self.cst_bf[:, 512:544]
        self.Umat = self.cst_f[:, 0:128]; self.ones_f = self.cst_f[:, 128:256]
        self.rd = sb("rd", [128, 512], F32); self.bcg = sb("bcg", [128, 512], F32); self.rden = self.bcg
        self.lf = sb("lf", [128, 4, NH], F32); self.bfb = sb("bfb", [128, NH], F32)
        self.c_all = sb("c_all", [128, NKT, NH], F32); self.bias_all = sb("bias_all", [128, NKT, NH], F32)
        self.Sprev = sb("Sprev", [128, NH], F32); self.cref = sb("cref", [128, NH], F32)
        self.lfs = sb("lfs", [128, P // 128, NH], F32); self.lfn0 = sb("lfn0", [DS, NH], F32)
        self.bias_new = sb("bias_new", [DS, NH], F32)
        self.vnew = sb("vnew", [128, NH, 65], BF16); self.vnew0 = sb("vnew0", [DS, NH, 65], BF16)
        self.kc_bf = sb("kc_bf", [128, 8, 128], BF16)
        self.MKT = sb("MKT", [128, 8, NMEM], BF16); self.MV = sb("MV", [128, 2, D], BF16)
        self.a_sb = sb("a_sb", [128, 4, MLA_DOWN], F32)
        self.cq_bf = sb("cq_bf", [128, 4, MLA_QL], BF16); self.cqT = sb("cqT", [128, 3, TS], BF16)
        self.ckv_bf = sb("ckv_bf", [128, 4, MLA_KVL], BF16); self.ckvT = sb("ckvT", [128, 2, TS], BF16)
        self.kr_bf = sb("kr_bf", [128, 4, MLA_R], BF16); self.krT = sb("krT", [128, 128], BF16)
        self.gq = sb("gq", [128, MLA_QL], F32); self.gkv = sb("gkv", [128, MLA_KVL], F32)
        self.rtm = sb("rtm", [128, 4, 32], F32); self.rfm = sb("rfm", [128, 2, TS], F32)
        self.xr = sb("xr", [128, TS], BF16)
        self.t1 = sb("t1", [128, TS], F32); self.t2 = sb("t2", [128, TS], F32)
        self.krc = sb("krc", [128, P // 128, MLA_R], BF16)
        self.psA = [ps(f"psA{i}", [128, 512], F32) for i in range(2)]
        self.psS = [ps(f"psS{i}", [128, 512], F32) for i in range(2)]
        self.psO = [ps(f"psO{i}", [128, 512], F32) for i in range(2)]
        self.psM = ps("psM", [128, 512], F32)
        self.psT = ps("psT", [128, 1024], BF16)

        o = self.dram
        self.dma("sp", self.cst_bf[:, :], o["cst_bf"].ap(), writes=["cst"])
        self.dma("sp", self.cst_f[:, :], o["cst_f"].ap(), writes=["cst"])
        for i in range(3):
            self.memset(self.VC[i][:, :, :, 64:65], 1.0, [("VC", i)])
        self.memset(self.vt_bf[:, :, :, 64:65], 1.0, [("vt_bf", s) for s in range(4)])
        self.memset(self.vnew[:, :, 64:65], 1.0, ["vnew"])
        self.memset(self.one_col[:, :], 1.0, ["one_col"])
        self.W = {}
        try:
            self.phase("convert0")
            self.convert_layer(0)
            self.phase("memory_kv")
            self.memory_kv()
            tiles = [Tile("s", 0)] + [Tile("p", j) for j in range(T // TS)]
            for l in range(L):
                if l + 1 < L:
                    self.convert_layer(l + 1)
                if l % 2 == 0:
                    self.memset(self.Sprev[:, :], 0.0, ["Sprev"])
                for tl in tiles:
                    if tl.kind == "p" and tl.j == 0 and l % 2 == 0:
                        self.memset(self.Sprev[:, :], 0.0, ["Sprev"])
                    self.layer_tile(l, tl)
        except StopBuild:
            pass
        self.S.emit(self.nc)
        self.es.close()
        return self.nc


def _constants(cfg):
    bf = ml_dtypes.bfloat16
    cb = np.zeros((128, 640), np.float32)
    k = np.arange(128)[:, None]
    q = np.arange(128)[None, :]
    cb[:, 0:128] = np.eye(128)
    cb[:, 128:256] = np.where(k > q, NEG, 0.0)
    cb[:, 256:384] = np.where((k // 64) > (q // 64), NEG, 0.0)
    cb[:, 384:512] = 1.0
    R = np.zeros((32, 32), np.float32)
    for m in range(16):
        R[m + 16, m] = -1.0
        R[m, m + 16] = 1.0
    cb[64:96, 512:544] = R
    cf = np.zeros((128, 256), np.float32)
    cf[:, 0:128] = (k <= q)
    cf[:, 128:256] = 1.0
    half = 16
    inv = (np.float32(10000.0) ** (-(np.arange(half, dtype=np.float32) / np.float32(half)))).astype(np.float32)

    def tables(pos):
        ang = pos.astype(np.float32)[:, None] * inv[None, :]
        return np.cos(ang).astype(np.float32), np.sin(ang).astype(np.float32)
    cp, sp_ = tables(np.arange(cfg.T))
    rope_tm_p = np.concatenate([cp, sp_], axis=1)
    rope_fm_p = np.stack([np.concatenate([cp, cp], 1).T, np.concatenate([sp_, sp_], 1).T]).astype(np.float32)
    cs, ss = tables(cfg.P + np.arange(DS))
    rope_tm_s = np.zeros((128, 32), np.float32)
    rope_fm_s = np.zeros((2, 32, 128), np.float32)
    for b in range(NSEQ):
        rope_tm_s[32 * b:32 * b + DS] = np.concatenate([cs, ss], 1)
        rope_fm_s[0, :, 32 * b:32 * b + DS] = np.concatenate([cs, cs], 1).T
        rope_fm_s[1, :, 32 * b:32 * b + DS] = np.concatenate([ss, ss], 1).T
    return dict(cst_bf=cb.astype(bf), cst_f=cf, rope_tm_p=np.ascontiguousarray(rope_tm_p),
                rope_tm_s=rope_tm_s, rope_fm_p=np.ascontiguousarray(rope_fm_p), rope_fm_s=rope_fm_s)


def run(cfg, inputs):
    nc = Builder(cfg).build()
    T, P, L, NF, NM = cfg.T, cfg.P, cfg.L, cfg.NF, cfg.NM
    consts = _constants(cfg)
    c32 = lambda a: np.ascontiguousarray(a, dtype=np.float32)
    shared = {}
    for k in ("g_mix", "g_cross", "g_mem", "g_ffn", "g_final", "b_fox_f", "g_mla_q", "g_mla_kv"):
        shared[k] = c32(inputs[k]).reshape(-1)
    for k in ("w_fox_in", "w_fox_out", "w_mla_a", "w_mla_qb", "w_mla_kvb", "w_mla_out", "w_x_q", "w_x_kv", "w_x_o",
              "w_ffn_gu", "w_ffn_down"):
        shared[k] = c32(inputs[k])
    shared.update(consts)
    in_maps = []
    for c in range(cfg.ncores):
        m = dict(shared)
        m["x_prompt"] = c32(inputs["x_prompt"][c])
        m["x_sample"] = c32(inputs["x_sample"][NSEQ * c:NSEQ * (c + 1)]).reshape(NSEQ * DS, D)
        m["mem_prompt"] = c32(inputs["mem_prompt"][c])
        sl = slice(NSEQ * c, NSEQ * (c + 1))
        m["cache_fox_k"] = c32(inputs["cache_fox_k"][:, sl]).reshape(NF, NSEQ, P, D)
        m["cache_fox_v"] = c32(inputs["cache_fox_v"][:, sl]).reshape(NF, NSEQ, P, D)
        m["cache_fox_logf"] = c32(inputs["cache_fox_logf"][:, sl])
        m["cache_mla_ckv"] = c32(inputs["cache_mla_ckv"][:, sl])
        m["cache_mla_krope"] = c32(inputs["cache_mla_krope"][:, sl])
        m["cache_mem_k"] = c32(inputs["cache_mem_k"][:, sl]).reshape(L, NSEQ, NMEM, D)
        m["cache_mem_v"] = c32(inputs["cache_mem_v"][:, sl]).reshape(L, NSEQ, NMEM, D)
        in_maps.append(m)
    res = run_bass_kernel_spmd(nc, in_maps, core_ids=list(range(cfg.ncores)))
    R = res.results
    B = cfg.ncores

    def gat(name, shape_per_core, axis):
        return np.stack([np.asarray(R[c][name], dtype=np.float32).reshape(shape_per_core) for c in range(B)], axis=axis)

    y_p = gat("y_p", (T, D), 0)
    y_s = gat("y_s", (NSEQ, DS, D), 0).reshape(B * NSEQ, DS, D)
    fk_p = gat("fk_p", (NF, T, NH, HD), 1)
    fv_p = gat("fv_p", (NF, T, NH, HD), 1)
    fl_p = gat("fl_p", (NF, T, NH), 1)
    mc_p = gat("mc_p", (NM, T, MLA_KVL), 1)
    mr_p = gat("mr_p", (NM, T, MLA_R), 1)
    mk_p = gat("memk_p", (L, NMEM, 4, 256), 1)
    mv_p = gat("memv_p", (L, NMEM, 4, 256), 1)
    fk_s = gat("fk_s", (NF, NSEQ, DS, NH, HD), 1).reshape(NF, B * NSEQ, DS, NH, HD)
    fv_s = gat("fv_s", (NF, NSEQ, DS, NH, HD), 1).reshape(NF, B * NSEQ, DS, NH, HD)
    fl_s = gat("fl_s", (NF, NSEQ, DS, NH), 1).reshape(NF, B * NSEQ, DS, NH)
    mc_s = gat("mc_s", (NM, NSEQ, DS, MLA_KVL), 1).reshape(NM, B * NSEQ, DS, MLA_KVL)
    mr_s = gat("mr_s", (NM, NSEQ, DS, MLA_R), 1).reshape(NM, B * NSEQ, DS, MLA_R)
    return (y_p, y_s, fk_p, fv_p, fl_p, mc_p, mr_p, mk_p, mv_p, fk_s, fv_s, fl_s, mc_s, mr_s)


def kernel(**inputs):
    cfg = Cfg(T=4096, P=2048, L=4, ncores=8)
    return run(cfg, inputs)
```

```python
import numpy as np
import ml_dtypes
from contextlib import ExitStack
import concourse.bass as bass
import concourse.mybir as mybir
from concourse.bass_utils import run_bass_kernel_spmd

F32 = mybir.dt.float32
BF16 = mybir.dt.bfloat16
AF = mybir.ActivationFunctionType
ALU = mybir.AluOpType

ENGINES = ("pe", "act", "dve", "pool", "sp")
N_DMA_SEMS = 20


class Op:
    __slots__ = ("eng", "fn", "deps", "needs_inc", "count", "idx", "dma", "dsem", "dval")

    def __init__(self, eng, fn):
        self.eng = eng
        self.fn = fn
        self.deps = []
        self.needs_inc = False
        self.count = None
        self.idx = None
        self.dma = False
        self.dsem = None
        self.dval = None


class Sched:
    def __init__(self):
        self.streams = {e: [] for e in ENGINES}
        self.state = {}
        self.seen = {e: {} for e in ENGINES}
        self.dma_rr = {e: 0 for e in ENGINES}
        self.dma_last = {e: [None] * N_DMA_SEMS for e in ENGINES}
        self.dma_cnt = {e: [0] * N_DMA_SEMS for e in ENGINES}

    def _add_dep(self, op, d):
        if d is None or d is op:
            return
        if d.dma:
            key = ("d", id(d))
            if key in self.seen[op.eng]:
                return
            self.seen[op.eng][key] = True
            op.deps.append(d)
            return
        if d.eng == op.eng and op.eng == "pe" and not op.dma:
            return
        prev = self.seen[op.eng].get(d.eng, -1)
        if d.idx <= prev:
            return
        self.seen[op.eng][d.eng] = d.idx
        d.needs_inc = True
        op.deps.append(d)

    @staticmethod
    def _is_psum(k):
        return k == "psM" or (isinstance(k, tuple) and k[0] in ("psA", "psS", "psO", "psT"))

    def op(self, eng, fn, reads=(), writes=(), dma=False):
        excl = [k for k in reads if self._is_psum(k)]
        if excl:
            reads = [k for k in reads if not self._is_psum(k)]
            writes = list(writes) + [k for k in excl if k not in writes]
        o = Op(eng, fn)
        o.dma = dma
        o.idx = len(self.streams[eng])
        if dma:
            s = self.dma_rr[eng]
            self.dma_rr[eng] = (s + 1) % N_DMA_SEMS
            prev = self.dma_last[eng][s]
            if prev is not None:
                key = ("d", id(prev))
                if key not in self.seen[eng]:
                    self.seen[eng][key] = True
                    o.deps.append(prev)
            self.dma_cnt[eng][s] += 1
            o.dsem = s
            o.dval = 16 * self.dma_cnt[eng][s]
            self.dma_last[eng][s] = o
        for k in reads:
            st = self.state.get(k)
            if st is not None:
                self._add_dep(o, st[0])
        for k in writes:
            st = self.state.get(k)
            if st is not None:
                self._add_dep(o, st[0])
                for r in st[1]:
                    self._add_dep(o, r)
        for k in reads:
            st = self.state.setdefault(k, [None, []])
            st[1].append(o)
        for k in writes:
            self.state[k] = [o, []]
        self.streams[eng].append(o)
        return o

    def emit(self, nc):
        with ExitStack() as es:
            prog = {e: es.enter_context(nc.semaphore(f"prog_{e}")) for e in ENGINES}
            dsem = {e: [es.enter_context(nc.semaphore(f"dma_{e}_{i}")) for i in range(N_DMA_SEMS)]
                    for e in ("sp", "act", "pool")}
            for e in ENGINES:
                c = 0
                for o in self.streams[e]:
                    if o.needs_inc and not o.dma:
                        c += 1
                        o.count = c
            block = es.enter_context(nc.Block())

            def run(ename, eng):
                for o in self.streams[ename]:
                    for d in o.deps:
                        if d.dma:
                            eng.wait_ge(dsem[d.eng][d.dsem], d.dval)
                        else:
                            eng.wait_ge(prog[d.eng], d.count)
                    ins = o.fn(eng)
                    if o.dma:
                        ins.then_inc(dsem[ename][o.dsem], 16)
                    elif o.needs_inc:
                        ins.then_inc(prog[ename], 1)
                if ename in dsem:
                    for s in range(N_DMA_SEMS):
                        last = self.dma_last[ename][s]
                        if last is not None:
                            eng.wait_ge(dsem[ename][s], last.dval)

            @block.sync
            def _(eng):
                run("sp", eng)

            @block.scalar
            def _(eng):
                run("act", eng)

            @block.vector
            def _(eng):
                run("dve", eng)

            @block.gpsimd
            def _(eng):
                run("pool", eng)

            @block.tensor
            def _(eng):
                run("pe", eng)


D = 1024
NH = 16
HD = 64
FOX_IN = 4112
MLA_QL, MLA_KVL, MLA_R = 384, 256, 32
MLA_DOWN = 672
MLA_SCALE = float(96 ** -0.5)
FOX_SCALE = 0.125
X_SCALE = 1.0 / 16.0
DFF = 2816
NMEM = 256
EPS = 1e-6
NEG = -30000.0
TS = 512
DS = 16
NSEQ = 4


class Cfg:
    def __init__(self, T=4096, P=2048, L=4, ncores=8):
        self.T, self.P, self.L, self.ncores = T, P, L, ncores
        self.skip = set()
        self.stop = None
        self.NF = (L + 1) // 2
        self.NM = L // 2
        assert T % TS == 0 and P % 1024 == 0


class StopBuild(Exception):
    pass


class Tile:
    def __init__(self, kind, j):
        self.kind = kind
        self.j = j
        if kind == "p":
            self.nsub, self.NT, self.pos0 = TS // 128, TS, j * TS
        else:
            self.nsub, self.NT, self.pos0 = 1, 128, 0


class Builder:
    def __init__(self, cfg):
        self.cfg = cfg
        self.nc = bass.Bass("TRN2", target_bir_lowering=False)
        self.S = Sched()
        self.es = ExitStack()
        self.dram = {}
        self.rrA = 0
        self.rrE = 0
        self.rrW = 0
        self.rrStage = 0
        self.rrPT = 0
        self.rrS = 0
        self.rrO = 0
        self.rrKT = 0
        self.rrT = 0
        self.rrAcc = 0
        self.rrKV = 0
        self.kv_w_loaded = {}
        self.pend = None
        self.deferred = []
        self.rrQM = 0
        self.cur_qm = None
        self.cur_qrm = None

    def kq(self, c):
        return ("R", c)

    def kk(self, c):
        return ("R", 8 + c)

    def ko(self, c):
        return ("R", 16 + c)

    def ka(self, f):
        return ("R", f)

    def din(self, name, shape, dt=F32):
        t = self.nc.dram_tensor(name, list(shape), dt, kind="ExternalInput")
        self.dram[name] = t
        return t

    def dout(self, name, shape, dt=F32):
        t = self.nc.dram_tensor(name, list(shape), dt, kind="ExternalOutput")
        self.dram[name] = t
        return t

    def dscr(self, name, shape, dt):
        t = self.nc.dram_tensor(name, list(shape), dt)
        self.dram[name] = t
        return t

    def sb(self, name, shape, dt):
        return self.es.enter_context(self.nc.sbuf_tensor(name, list(shape), dt))

    def ps(self, name, shape, dt):
        return self.es.enter_context(self.nc.psum_tensor(name, list(shape), dt))

    def dma(self, q, out, in_, reads=(), writes=()):
        return self.S.op(q, lambda e: e.dma_start(out=out, in_=in_), reads, writes, dma=True)

    def mm(self, out, lhsT, rhs, start, stop, reads, writes):
        return self.S.op("pe", lambda e: e.matmul(out, lhsT=lhsT, rhs=rhs, start=start, stop=stop),
                         reads, writes)

    def tr(self, out, in_, reads, writes):
        idn = self.ident[0:in_.shape[0], 0:in_.shape[0]]
        return self.S.op("pe", lambda e: e.transpose(out, in_, idn), list(reads) + ["cst"], writes)

    def act(self, out, in_, func, reads, writes, bias=None, scale=None, accum_out=None):
        kw = {}
        if bias is not None:
            kw["bias"] = bias
        if scale is not None:
            kw["scale"] = scale
        if accum_out is not None:
            kw["accum_out"] = accum_out
        return self.S.op("act", lambda e: e.activation(out=out, in_=in_, func=func, **kw), reads, writes)

    def tt(self, out, in0, in1, op, reads, writes, eng="dve"):
        return self.S.op(eng, lambda e: e.tensor_tensor(out=out, in0=in0, in1=in1, op=op), reads, writes)

    def ts(self, out, in0, s1, s2, op0, op1, reads, writes, eng="dve"):
        if op1 is None:
            return self.S.op(eng, lambda e: e.tensor_scalar(out=out, in0=in0, scalar1=s1, scalar2=None, op0=op0),
                             reads, writes)
        return self.S.op(eng, lambda e: e.tensor_scalar(out=out, in0=in0, scalar1=s1, scalar2=s2, op0=op0, op1=op1),
                         reads, writes)

    def stt(self, out, in0, scalar, in1, op0, op1, reads, writes):
        return self.S.op("dve", lambda e: e.scalar_tensor_tensor(out=out, in0=in0, scalar=scalar, in1=in1,
                                                                  op0=op0, op1=op1), reads, writes)

    def cp(self, out, in_, reads, writes, eng="dve"):
        if eng == "act":
            return self.S.op("act", lambda e: e.copy(out=out, in_=in_), reads, writes)
        return self.S.op(eng, lambda e: e.tensor_copy(out=out, in_=in_), reads, writes)

    def evac(self, out, in_, reads, writes):
        self.rrE ^= 1
        return self.cp(out, in_, reads, writes, eng=("act" if self.rrE else "dve"))

    def recip(self, out, in_, reads, writes):
        return self.S.op("dve", lambda e: e.reciprocal(out=out, in_=in_), reads, writes)

    def memset(self, ap, val, writes, eng="pool"):
        return self.S.op(eng, lambda e: e.memset(ap, val), (), writes)

    def nextA(self):
        self.rrA ^= 1
        return self.psA[self.rrA], ("psA", self.rrA)

    def nextS(self):
        self.rrS ^= 1
        return self.psS[self.rrS], ("psS", self.rrS)

    def nextO(self):
        self.rrO ^= 1
        return self.psO[self.rrO], ("psO", self.rrO)

    def nextPT(self):
        self.rrPT = (self.rrPT + 1) % len(self.PT)
        return self.PT[self.rrPT], ("PT", self.rrPT)

    def nextT(self):
        return self.psT[:, 0:512], ("psT", 0)

    def nextStage(self):
        self.rrStage = (self.rrStage + 1) % len(self.stage)
        return self.stage[self.rrStage], ("stage", self.rrStage)

    def bcast_row(self, tname, row_off, n):
        return bass.AP(self.dram[tname], row_off, [[0, 128], [1, n]])

    def conv_weight(self, name, src_name, src_l, K, c0, c1, cb, src_cols):
        KC = K // 128
        M = c1 - c0
        nblk = (M + cb - 1) // cb
        scr = self.dscr(name, [nblk, 128, KC * cb], BF16)
        src = self.dram[src_name].ap()[src_l]
        for b in range(nblk):
            w = min(cb, M - b * cb)
            o = scr.ap()[b].rearrange("p (k c) -> p k c", k=KC)[:, :, 0:w]
            i = src[:, c0 + b * cb: c0 + b * cb + w].rearrange("(k p) c -> p k c", p=128)
            self.dma("pool", o, i, reads=(), writes=[("wb", name, b)])
        return dict(name=name, KC=KC, cb=cb, nblk=nblk, M=M, scr=scr)

    def wplan(self, items):
        self.plan = list(items)
        self.plan_i = 0
        self.plan_loaded = 0
        self.loaded = {}

    def _issue_load(self, i):
        W, b = self.plan[i]
        slot = self.rrW
        self.rrW = (self.rrW + 1) % len(self.wbuf)
        n = W["KC"] * W["cb"]
        w = min(W["cb"], W["M"] - b * W["cb"]) if W["name"][:2] != "wd" else W["cb"]
        kcv = W["KC"]
        if W["name"][:2] == "wd":
            nfb = W["nblk"] // 2
            kcv = min(4, DFF // 128 - (b % nfb) * 4)
        self.dma("sp", self.wbuf[slot][:, 0:n].rearrange("p (k c) -> p k c", k=W["KC"])[:, 0:kcv, 0:w],
                 W["scr"].ap()[b].rearrange("p (k c) -> p k c", k=W["KC"])[:, 0:kcv, 0:w],
                 reads=[("wb", W["name"], b)], writes=[("w", slot)])
        self.loaded[i] = slot

    def wnext(self, W, b):
        i = self.plan_i
        assert self.plan[i][0] is W and self.plan[i][1] == b, (self.plan[i][0]["name"], self.plan[i][1], W["name"], b)
        while self.plan_loaded < min(len(self.plan), i + 2):
            self._issue_load(self.plan_loaded)
            self.plan_loaded += 1
        slot = self.loaded.pop(i)
        self.plan_i += 1
        cb = W["cb"]
        buf = self.wbuf[slot]

        def view(kc, a, b2):
            return buf[:, kc * cb + a: kc * cb + b2]
        return view, ("w", slot)

    def rstd_of(self, src_fn, nsub, n, rkey, reads):
        for s in range(nsub):
            self.act(self.junk[:, 0:n], src_fn(s), AF.Square, reads, ["junk", (rkey, "ss", s)],
                     accum_out=self.ss[:, s:s + 1])
        self.ts(self.ss2[:, 0:nsub], self.ss[:, 0:nsub], 1.0 / n, EPS, ALU.mult, ALU.add,
                [(rkey, "ss", s) for s in range(nsub)], ["ss2"])
        self.S.op("act", lambda e: e.sqrt(out=self.ss2[:, 0:nsub], in_=self.ss2[:, 0:nsub]), ["ss2"], ["ss2"])
        self.recip(self.rstd[:, 0:nsub], self.ss2[:, 0:nsub], ["ss2"], ["rstd"])

    def norm_hT(self, tl, gname, l):
        gk = ("gbc", gname)
        self.dma("sp", self.gbc[:, :], self.bcast_row(gname, l * D, D), writes=[gk])
        self.rstd_of(lambda s: self.x_sb[:, s, :], tl.nsub, D, "x", ["x"])
        for s in range(tl.nsub):
            self.stt(self.h_bf[:, s, :], self.x_sb[:, s, :], self.rstd[:, s:s + 1], self.gbc[:, :],
                     ALU.mult, ALU.mult, ["x", "rstd", gk], [("h_bf", s)])
        self.transpose_to(self.h_bf, [("h_bf", s) for s in range(tl.nsub)], self.hT, "hT", 8, tl)

    def transpose_to(self, src, src_keys, dst, dkey, nchunks, tl, csz=128, dst_off=0, keyfn=None):
        for c in range(nchunks):
            pT, pk = self.nextT()
            for s in range(tl.nsub):
                self.tr(pT[0:csz, s * 128:(s + 1) * 128], src[:, s, c * csz:(c + 1) * csz], src_keys, [pk])
            self.evac(dst[0:csz, c, dst_off:dst_off + tl.NT], pT[0:csz, 0:tl.NT], [pk],
                      [keyfn(c) if keyfn else (dkey, c)])

    def rows_dma_in(self, tl, dst_fn, src_t, ncols, row_base=0, writes=()):
        if tl.kind == "p":
            for s in range(tl.nsub):
                r0 = row_base + tl.pos0 + s * 128
                self.dma("sp", dst_fn(s), src_t[r0:r0 + 128, :], writes=writes)
        else:
            for b in range(NSEQ):
                self.dma("sp", dst_fn(0)[32 * b:32 * b + DS, :], src_t[row_base + b * DS: row_base + (b + 1) * DS, :],
                         writes=writes)

    def rows_dma_out(self, tl, s, src_ap, dst_t, c0, c1, reads, row_base=0):
        if tl.kind == "p":
            r0 = row_base + tl.pos0 + s * 128
            self.dma("sp", dst_t[r0:r0 + 128, c0:c1], src_ap, reads=reads)
        else:
            for b in range(NSEQ):
                self.dma("sp", dst_t[row_base + b * DS: row_base + (b + 1) * DS, c0:c1],
                         src_ap[32 * b:32 * b + DS, :], reads=reads)

    def proj_fm(self, W, nchunks, tl, src, skey, KC, sink):
        per_blk = W["cb"] // 128
        for c in range(nchunks):
            blk, cc = divmod(c, per_blk)
            if cc == 0:
                wv, wk = self.wnext(W, blk)
            ps, pk = self.nextA()
            for kc in range(KC):
                self.mm(ps[:, 0:tl.NT], wv(kc, cc * 128, cc * 128 + 128), src[:, kc, 0:tl.NT],
                        kc == 0, kc == KC - 1, [wk] + [(skey, kc)], [pk])
            sink(c, ps, pk)

    def proj_tm(self, W, tl, src, skey, KC, sink, keyfn=None):
        for blk in range(W["nblk"]):
            wv, wk = self.wnext(W, blk)
            w = min(W["cb"], W["M"] - blk * W["cb"])
            for s in range(tl.nsub):
                ps, pk = self.nextA()
                for kc in range(KC):
                    self.mm(ps[:, 0:w], src[:, kc, s * 128:(s + 1) * 128], wv(kc, 0, w),
                            kc == 0, kc == KC - 1, [wk, keyfn(kc) if keyfn else (skey, kc)], [pk])
                sink(blk, s, ps, pk, w)

    def out_proj_residual(self, W, tl, src, skey, KC, keyfn=None):
        def sink(blk, s, ps, pk, w):
            self.tt(self.x_sb[:, s, blk * 512: blk * 512 + w], ps[:, 0:w], self.x_sb[:, s, blk * 512: blk * 512 + w],
                    ALU.add, [pk, "x"], ["x"])
        self.proj_tm(W, tl, src, skey, KC, sink, keyfn=keyfn)

    def next_acc_pair(self, allow_psA):
        if allow_psA:
            self.rrAcc ^= 1
        else:
            self.rrAcc = 0
        if self.rrAcc == 0:
            return [(self.psO[0], ("psO", 0)), (self.psO[1], ("psO", 1))]
        return [(self.psA[0], ("psA", 0)), (self.psA[1], ("psA", 1))]

    def next_kv(self):
        self.rrKV = (self.rrKV + 1) % len(self.KC)
        return self.rrKV

    def normalize_head(self, psO, ok, r, c, q0, q1, n, gate):
        self.recip(self.rd[64:65, 0:n], psO[64:65, 0:n], [ok], ["rd"])
        self.mm(self.psM[0:64, 0:n], self.ones_f[64:65, 0:64], self.rd[64:65, 0:n], True, True, ["rd", "cst"], ["psM"])
        if gate:
            self.tt(self.bcg[0:64, 0:n], self.psM[0:64, 0:n], self.gT[r * 64:(r + 1) * 64, c, q0:q1], ALU.mult,
                    ["psM", ("gT", c)], ["bcg"])
        else:
            self.cp(self.bcg[0:64, 0:n], self.psM[0:64, 0:n], ["psM"], ["bcg"], eng="act")
        self.tt(self.oT[r * 64:(r + 1) * 64, c, q0:q1], psO[0:64, 0:n], self.bcg[0:64, 0:n], ALU.mult,
                [ok, "bcg"], [self.ko(c)])

    def pipe_push(self, pv):
        prev, self.pend = self.pend, pv
        if prev is not None:
            prev()
        d, self.deferred = self.deferred, []
        for fn in d:
            fn()

    def pipe_flush(self):
        prev, self.pend = self.pend, None
        if prev is not None:
            prev()
        d, self.deferred = self.deferred, []
        for fn in d:
            fn()

    def prep_qmask(self, c, q0, q1, mla):
        if mla:
            base = 0
        else:
            self.rrQM ^= 1
            base = 2 * self.rrQM
        self.cur_qm, self.cur_qrm = [], []
        for r in range(2):
            buf, key = self.qm[base + r], ("qm", base + r)
            self.cp(buf[r * 64:(r + 1) * 64, q0:q1], self.qT[r * 64:(r + 1) * 64, c, q0:q1], [self.kq(c)], [key], eng="pool")
            self.cur_qm.append((buf, key))
            if mla:
                rbuf, rkey = self.qm[2 + r], ("qm", 2 + r)
                self.cp(rbuf[r * 64:r * 64 + 32, q0:q1], self.qR[r * 64:r * 64 + 32, c, q0:q1], [("gT", c)], [rkey], eng="pool")
                self.cur_qrm.append((rbuf, rkey))

    def score_pv(self, c, r, kt, KCs, kcol, kkey, VCs, vt, vkey, psO, ok, q0, qn, n0, mla, mask, first, last):
        h = 2 * c + r
        scale = MLA_SCALE if mla else FOX_SCALE
        psS, sk = self.nextS()
        diag = mask is not None
        qm, qmk = self.cur_qm[r]
        self.mm(psS[:, n0:qn], KCs[:, kcol:kcol + 128], qm[:, q0 + n0:q0 + qn],
                True, (not mla) and (not diag), [kkey, qmk], [sk])
        if mla:
            qrm, qrk = self.cur_qrm[r]
            self.mm(psS[:, n0:qn], self.KR[:, kt * 128:(kt + 1) * 128],
                    qrm[:, q0 + n0:q0 + qn], False, not diag, [("KR", kt // 4), qrk], [sk])
        if diag:
            self.mm(psS[:, n0:n0 + 128], self.ident[:, :], mask, False, True, ["cst"], [sk])
        PT, ptk = self.nextPT()
        if mla:
            self.act(PT[:, n0:qn], psS[:, n0:qn], AF.Exp, [sk], [ptk], scale=scale)
        else:
            self.act(PT[:, n0:qn], psS[:, n0:qn], AF.Exp, [sk, ("bias", kt)], [ptk],
                     bias=self.bias_all[:, kt, h:h + 1], scale=scale)
        self.pipe_push(lambda: self.mm(psO[0:65, n0:qn], VCs[:, vt, r, 0:65], PT[:, n0:qn], first, last,
                                       [vkey, ptk], [ok]))

    def attn_prompt_pair(self, tl, c, mla):
        NT = tl.NT
        nkt = (tl.j + 1) * (TS // 128)
        accs = self.next_acc_pair(True)
        self.prep_qmask(c, 0, NT, mla)
        mask = self.maskM[:, :] if mla else self.maskF[:, :]
        for g in range((nkt + 7) // 8):
            nt = min(8, nkt - g * 8)
            slot = self.next_kv()
            kkey, vkey = ("KC", slot), ("VC", slot)
            jj = [("KTd", c, j2) for j2 in range(g * 2, min(g * 2 + 2, tl.j + 1))]
            vv = [("Vd", c, j2) for j2 in range(g * 2, min(g * 2 + 2, tl.j + 1))]
            self.dma("sp", self.KC[slot][:, 0:nt * 128], self.KTd.ap()[c][:, g * 1024:g * 1024 + nt * 128],
                     reads=jj, writes=[kkey])
            self.dma("sp", self.VC[slot][:, 0:nt, :, :].rearrange("p t h e -> p t (h e)"),
                     self.Vd.ap()[c][:, g * 8:g * 8 + nt, :], reads=vv, writes=[vkey])
            for t in range(nt):
                for r in range(2):
                    psO, ok = accs[r]
                    kt = g * 8 + t
                    d = kt - tl.j * (TS // 128)
                    n0 = 0 if d < 0 else d * 128
                    self.score_pv(c, r, kt, self.KC[slot], t * 128, kkey, self.VC[slot], t, vkey, psO, ok,
                                  0, NT, n0, mla, mask if d >= 0 else None, kt == 0, kt == nkt - 1)

        def norm():
            for r in range(2):
                psO, ok = accs[r]
                self.normalize_head(psO, ok, r, c, 0, NT, NT, gate=not mla)
        self.deferred.append(norm)

    def attn_sample_pair(self, l, b, c, mla):
        cfg = self.cfg
        lj = l // 2
        o = self.dram
        q0 = 32 * b
        npast = cfg.P // 128
        accs = self.next_acc_pair(not mla)
        self.prep_qmask(c, q0, q0 + DS, mla)
        for g in range(cfg.P // 1024):
            slot = self.next_kv()
            kkey, vkey = ("KC", slot), ("VC", slot)
            if not mla:
                src = o["cache_fox_k"].ap()[lj, b, g * 1024:(g + 1) * 1024, c * 128:(c + 1) * 128].rearrange("(t p) d -> p t d", p=128)
                self.dma("pool", self.kc_bf[:, :, :], src, writes=["kc_bf"])
                for g4 in range(2):
                    pT, pk = self.nextT()
                    for i in range(4):
                        self.tr(pT[:, i * 128:(i + 1) * 128], self.kc_bf[:, g4 * 4 + i, :], ["kc_bf"], [pk])
                    self.evac(self.KC[slot][:, g4 * 512:(g4 + 1) * 512], pT[:, 0:512], [pk], [kkey])
                for r2 in range(2):
                    srcv = o["cache_fox_v"].ap()[lj, b, g * 1024:(g + 1) * 1024, c * 128 + r2 * 64: c * 128 + (r2 + 1) * 64].rearrange(
                        "(t p) d -> p t d", p=128)
                    self.dma("pool", self.VC[slot][:, :, r2, 0:64], srcv, writes=[vkey])
            else:
                for half in range(2):
                    ps, pk = self.nextA()
                    for kc in range(2):
                        self.mm(ps[:, 0:512], self.wkvk[:, kc, c * 128:(c + 1) * 128],
                                self.ckvTs[:, kc, g * 1024 + half * 512: g * 1024 + (half + 1) * 512],
                                kc == 0, kc == 1, ["wkvk", ("hT", kc)], [pk])
                    self.evac(self.KC[slot][:, half * 512:(half + 1) * 512], ps[:, 0:512], [pk], [kkey])
                for t4 in range(2):
                    ps, pk = self.nextA()
                    for i in range(4):
                        t = t4 * 4 + i
                        for kc in range(2):
                            self.mm(ps[:, i * 128:(i + 1) * 128], self.ckvTs[:, kc, (g * 8 + t) * 128:(g * 8 + t + 1) * 128],
                                    self.wkvv[:, kc, c * 128:(c + 1) * 128], kc == 0, kc == 1, ["wkvv", ("hT", kc)], [pk])
                    self.evac(self.VC[slot][:, t4 * 4:(t4 + 1) * 4, :, 0:64],
                              ps[:, 0:512].rearrange("p (t h d) -> p t h d", t=4, d=64), [pk], [vkey])
            for t in range(8):
                for r in range(2):
                    psO, ok = accs[r]
                    kt = g * 8 + t
                    self.score_pv(c, r, kt, self.KC[slot], t * 128, kkey, self.VC[slot], t, vkey, psO, ok,
                                  q0, DS, 0, mla, None, kt == 0, False)
        for r in range(2):
            h = 2 * c + r
            psO, ok = accs[r]
            scale = MLA_SCALE if mla else FOX_SCALE
            psS, sk = self.nextS()
            PT, ptk = self.nextPT()
            qm, qmk = self.cur_qm[r]
            self.mm(psS[0:DS, 0:DS], self.kT[:, c, q0:q0 + DS], qm[:, q0:q0 + DS],
                    True, False, [self.kk(c), qmk], [sk])
            if mla:
                qrm, qrk = self.cur_qrm[r]
                self.mm(psS[0:DS, 0:DS], self.krT[:, q0:q0 + DS], qrm[:, q0:q0 + DS],
                        False, True, ["krT", qrk], [sk])
                self.act(PT[0:DS, 0:DS], psS[0:DS, 0:DS], AF.Exp, [sk], [ptk], scale=scale)
            else:
                self.mm(psS[0:DS, 0:DS], self.ident[0:DS, 0:DS], self.maskF[0:DS, 0:DS], False, True, ["cst"], [sk])
                self.act(PT[0:DS, 0:DS], psS[0:DS, 0:DS], AF.Exp, [sk, "bias_new"], [ptk],
                         bias=self.bias_new[0:DS, h:h + 1], scale=scale)
            self.pipe_push(lambda psO=psO, ok=ok, h=h, PT=PT, ptk=ptk: self.mm(
                psO[0:65, 0:DS], self.vnew0[0:DS, h, 0:65], PT[0:DS, 0:DS], False, True, ["vnew0", ptk], [ok]))

        def norm():
            for r in range(2):
                psO, ok = accs[r]
                self.normalize_head(psO, ok, r, c, q0, q0 + DS, DS, gate=not mla)
        self.deferred.append(norm)
        if mla:
            self.pipe_flush()

    def cumsum_tile(self, lf_ap, lf_key, dst_ap, dst_key):
        self.mm(self.psM[:, 0:16], self.Umat[:, :], lf_ap, True, False, [lf_key, "cst"], ["psM"])
        self.mm(self.psM[:, 0:16], self.ones_f[:, :], self.Sprev[:, :], False, True, ["Sprev", "cst"], ["psM"])
        self.cp(dst_ap, self.psM[:, 0:16], ["psM"], [dst_key])
        self.tt(self.Sprev[:, :], self.Sprev[:, :], lf_ap, ALU.add, ["Sprev", lf_key], ["Sprev"])

    def carry_bcast(self, dst_ap, dst_key):
        self.mm(self.psM[:, 0:16], self.ones_f[:, :], self.Sprev[:, :], True, True, ["Sprev", "cst"], ["psM"])
        self.cp(dst_ap, self.psM[:, 0:16], ["psM"], [dst_key])

    def store_v_tile(self, tl):
        for c in range(8):
            self.dma("sp", self.Vd.ap()[c][:, tl.j * 4:(tl.j + 1) * 4, :],
                     self.vt_bf[:, :, 2 * c:2 * c + 2, :].rearrange("p s h e -> p s (h e)"),
                     reads=[("vt_bf", s) for s in range(4)], writes=[("Vd", c, tl.j)])

    def fox_mixer(self, l, tl):
        cfg = self.cfg
        lj = l // 2
        W = self.W[l]
        NT, nsub = tl.NT, tl.nsub
        o = self.dram
        isp = tl.kind == "p"
        rb = lj * cfg.T if isp else lj * NSEQ * DS
        self.norm_hT(tl, "g_mix", l)
        self.proj_fm(W["q"], 8, tl, self.hT, "hT", 8,
                     lambda c, ps, pk: self.evac(self.qT[:, c, 0:NT], ps[:, 0:NT], [pk], [self.kq(c)]))

        self.phase("fox k")
        def k_sink(blk, s, ps, pk, w):
            st, sk = self.nextStage()
            self.cp(st[:, 0:w], ps[:, 0:w], [pk], [sk], eng="act")
            self.cp(self.h_bf[:, s, blk * 512: blk * 512 + w], st[:, 0:w], [sk], [("h_bf", s)], eng="pool")
            self.rows_dma_out(tl, s, st[:, 0:w], o["fk_p" if isp else "fk_s"].ap(), blk * 512, blk * 512 + w, [sk], row_base=rb)
        self.proj_tm(W["k"], tl, self.hT, "hT", 8, k_sink)
        self.transpose_to(self.h_bf, [("h_bf", s) for s in range(nsub)], self.kT, None, 8, tl, keyfn=self.kk)
        if isp:
            for c in range(8):
                self.dma("sp", self.KTd.ap()[c][:, tl.pos0:tl.pos0 + NT], self.kT[:, c, 0:NT],
                         reads=[self.kk(c)], writes=[("KTd", c, tl.j)])

        self.phase("fox v")
        def v_sink(blk, s, ps, pk, w):
            st, sk = self.nextStage()
            self.cp(st[:, 0:w], ps[:, 0:w], [pk], [sk], eng="act")
            if isp:
                self.cp(self.vt_bf[:, s, blk * 8:(blk + 1) * 8, 0:64], st[:, 0:w].rearrange("p (h d) -> p h d", d=64),
                        [sk], [("vt_bf", s)], eng="pool")
            else:
                self.cp(self.vnew[:, blk * 8:(blk + 1) * 8, 0:64], st[:, 0:w].rearrange("p (h d) -> p h d", d=64),
                        [sk], ["vnew"], eng="pool")
            self.rows_dma_out(tl, s, st[:, 0:w], o["fv_p" if isp else "fv_s"].ap(), blk * 512, blk * 512 + w, [sk], row_base=rb)
        self.proj_tm(W["v"], tl, self.hT, "hT", 8, v_sink)
        if isp:
            self.store_v_tile(tl)

        self.phase("fox f")
        bk = "bfb"
        self.dma("sp", self.bfb[:, :], self.bcast_row("b_fox_f", lj * NH, NH), writes=[bk])
        wv, wk = self.wnext(W["f"], 0)
        for s in range(nsub):
            for kc in range(8):
                self.mm(self.psM[:, 0:16], self.hT[:, kc, s * 128:(s + 1) * 128], wv(kc, 0, 16), kc == 0, kc == 7,
                        [wk, ("hT", kc)], ["psM"])
            self.tt(self.lf[:, s, :], self.psM[:, 0:16], self.bfb[:, :], ALU.add, ["psM", bk], [("lf", s)])
            self.act(self.lf[:, s, :], self.lf[:, s, :], AF.Exp, [("lf", s)], [("lf", s)], scale=-1.0)
            self.act(self.lf[:, s, :], self.lf[:, s, :], AF.Ln, [("lf", s), "one_col"], [("lf", s)], bias=self.one_col[:, 0:1], scale=1.0)
            self.ts(self.lf[:, s, :], self.lf[:, s, :], -1.0, None, ALU.mult, None, [("lf", s)], [("lf", s)])
            self.rows_dma_out(tl, s, self.lf[:, s, :], o["fl_p" if isp else "fl_s"].ap(), 0, NH, [("lf", s)], row_base=rb)

        self.phase("fox g")
        self.proj_fm(W["g"], 8, tl, self.hT, "hT", 8,
                     lambda c, ps, pk: self.act(self.gT[:, c, 0:NT], ps[:, 0:NT], AF.Sigmoid, [pk], [("gT", c)]))

        self.phase("fox attn")
        if isp:
            nkt = (tl.j + 1) * 4
            for s in range(nsub):
                kt = tl.j * 4 + s
                if s == 2:
                    self.carry_bcast(self.cref[:, :], "cref")
                self.cumsum_tile(self.lf[:, s, :], ("lf", s), self.c_all[:, kt, :], ("c_all", kt))
            for kt in range(nkt):
                self.tt(self.bias_all[:, kt, :], self.cref[:, :], self.c_all[:, kt, :], ALU.subtract,
                        ["cref", ("c_all", kt)], [("bias", kt)])
            for c in range(8):
                self.attn_prompt_pair(tl, c, False)
        else:
            npast = cfg.P // 128
            for c in range(8):
                self.memset(self.oT[:, c, 0:128], 0.0, [self.ko(c)])
            for b in range(NSEQ):
                self.dma("sp", self.lfs[:, 0:npast, :],
                         o["cache_fox_logf"].ap()[lj, b].rearrange("(t p) h -> p t h", p=128), writes=["lfs"])
                self.memset(self.Sprev[:, :], 0.0, ["Sprev"])
                for kt in range(npast):
                    self.cumsum_tile(self.lfs[:, kt, :], "lfs", self.c_all[:, kt, :], ("c_all", kt))
                self.carry_bcast(self.cref[:, :], "cref")
                for kt in range(npast):
                    self.tt(self.bias_all[:, kt, :], self.cref[:, :], self.c_all[:, kt, :], ALU.subtract,
                            ["cref", ("c_all", kt)], [("bias", kt)])
                self.phase(f"fox s-attn b{b} newbias")
                self.cp(self.lfn0[0:DS, :], self.lf[32 * b:32 * b + DS, 0, :], [("lf", 0)], ["lfn0"])
                self.mm(self.psM[0:DS, 0:16], self.Umat[0:DS, 0:DS], self.lfn0[0:DS, :], True, True, ["lfn0", "cst"], ["psM"])
                self.ts(self.bias_new[0:DS, :], self.psM[0:DS, 0:16], -1.0, None, ALU.mult, None, ["psM"], ["bias_new"])
                self.cp(self.vnew0[0:DS, :, :], self.vnew[32 * b:32 * b + DS, :, :], ["vnew"], ["vnew0"])
                self.phase(f"fox s-attn b{b} pairs")
                for c in range(8):
                    self.attn_sample_pair(l, b, c, False)
                self.pipe_flush()
        self.pipe_flush()
        self.phase("fox out")
        self.out_proj_residual(W["o"], tl, self.oT, None, 8, keyfn=self.ko)

    def mla_mixer(self, l, tl):
        cfg = self.cfg
        lj = l // 2
        W = self.W[l]
        NT, nsub = tl.NT, tl.nsub
        o = self.dram
        isp = tl.kind == "p"
        rb = lj * cfg.T if isp else lj * NSEQ * DS
        self.norm_hT(tl, "g_mix", l)
        if not self.kv_w_loaded.get(l):
            self.kv_w_loaded[l] = True
            self.dma("sp", self.wkvk[:, :, :].rearrange("p k c -> p (k c)"), W["kvk"]["scr"].ap()[0],
                     reads=[("wb", W["kvk"]["name"], 0)], writes=["wkvk"])
            self.dma("sp", self.wkvv[:, :, :].rearrange("p k c -> p (k c)"), W["kvv"]["scr"].ap()[0],
                     reads=[("wb", W["kvv"]["name"], 0)], writes=["wkvv"])

        def a_sink(blk, s, ps, pk, w):
            self.evac(self.a_sb[:, s, blk * 512: blk * 512 + w], ps[:, 0:w], [pk], [("a_sb", s, blk)])
        self.proj_tm(W["a"], tl, self.hT, "hT", 8, a_sink)
        akeys = [("a_sb", s, blk) for s in range(nsub) for blk in range(2)]
        self.dma("sp", self.gq[:, :], self.bcast_row("g_mla_q", lj * MLA_QL, MLA_QL), writes=["gq"])
        self.dma("sp", self.gkv[:, :], self.bcast_row("g_mla_kv", lj * MLA_KVL, MLA_KVL), writes=["gkv"])
        self.rstd_of(lambda s: self.a_sb[:, s, 0:MLA_QL], nsub, MLA_QL, "cq", akeys)
        for s in range(nsub):
            self.stt(self.cq_bf[:, s, :], self.a_sb[:, s, 0:MLA_QL], self.rstd[:, s:s + 1], self.gq[:, :],
                     ALU.mult, ALU.mult, akeys + ["rstd", "gq"], [("cq_bf", s)])
        self.transpose_to(self.cq_bf, [("cq_bf", s) for s in range(nsub)], self.cqT, "cqT", 3, tl)
        self.rstd_of(lambda s: self.a_sb[:, s, MLA_QL:MLA_QL + MLA_KVL], nsub, MLA_KVL, "ckv", akeys)
        for s in range(nsub):
            st, sk = self.nextStage()
            self.stt(st[:, 0:MLA_KVL], self.a_sb[:, s, MLA_QL:MLA_QL + MLA_KVL], self.rstd[:, s:s + 1], self.gkv[:, :],
                     ALU.mult, ALU.mult, akeys + ["rstd", "gkv"], [sk])
            self.cp(self.ckv_bf[:, s, :], st[:, 0:MLA_KVL], [sk], [("ckv_bf", s)], eng="act")
            self.rows_dma_out(tl, s, st[:, 0:MLA_KVL], o["mc_p" if isp else "mc_s"].ap(), 0, MLA_KVL, [sk], row_base=rb)
        self.transpose_to(self.ckv_bf, [("ckv_bf", s) for s in range(nsub)], self.ckvT, "ckvT", 2, tl)
        if isp:
            self.dma("sp", self.rtm[:, 0:nsub, :],
                     o["rope_tm_p"].ap()[tl.pos0:tl.pos0 + NT, :].rearrange("(s p) c -> p s c", p=128), writes=["rtm"])
        else:
            self.dma("sp", self.rtm[:, 0, :], o["rope_tm_s"].ap(), writes=["rtm"])
        A0 = MLA_QL + MLA_KVL
        for s in range(nsub):
            x1 = self.a_sb[:, s, A0:A0 + 16]
            x2 = self.a_sb[:, s, A0 + 16:A0 + 32]
            cs = self.rtm[:, s, 0:16]
            sn = self.rtm[:, s, 16:32]
            st, sk = self.nextStage()
            rk = akeys + ["rtm"]
            self.tt(st[:, 0:16], x1, cs, ALU.mult, rk, [sk])
            self.tt(st[:, 32:48], x2, sn, ALU.mult, rk, [sk])
            self.tt(st[:, 0:16], st[:, 0:16], st[:, 32:48], ALU.subtract, [sk], [sk])
            self.tt(st[:, 16:32], x1, sn, ALU.mult, rk, [sk])
            self.tt(st[:, 32:48], x2, cs, ALU.mult, rk, [sk])
            self.tt(st[:, 16:32], st[:, 16:32], st[:, 32:48], ALU.add, [sk], [sk])
            self.cp(self.kr_bf[:, s, :], st[:, 0:32], [sk], [("kr_bf", s)], eng="act")
            self.rows_dma_out(tl, s, st[:, 0:32], o["mr_p" if isp else "mr_s"].ap(), 0, MLA_R, [sk], row_base=rb)
        pT, pk = self.nextT()
        for s in range(nsub):
            self.tr(pT[0:32, s * 128:(s + 1) * 128], self.kr_bf[:, s, :], [("kr_bf", s)], [pk])
        for r in range(2):
            if isp:
                self.evac(self.KR[r * 64:r * 64 + 32, tl.pos0:tl.pos0 + NT], pT[0:32, 0:NT], [pk], [("KR", tl.j)])
            else:
                self.evac(self.krT[r * 64:r * 64 + 32, 0:NT], pT[0:32, 0:NT], [pk], ["krT"])

        if isp:
            for i in range(2):
                self.dma("sp", self.rfm[64:96, i, 0:NT], o["rope_fm_p"].ap()[i, :, tl.pos0:tl.pos0 + NT], writes=["rfm"])
        else:
            for i in range(2):
                self.dma("sp", self.rfm[64:96, i, 0:NT], o["rope_fm_s"].ap()[i], writes=["rfm"])
        Wq = W["qb"]
        for h in range(NH):
            blk, hh = divmod(h, 4)
            if hh == 0:
                wv, wk = self.wnext(Wq, blk)
            c, r = divmod(h, 2)
            ps, pk = self.nextA()
            for kc in range(3):
                self.mm(ps[0:96, 0:NT], wv(kc, hh * 96, hh * 96 + 96), self.cqT[:, kc, 0:NT], kc == 0, kc == 2,
                        [wk, ("cqT", kc)], [pk])
            self.cp(self.qT[r * 64:(r + 1) * 64, c, 0:NT], ps[0:64, 0:NT], [pk], [self.kq(c)], eng="act")
            self.cp(self.xr[64:96, 0:NT], ps[64:96, 0:NT], [pk], ["xr"], eng="act")
            self.mm(self.psM[64:96, 0:NT], self.Rrot[64:96, 0:32], self.xr[64:96, 0:NT], True, True, ["xr", "cst"], ["psM"])
            self.tt(self.t1[64:96, 0:NT], ps[64:96, 0:NT], self.rfm[64:96, 0, 0:NT], ALU.mult, [pk, "rfm"], ["t1"])
            self.tt(self.t2[64:96, 0:NT], self.psM[64:96, 0:NT], self.rfm[64:96, 1, 0:NT], ALU.mult, ["psM", "rfm"], ["t2"])
            self.tt(self.qR[r * 64:r * 64 + 32, c, 0:NT], self.t1[64:96, 0:NT], self.t2[64:96, 0:NT], ALU.add,
                    ["t1", "t2"], [("gT", c)])

        for c in range(8):
            ps, pk = self.nextA()
            for kc in range(2):
                self.mm(ps[:, 0:NT], self.wkvk[:, kc, c * 128:(c + 1) * 128], self.ckvT[:, kc, 0:NT], kc == 0, kc == 1,
                        ["wkvk", ("ckvT", kc)], [pk])
            self.evac(self.kT[:, c, 0:NT], ps[:, 0:NT], [pk], [self.kk(c)])
            if isp:
                self.dma("sp", self.KTd.ap()[c][:, tl.pos0:tl.pos0 + NT], self.kT[:, c, 0:NT],
                         reads=[self.kk(c)], writes=[("KTd", c, tl.j)])
        for s in range(nsub):
            for half in range(2):
                ps, pk = self.nextA()
                for kc in range(2):
                    self.mm(ps[:, 0:512], self.ckvT[:, kc, s * 128:(s + 1) * 128], self.wkvv[:, kc, half * 512:(half + 1) * 512],
                            kc == 0, kc == 1, ["wkvv", ("ckvT", kc)], [pk])
                if isp:
                    self.evac(self.vt_bf[:, s, half * 8:(half + 1) * 8, 0:64],
                              ps[:, 0:512].rearrange("p (h d) -> p h d", d=64), [pk], [("vt_bf", s)])
                else:
                    self.evac(self.vnew[:, half * 8:(half + 1) * 8, 0:64],
                              ps[:, 0:512].rearrange("p (h d) -> p h d", d=64), [pk], ["vnew"])
        self.phase("mla attn")
        if isp:
            self.store_v_tile(tl)
            for c in range(8):
                self.attn_prompt_pair(tl, c, True)
        else:
            npast = cfg.P // 128
            for c in range(8):
                self.memset(self.oT[:, c, 0:128], 0.0, [self.ko(c)])
            for b in range(NSEQ):
                self.dma("pool", self.ckvc[:, 0:npast, :],
                         o["cache_mla_ckv"].ap()[lj, b].rearrange("(t p) d -> p t d", p=128),
                         writes=[("h_bf", s) for s in range(4)])
                for kc in range(2):
                    for g in range(0, npast, 4):
                        pT, pk = self.nextT()
                        for i in range(4):
                            self.tr(pT[:, i * 128:(i + 1) * 128], self.ckvc[:, g + i, kc * 128:(kc + 1) * 128],
                                    [("h_bf", s) for s in range(4)], [pk])
                        self.evac(self.ckvTs[:, kc, g * 128:(g + 4) * 128], pT[:, 0:512], [pk], [("hT", kk2) for kk2 in range(8)])
                self.dma("pool", self.krc[:, 0:npast, :],
                         o["cache_mla_krope"].ap()[lj, b].rearrange("(t p) d -> p t d", p=128), writes=["krc"])
                for g in range(0, npast, 4):
                    pT, pk = self.nextT()
                    for i in range(4):
                        self.tr(pT[0:32, i * 128:(i + 1) * 128], self.krc[:, g + i, :], ["krc"], [pk])
                    for r in range(2):
                        self.evac(self.KR[r * 64:r * 64 + 32, g * 128:(g + 4) * 128], pT[0:32, 0:512], [pk], [("KR", g // 4)])
                self.cp(self.vnew0[0:DS, :, :], self.vnew[32 * b:32 * b + DS, :, :], ["vnew"], ["vnew0"])
                for c in range(8):
                    self.attn_sample_pair(l, b, c, True)
                self.pipe_flush()
        self.pipe_flush()
        self.phase("mla out")
        self.out_proj_residual(W["o"], tl, self.oT, None, 8, keyfn=self.ko)

    def load_mem_kv(self, ksrc, vsrc, reads):
        self.dma("pool", self.h_bf[:, 0:2, :], ksrc.rearrange("(t p) d -> p t d", p=128), reads=reads,
                 writes=[("h_bf", 0), ("h_bf", 1)])
        self.dma("pool", self.MV[:, :, :], vsrc.rearrange("(t p) d -> p t d", p=128), reads=reads, writes=["MV"])
        for g in range(0, 8, 2):
            pT, pk = self.nextT()
            for i in range(2):
                for t in range(2):
                    self.tr(pT[:, i * 256 + t * 128: i * 256 + (t + 1) * 128], self.h_bf[:, t, (g + i) * 128:(g + i + 1) * 128],
                            [("h_bf", 0), ("h_bf", 1)], [pk])
            self.evac(self.MKT[:, g:g + 2, :], pT[:, 0:512].rearrange("p (i k) -> p i k", i=2), [pk], ["MKT"])

    def cross_heads(self, n0, n1):
        for hx in range(4):
            pts = []
            for mkt in range(2):
                psS, sk = self.nextS()
                for dc in range(2):
                    self.mm(psS[:, n0:n1], self.MKT[:, hx * 2 + dc, mkt * 128:(mkt + 1) * 128], self.qT[:, hx * 2 + dc, n0:n1],
                            dc == 0, dc == 1, ["MKT", self.kq(hx * 2 + dc)], [sk])
                PT, ptk = self.nextPT()
                self.act(PT[:, n0:n1], psS[:, n0:n1], AF.Exp, [sk], [ptk], scale=X_SCALE)
                pts.append((PT, ptk))
            for mkt in range(2):
                self.mm(self.psM[:, n0:n1], self.ones_b[:, :], pts[mkt][0][:, n0:n1], mkt == 0, mkt == 1,
                        [pts[mkt][1], "cst"], ["psM"])
            self.recip(self.rden[:, n0:n1], self.psM[:, n0:n1], ["psM"], ["bcg"])
            for dc in range(2):
                psO, ok = self.nextO()
                for mkt in range(2):
                    self.mm(psO[:, n0:n1], self.MV[:, mkt, hx * 256 + dc * 128: hx * 256 + (dc + 1) * 128],
                            pts[mkt][0][:, n0:n1], mkt == 0, mkt == 1, ["MV", pts[mkt][1]], [ok])
                self.tt(self.oT[:, hx * 2 + dc, n0:n1], psO[:, n0:n1], self.rden[:, n0:n1], ALU.mult,
                        [ok, "bcg"], [self.ko(hx * 2 + dc)])

    def cross_attn(self, l, tl):
        o = self.dram
        W = self.W[l]
        NT = tl.NT
        self.norm_hT(tl, "g_cross", l)
        self.proj_fm(W["xq"], 8, tl, self.hT, "hT", 8,
                     lambda c, ps, pk: self.evac(self.qT[:, c, 0:NT], ps[:, 0:NT], [pk], [self.kq(c)]))
        if tl.kind == "p":
            if tl.j == 0:
                self.load_mem_kv(o["memk_p"].ap()[l], o["memv_p"].ap()[l], [("memkv_out", l)])
            self.cross_heads(0, NT)
        else:
            for c in range(8):
                self.memset(self.oT[:, c, 0:128], 0.0, [self.ko(c)])
            for b in range(NSEQ):
                self.load_mem_kv(o["cache_mem_k"].ap()[l, b], o["cache_mem_v"].ap()[l, b], [])
                self.cross_heads(32 * b, 32 * b + DS)
        self.out_proj_residual(W["xo"], tl, self.oT, None, 8, keyfn=self.ko)

    def ffn(self, l, tl):
        W = self.W[l]
        NT = tl.NT
        self.norm_hT(tl, "g_ffn", l)
        NFC = DFF // 128
        for grp in range(0, NFC, 4):
            n = min(4, NFC - grp)
            gv, gk = self.wnext(W["gu_g"], grp // 4)
            uv, uk = self.wnext(W["gu_u"], grp // 4)
            for i in range(n):
                f = grp + i
                psg, pgk = self.nextA()
                for kc in range(8):
                    self.mm(psg[:, 0:NT], gv(kc, i * 128, i * 128 + 128), self.hT[:, kc, 0:NT], kc == 0, kc == 7,
                            [gk, ("hT", kc)], [pgk])
                psu, puk = self.nextS()
                for kc in range(8):
                    self.mm(psu[:, 0:NT], uv(kc, i * 128, i * 128 + 128), self.hT[:, kc, 0:NT], kc == 0, kc == 7,
                            [uk, ("hT", kc)], [puk])
                sg, sgk = self.nextStage()
                self.act(sg[:, 0:NT], psg[:, 0:NT], AF.Silu, [pgk], [sgk])
                self.tt(self.aT[:, f, 0:NT], psu[:, 0:NT], sg[:, 0:NT], ALU.mult, [puk, sgk], [self.ka(f)])
        Wd = W["down"]
        for half in range(2):
            accs = [(self.psA[0], ("psA", 0)), (self.psA[1], ("psA", 1)), (self.psS[0], ("psS", 0)), (self.psS[1], ("psS", 1))]
            for fb in range(Wd["nblk"] // 2):
                dv, dk = self.wnext(Wd, half * (Wd["nblk"] // 2) + fb)
                nf = min(4, NFC - fb * 4)
                for s in range(tl.nsub):
                    ps, pk = accs[s]
                    for i in range(nf):
                        f = fb * 4 + i
                        self.mm(ps[:, 0:512], self.aT[:, f, s * 128:(s + 1) * 128], dv(i, 0, 512), f == 0, f == NFC - 1,
                                [dk, self.ka(f)], [pk])
            for s in range(tl.nsub):
                ps, pk = accs[s]
                self.tt(self.x_sb[:, s, half * 512:(half + 1) * 512], ps[:, 0:512], self.x_sb[:, s, half * 512:(half + 1) * 512],
                        ALU.add, [pk, "x"], ["x"])

    def memory_kv(self):
        cfg = self.cfg
        o = self.dram
        tl = Tile("p", 0)
        tl.nsub, tl.NT = 2, 256
        for s in range(2):
            self.dma("sp", self.x_sb[:, s, :], o["mem_prompt"].ap()[s * 128:(s + 1) * 128, :], writes=["x"])
        self.rstd_of(lambda s: self.x_sb[:, s, :], 2, D, "x", ["x"])
        for l in range(cfg.L):
            gk = ("gbc", "g_mem")
            self.dma("sp", self.gbc[:, :], self.bcast_row("g_mem", l * D, D), writes=[gk])
            for s in range(2):
                self.stt(self.h_bf[:, s, :], self.x_sb[:, s, :], self.rstd[:, s:s + 1], self.gbc[:, :],
                         ALU.mult, ALU.mult, ["x", "rstd", gk], [("h_bf", s)])
            self.transpose_to(self.h_bf, [("h_bf", s) for s in range(2)], self.hT, "hT", 8, tl)
            for blk in range(4):
                slot = self.rrW
                self.rrW = (self.rrW + 1) % len(self.wbuf)
                src = o["w_x_kv"].ap()[l][:, blk * 512:(blk + 1) * 512].rearrange("(k p) c -> p k c", p=128)
                self.dma("pool", self.wbuf[slot][:, 0:4096].rearrange("p (k c) -> p k c", k=8), src, writes=[("w", slot)])
                for s in range(2):
                    ps, pk = self.nextA()
                    for kc in range(8):
                        self.mm(ps[:, 0:512], self.hT[:, kc, s * 128:(s + 1) * 128],
                                self.wbuf[slot][:, kc * 512:(kc + 1) * 512], kc == 0, kc == 7, [("w", slot), ("hT", kc)], [pk])
                    st, sk = self.nextStage()
                    self.evac(st[:, 0:512], ps[:, 0:512], [pk], [sk])
                    dst = o["memk_p"] if blk < 2 else o["memv_p"]
                    cb = (blk % 2) * 512
                    self.dma("sp", dst.ap()[l][s * 128:(s + 1) * 128, cb:cb + 512], st[:, 0:512], reads=[sk],
                             writes=[("memkv_out", l)])

    def weight_plan(self, l):
        W = self.W[l]
        p = []
        if l % 2 == 0:
            p += [(W["q"], 0), (W["q"], 1), (W["k"], 0), (W["k"], 1), (W["v"], 0), (W["v"], 1), (W["f"], 0),
                  (W["g"], 0), (W["g"], 1), (W["o"], 0), (W["o"], 1)]
        else:
            p += [(W["a"], 0), (W["a"], 1)] + [(W["qb"], i) for i in range(4)] + [(W["o"], 0), (W["o"], 1)]
        if "cross" not in self.cfg.skip:
            p += [(W["xq"], 0), (W["xq"], 1), (W["xo"], 0), (W["xo"], 1)]
        if "ffn" not in self.cfg.skip:
            for g in range(6):
                p += [(W["gu_g"], g), (W["gu_u"], g)]
            nb = W["down"]["nblk"]
            p += [(W["down"], i) for i in range(nb)]
        return p

    def convert_layer(self, l):
        cw = self.conv_weight
        W = {}
        lj = l // 2
        if l % 2 == 0:
            W["q"] = cw(f"wq{l}", "w_fox_in", lj, D, 0, 1024, 512, FOX_IN)
            W["k"] = cw(f"wk{l}", "w_fox_in", lj, D, 1024, 2048, 512, FOX_IN)
            W["v"] = cw(f"wv{l}", "w_fox_in", lj, D, 2048, 3072, 512, FOX_IN)
            W["f"] = cw(f"wf{l}", "w_fox_in", lj, D, 3072, 3088, 16, FOX_IN)
            W["g"] = cw(f"wg{l}", "w_fox_in", lj, D, 3088, 4112, 512, FOX_IN)
            W["o"] = cw(f"wo{l}", "w_fox_out", lj, D, 0, 1024, 512, D)
        else:
            W["a"] = cw(f"wa{l}", "w_mla_a", lj, D, 0, MLA_DOWN, 512, MLA_DOWN)
            W["qb"] = cw(f"wqb{l}", "w_mla_qb", lj, MLA_QL, 0, 1536, 384, 1536)
            for nm, off in (("kvk", 0), ("kvv", 64)):
                scr = self.dscr(f"w{nm}{l}", [1, 128, 2 * 1024], BF16)
                srcw = self.dram["w_mla_kvb"].ap()[lj].rearrange("(k p) (h e) -> p k h e", p=128, e=128)
                dstw = scr.ap()[0].rearrange("p (k h d) -> p k h d", k=2, d=64)
                for kc in range(2):
                    self.dma("pool", dstw[:, kc, :, :], srcw[:, kc, :, off:off + 64], writes=[("wb", f"w{nm}{l}", 0)])
                W[nm] = dict(name=f"w{nm}{l}", KC=2, cb=1024, nblk=1, M=1024, scr=scr)
            W["o"] = cw(f"wo{l}", "w_mla_out", lj, D, 0, 1024, 512, D)
        W["xq"] = cw(f"wxq{l}", "w_x_q", l, D, 0, 1024, 512, D)
        W["xo"] = cw(f"wxo{l}", "w_x_o", l, D, 0, 1024, 512, D)
        W["gu_g"] = cw(f"wgg{l}", "w_ffn_gu", l, D, 0, DFF, 512, 2 * DFF)
        W["gu_u"] = cw(f"wgu{l}", "w_ffn_gu", l, D, DFF, 2 * DFF, 512, 2 * DFF)
        name = f"wd{l}"
        nfb = (DFF // 128 + 3) // 4
        scr = self.dscr(name, [2 * nfb, 128, 4 * 512], BF16)
        src = self.dram["w_ffn_down"].ap()[l]
        for half in range(2):
            for fb in range(nfb):
                nf = min(4, DFF // 128 - fb * 4)
                i = src[fb * 512: fb * 512 + nf * 128, half * 512:(half + 1) * 512].rearrange("(k p) c -> p k c", p=128)
                ov = scr.ap()[half * nfb + fb].rearrange("p (k c) -> p k c", k=4)[:, 0:nf, :]
                self.dma("pool", ov, i, writes=[("wb", name, half * nfb + fb)])
        W["down"] = dict(name=name, KC=4, cb=512, nblk=2 * nfb, M=1024, scr=scr)
        self.W[l] = W

    def phase(self, name):
        self.nphase = getattr(self, "nphase", 0) + 1
        if self.cfg.stop is not None and self.nphase > self.cfg.stop:
            print("STOP before phase", self.nphase, name)
            raise StopBuild()
        if self.cfg.stop is not None:
            print("phase", self.nphase, name)

    def layer_tile(self, l, tl):
        cfg = self.cfg
        o = self.dram
        last = l == cfg.L - 1
        self.phase(f"L{l} {tl.kind}{tl.j} load+mixer")
        if tl.kind == "s":
            self.memset(self.x_sb[:, 0, :], 0.0, ["x"])
            src = o["x_sample"] if l == 0 else o["xres_s"]
            self.rows_dma_in(tl, lambda s: self.x_sb[:, s, :], src.ap(), D, writes=["x"])
        else:
            src = o["x_prompt"] if l == 0 else o["xres_p"]
            rd = [] if l == 0 else [("xres_p", tl.j)]
            for s in range(tl.nsub):
                r0 = tl.pos0 + s * 128
                self.dma("sp", self.x_sb[:, s, :], src.ap()[r0:r0 + 128, :], reads=rd, writes=["x"])
        self.wplan(self.weight_plan(l))
        if l % 2 == 0:
            self.fox_mixer(l, tl)
        else:
            self.mla_mixer(l, tl)
        if "cross" not in cfg.skip:
            self.phase(f"L{l} {tl.kind}{tl.j} cross")
            self.cross_attn(l, tl)
        if "ffn" not in cfg.skip:
            self.phase(f"L{l} {tl.kind}{tl.j} ffn")
            self.ffn(l, tl)
        assert self.plan_i == len(self.plan)
        if not last:
            if tl.kind == "s":
                self.rows_dma_out(tl, 0, self.x_sb[:, 0, :], o["xres_s"].ap(), 0, D, ["x"])
            else:
                for s in range(tl.nsub):
                    r0 = tl.pos0 + s * 128
                    self.dma("sp", o["xres_p"].ap()[r0:r0 + 128, :], self.x_sb[:, s, :], reads=["x"],
                             writes=[("xres_p", tl.j)])
        else:
            gk = ("gbc", "g_final")
            self.dma("sp", self.gbc[:, :], self.bcast_row("g_final", 0, D), writes=[gk])
            self.rstd_of(lambda s: self.x_sb[:, s, :], tl.nsub, D, "x", ["x"])
            for s in range(tl.nsub):
                for half in range(2):
                    st, sk = self.nextStage()
                    self.stt(st[:, 0:512], self.x_sb[:, s, half * 512:(half + 1) * 512], self.rstd[:, s:s + 1],
                             self.gbc[:, half * 512:(half + 1) * 512], ALU.mult, ALU.mult, ["x", "rstd", gk], [sk])
                    self.rows_dma_out(tl, s, st[:, 0:512], o["y_p" if tl.kind == "p" else "y_s"].ap(),
                                      half * 512, (half + 1) * 512, [sk])

    def build(self):
        cfg = self.cfg
        T, P, L, NF, NM = cfg.T, cfg.P, cfg.L, cfg.NF, cfg.NM
        din, dout = self.din, self.dout
        din("x_prompt", [T, D]); din("x_sample", [NSEQ * DS, D]); din("mem_prompt", [NMEM, D])
        din("cache_fox_k", [NF, NSEQ, P, D]); din("cache_fox_v", [NF, NSEQ, P, D]); din("cache_fox_logf", [NF, NSEQ, P, NH])
        din("cache_mla_ckv", [max(NM, 1), NSEQ, P, MLA_KVL]); din("cache_mla_krope", [max(NM, 1), NSEQ, P, MLA_R])
        din("cache_mem_k", [L, NSEQ, NMEM, D]); din("cache_mem_v", [L, NSEQ, NMEM, D])
        din("g_mix", [L * D]); din("g_cross", [L * D]); din("g_mem", [L * D]); din("g_ffn", [L * D]); din("g_final", [D])
        din("w_fox_in", [NF, D, FOX_IN]); din("b_fox_f", [NF * NH]); din("w_fox_out", [NF, D, D])
        din("w_mla_a", [max(NM, 1), D, MLA_DOWN]); din("g_mla_q", [max(NM, 1) * MLA_QL]); din("g_mla_kv", [max(NM, 1) * MLA_KVL])
        din("w_mla_qb", [max(NM, 1), MLA_QL, 1536]); din("w_mla_kvb", [max(NM, 1), MLA_KVL, 2048]); din("w_mla_out", [max(NM, 1), D, D])
        din("w_x_q", [L, D, D]); din("w_x_kv", [L, D, 2 * D]); din("w_x_o", [L, D, D])
        din("w_ffn_gu", [L, D, 2 * DFF]); din("w_ffn_down", [L, DFF, D])
        din("cst_bf", [128, 640], BF16); din("cst_f", [128, 256])
        din("rope_tm_p", [T, 32]); din("rope_tm_s", [128, 32]); din("rope_fm_p", [2, 32, T]); din("rope_fm_s", [2, 32, 128])
        dout("y_p", [T, D]); dout("y_s", [NSEQ * DS, D])
        dout("fk_p", [NF * T, D]); dout("fv_p", [NF * T, D]); dout("fl_p", [NF * T, NH])
        dout("mc_p", [max(NM, 1) * T, MLA_KVL]); dout("mr_p", [max(NM, 1) * T, MLA_R])
        dout("memk_p", [L, NMEM, D]); dout("memv_p", [L, NMEM, D])
        dout("fk_s", [NF * NSEQ * DS, D]); dout("fv_s", [NF * NSEQ * DS, D]); dout("fl_s", [NF * NSEQ * DS, NH])
        dout("mc_s", [max(NM, 1) * NSEQ * DS, MLA_KVL]); dout("mr_s", [max(NM, 1) * NSEQ * DS, MLA_R])
        self.dscr("xres_p", [T, D], F32); self.dscr("xres_s", [NSEQ * DS, D], F32)
        self.KTd = self.dscr("KTd", [8, 128, T], BF16)

        sb, ps = self.sb, self.ps
        NKT = max(T, P) // 128
        self.Vd = self.dscr("Vd", [8, 128, T // 128, 130], BF16)
        self.x_sb = sb("x_sb", [128, 4, D], F32)
        self.h_bf = sb("h_bf", [128, 4, D], BF16)
        self.ckvc = self.h_bf[:, :, :].rearrange("p s d -> p (s d)")[:, 0:(P // 128) * MLA_KVL].rearrange("p (t d) -> p t d", d=MLA_KVL)
        self.hT = sb("hT", [128, 8, TS], BF16)
        self.ckvTs = self.hT[:, :, :].rearrange("p c n -> p (c n)")[:, 0:2 * P].rearrange("p (k n) -> p k n", k=2)
        assert (P // 128) * MLA_KVL <= 4 * D and 2 * P <= 8 * TS
        self.R = sb("R", [128, 24, TS], BF16)
        self.qT = self.R[:, 0:8, :]
        self.kT = self.R[:, 8:16, :]
        self.oT = self.R[:, 16:24, :]
        self.aT = self.R[:, 0:22, :]
        self.gT = sb("gT", [128, 8, TS], BF16)
        self.qR = self.gT
        self.vt_bf = sb("vt_bf", [128, 4, NH, 65], BF16)
        self.KC = [sb(f"KC{i}", [128, 1024], BF16) for i in range(3)]
        self.VC = [sb(f"VC{i}", [128, 8, 2, 65], BF16) for i in range(3)]
        self.KR = sb("KR", [128, max(T, P)], BF16)
        self.wbuf = [sb(f"wbuf{i}", [128, 4096], BF16) for i in range(3)]
        self.wkvk = sb("wkvk", [128, 2, 1024], BF16); self.wkvv = sb("wkvv", [128, 2, 1024], BF16)
        self.stage = [sb(f"stage{i}", [128, 512], F32) for i in range(4)]
        self.PT = [sb(f"PT{i}", [128, 512], BF16) for i in range(4)]
        self.gbc = sb("gbc", [128, D], F32)
        self.junk = sb("junk", [128, D], BF16)
        self.ss = sb("ss", [128, 4], F32); self.ss2 = sb("ss2", [128, 4], F32); self.rstd = sb("rstd", [128, 4], F32)
        self.one_col = sb("one_col", [128, 1], F32)
        self.cst_bf = sb("cst_bf_sb", [128, 640], BF16); self.cst_f = sb("cst_f_sb", [128, 256], F32)
        self.ident = self.cst_bf[:, 0:128]; self.maskF = self.cst_bf[:, 128:256]; self.maskM = self.cst_bf[:, 256:384]
        self.ones_b = self.cst_bf[:, 384:512]; self.Rrot = self.cst_bf[:, 512:544]
        self.Umat = self.cst_f[:, 0:128]; self.ones_f = self.cst_f[:, 128:256]
        self.bcg = sb("bcg", [128, 512], F32); self.rden = self.bcg
        self.rd = self.bcg
        self.qm = [sb(f"qm{i}", [128, TS], BF16) for i in range(4)]
        self.lf = sb("lf", [128, 4, NH], F32); self.bfb = sb("bfb", [128, NH], F32)
        self.c_all = sb("c_all", [128, NKT, NH], F32); self.bias_all = sb("bias_all", [128, NKT, NH], F32)
        self.Sprev = sb("Sprev", [128, NH], F32); self.cref = sb("cref", [128, NH], F32)
        self.lfs = sb("lfs", [128, P // 128, NH], F32); self.lfn0 = sb("lfn0", [DS, NH], F32)
        self.bias_new = sb("bias_new", [DS, NH], F32)
        self.vnew = sb("vnew", [128, NH, 65], BF16); self.vnew0 = sb("vnew0", [DS, NH, 65], BF16)
        self.kc_bf = sb("kc_bf", [128, 8, 128], BF16)
        self.MKT = sb("MKT", [128, 8, NMEM], BF16); self.MV = sb("MV", [128, 2, D], BF16)
        self.a_sb = sb("a_sb", [128, 4, MLA_DOWN], F32)
        self.cq_bf = sb("cq_bf", [128, 4, MLA_QL], BF16); self.cqT = sb("cqT", [128, 3, TS], BF16)
        self.ckv_bf = sb("ckv_bf", [128, 4, MLA_KVL], BF16); self.ckvT = sb("ckvT", [128, 2, TS], BF16)
        self.kr_bf = sb("kr_bf", [128, 4, MLA_R], BF16); self.krT = sb("krT", [128, 128], BF16)
        self.gq = sb("gq", [128, MLA_QL], F32); self.gkv = sb("gkv", [128, MLA_KVL], F32)
        self.rtm = sb("rtm", [128, 4, 32], F32); self.rfm = sb("rfm", [128, 2, TS], F32)
        self.xr = sb("xr", [128, TS], BF16)
        self.t1 = sb("t1", [128, TS], F32); self.t2 = sb("t2", [128, TS], F32)
        self.krc = sb("krc", [128, P // 128, MLA_R], BF16)
        self.psA = [ps(f"psA{i}", [128, 512], F32) for i in range(2)]
        self.psS = [ps(f"psS{i}", [128, 512], F32) for i in range(2)]
        self.psO = [ps(f"psO{i}", [128, 512], F32) for i in range(2)]
        self.psM = ps("psM", [128, 512], F32)
        self.psT = ps("psT", [128, 1024], BF16)

        o = self.dram
        self.dma("sp", self.cst_bf[:, :], o["cst_bf"].ap(), writes=["cst"])
        self.dma("sp", self.cst_f[:, :], o["cst_f"].ap(), writes=["cst"])
        for i in range(3):
            self.memset(self.VC[i][:, :, :, 64:65], 1.0, [("VC", i)])
        self.memset(self.vt_bf[:, :, :, 64:65], 1.0, [("vt_bf", s) for s in range(4)])
        self.memset(self.vnew[:, :, 64:65], 1.0, ["vnew"])
        self.memset(self.one_col[:, :], 1.0, ["one_col"])
        for i in range(4):
            self.memset(self.qm[i][:, :], 0.0, [("qm", i)])
        self.memset(self.KR[:, :], 0.0, [("KR", i) for i in range(max(T, P) // 512)])
        self.memset(self.krT[:, :], 0.0, ["krT"])
        self.W = {}
        try:
            self.phase("convert0")
            self.convert_layer(0)
            self.phase("memory_kv")
            self.memory_kv()
            tiles = [Tile("s", 0)] + [Tile("p", j) for j in range(T // TS)]
            for l in range(L):
                if l + 1 < L:
                    self.convert_layer(l + 1)
                if l % 2 == 0:
                    self.memset(self.Sprev[:, :], 0.0, ["Sprev"])
                for tl in tiles:
                    if tl.kind == "p" and tl.j == 0 and l % 2 == 0:
                        self.memset(self.Sprev[:, :], 0.0, ["Sprev"])
                    self.layer_tile(l, tl)
        except StopBuild:
            pass
        self.S.emit(self.nc)
        self.es.close()
        return self.nc


def _constants(cfg):
    bf = ml_dtypes.bfloat16
    cb = np.zeros((128, 640), np.float32)
    k = np.arange(128)[:, None]
    q = np.arange(128)[None, :]
    cb[:, 0:128] = np.eye(128)
    cb[:, 128:256] = np.where(k > q, NEG, 0.0)
    cb[:, 256:384] = np.where((k // 64) > (q // 64), NEG, 0.0)
    cb[:, 384:512] = 1.0
    R = np.zeros((32, 32), np.float32)
    for m in range(16):
        R[m + 16, m] = -1.0
        R[m, m + 16] = 1.0
    cb[64:96, 512:544] = R
    cf = np.zeros((128, 256), np.float32)
    cf[:, 0:128] = (k <= q)
    cf[:, 128:256] = 1.0
    half = 16
    inv = (np.float32(10000.0) ** (-(np.arange(half, dtype=np.float32) / np.float32(half)))).astype(np.float32)

    def tables(pos):
        ang = pos.astype(np.float32)[:, None] * inv[None, :]
        return np.cos(ang).astype(np.float32), np.sin(ang).astype(np.float32)
    cp, sp_ = tables(np.arange(cfg.T))
    rope_tm_p = np.concatenate([cp, sp_], axis=1)
    rope_fm_p = np.stack([np.concatenate([cp, cp], 1).T, np.concatenate([sp_, sp_], 1).T]).astype(np.float32)
    cs, ss = tables(cfg.P + np.arange(DS))
    rope_tm_s = np.zeros((128, 32), np.float32)
    rope_fm_s = np.zeros((2, 32, 128), np.float32)
    for b in range(NSEQ):
        rope_tm_s[32 * b:32 * b + DS] = np.concatenate([cs, ss], 1)
        rope_fm_s[0, :, 32 * b:32 * b + DS] = np.concatenate([cs, cs], 1).T
        rope_fm_s[1, :, 32 * b:32 * b + DS] = np.concatenate([ss, ss], 1).T
    return dict(cst_bf=cb.astype(bf), cst_f=cf, rope_tm_p=np.ascontiguousarray(rope_tm_p),
                rope_tm_s=rope_tm_s, rope_fm_p=np.ascontiguousarray(rope_fm_p), rope_fm_s=rope_fm_s)


def run(cfg, inputs):
    nc = Builder(cfg).build()
    T, P, L, NF, NM = cfg.T, cfg.P, cfg.L, cfg.NF, cfg.NM
    consts = _constants(cfg)
    c32 = lambda a: np.ascontiguousarray(a, dtype=np.float32)
    shared = {}
    for k in ("g_mix", "g_cross", "g_mem", "g_ffn", "g_final", "b_fox_f", "g_mla_q", "g_mla_kv"):
        shared[k] = c32(inputs[k]).reshape(-1)
    for k in ("w_fox_in", "w_fox_out", "w_mla_a", "w_mla_qb", "w_mla_kvb", "w_mla_out", "w_x_q", "w_x_kv", "w_x_o",
              "w_ffn_gu", "w_ffn_down"):
        shared[k] = c32(inputs[k])
    shared.update(consts)
    in_maps = []
    for c in range(cfg.ncores):
        m = dict(shared)
        m["x_prompt"] = c32(inputs["x_prompt"][c])
        m["x_sample"] = c32(inputs["x_sample"][NSEQ * c:NSEQ * (c + 1)]).reshape(NSEQ * DS, D)
        m["mem_prompt"] = c32(inputs["mem_prompt"][c])
        sl = slice(NSEQ * c, NSEQ * (c + 1))
        m["cache_fox_k"] = c32(inputs["cache_fox_k"][:, sl]).reshape(NF, NSEQ, P, D)
        m["cache_fox_v"] = c32(inputs["cache_fox_v"][:, sl]).reshape(NF, NSEQ, P, D)
        m["cache_fox_logf"] = c32(inputs["cache_fox_logf"][:, sl])
        m["cache_mla_ckv"] = c32(inputs["cache_mla_ckv"][:, sl])
        m["cache_mla_krope"] = c32(inputs["cache_mla_krope"][:, sl])
        m["cache_mem_k"] = c32(inputs["cache_mem_k"][:, sl]).reshape(L, NSEQ, NMEM, D)
        m["cache_mem_v"] = c32(inputs["cache_mem_v"][:, sl]).reshape(L, NSEQ, NMEM, D)
        in_maps.append(m)
    res = run_bass_kernel_spmd(nc, in_maps, core_ids=list(range(cfg.ncores)))
    R = res.results
    B = cfg.ncores

    def gat(name, shape_per_core, axis):
        return np.stack([np.asarray(R[c][name], dtype=np.float32).reshape(shape_per_core) for c in range(B)], axis=axis)

    y_p = gat("y_p", (T, D), 0)
    y_s = gat("y_s", (NSEQ, DS, D), 0).reshape(B * NSEQ, DS, D)
    fk_p = gat("fk_p", (NF, T, NH, HD), 1)
    fv_p = gat("fv_p", (NF, T, NH, HD), 1)
    fl_p = gat("fl_p", (NF, T, NH), 1)
    mc_p = gat("mc_p", (NM, T, MLA_KVL), 1)
    mr_p = gat("mr_p", (NM, T, MLA_R), 1)
    mk_p = gat("memk_p", (L, NMEM, 4, 256), 1)
    mv_p = gat("memv_p", (L, NMEM, 4, 256), 1)
    fk_s = gat("fk_s", (NF, NSEQ, DS, NH, HD), 1).reshape(NF, B * NSEQ, DS, NH, HD)
    fv_s = gat("fv_s", (NF, NSEQ, DS, NH, HD), 1).reshape(NF, B * NSEQ, DS, NH, HD)
    fl_s = gat("fl_s", (NF, NSEQ, DS, NH), 1).reshape(NF, B * NSEQ, DS, NH)
    mc_s = gat("mc_s", (NM, NSEQ, DS, MLA_KVL), 1).reshape(NM, B * NSEQ, DS, MLA_KVL)
    mr_s = gat("mr_s", (NM, NSEQ, DS, MLA_R), 1).reshape(NM, B * NSEQ, DS, MLA_R)
    return (y_p, y_s, fk_p, fv_p, fl_p, mc_p, mr_p, mk_p, mv_p, fk_s, fv_s, fl_s, mc_s, mr_s)


def kernel(**inputs):
    cfg = Cfg(T=4096, P=2048, L=4, ncores=8)
    return run(cfg, inputs)
```

```python
import numpy as np
import ml_dtypes
from contextlib import ExitStack
import concourse.bass as bass
import concourse.mybir as mybir
from concourse.bass_utils import run_bass_kernel_spmd

F32 = mybir.dt.float32
BF16 = mybir.dt.bfloat16
AF = mybir.ActivationFunctionType
ALU = mybir.AluOpType

ENGINES = ("pe", "act", "dve", "pool", "sp")
N_DMA_SEMS = 20


class Op:
    __slots__ = ("eng", "fn", "deps", "needs_inc", "count", "idx", "dma", "dsem", "dval")

    def __init__(self, eng, fn):
        self.eng = eng
        self.fn = fn
        self.deps = []
        self.needs_inc = False
        self.count = None
        self.idx = None
        self.dma = False
        self.dsem = None
        self.dval = None


class Sched:
    def __init__(self):
        self.streams = {e: [] for e in ENGINES}
        self.state = {}
        self.seen = {e: {} for e in ENGINES}
        self.dma_rr = {e: 0 for e in ENGINES}
        self.dma_last = {e: [None] * N_DMA_SEMS for e in ENGINES}
        self.dma_cnt = {e: [0] * N_DMA_SEMS for e in ENGINES}

    def _add_dep(self, op, d):
        if d is None or d is op:
            return
        if d.dma:
            key = ("d", id(d))
            if key in self.seen[op.eng]:
                return
            self.seen[op.eng][key] = True
            op.deps.append(d)
            return
        if d.eng == op.eng and op.eng == "pe" and not op.dma:
            return
        prev = self.seen[op.eng].get(d.eng, -1)
        if d.idx <= prev:
            return
        self.seen[op.eng][d.eng] = d.idx
        d.needs_inc = True
        op.deps.append(d)

    @staticmethod
    def _is_psum(k):
        return k == "psM" or (isinstance(k, tuple) and k[0] in ("psA", "psS", "psO", "psT"))

    def op(self, eng, fn, reads=(), writes=(), dma=False):
        excl = [k for k in reads if self._is_psum(k)]
        if excl:
            reads = [k for k in reads if not self._is_psum(k)]
            writes = list(writes) + [k for k in excl if k not in writes]
        o = Op(eng, fn)
        o.dma = dma
        o.idx = len(self.streams[eng])
        if dma:
            s = self.dma_rr[eng]
            self.dma_rr[eng] = (s + 1) % N_DMA_SEMS
            prev = self.dma_last[eng][s]
            if prev is not None:
                key = ("d", id(prev))
                if key not in self.seen[eng]:
                    self.seen[eng][key] = True
                    o.deps.append(prev)
            self.dma_cnt[eng][s] += 1
            o.dsem = s
            o.dval = 16 * self.dma_cnt[eng][s]
            self.dma_last[eng][s] = o
        for k in reads:
            st = self.state.get(k)
            if st is not None:
                self._add_dep(o, st[0])
        for k in writes:
            st = self.state.get(k)
            if st is not None:
                self._add_dep(o, st[0])
                for r in st[1]:
                    self._add_dep(o, r)
        for k in reads:
            st = self.state.setdefault(k, [None, []])
            st[1].append(o)
        for k in writes:
            self.state[k] = [o, []]
        self.streams[eng].append(o)
        return o

    def emit(self, nc):
        with ExitStack() as es:
            prog = {e: es.enter_context(nc.semaphore(f"prog_{e}")) for e in ENGINES}
            dsem = {e: [es.enter_context(nc.semaphore(f"dma_{e}_{i}")) for i in range(N_DMA_SEMS)]
                    for e in ("sp", "act", "pool")}
            for e in ENGINES:
                c = 0
                for o in self.streams[e]:
                    if o.needs_inc and not o.dma:
                        c += 1
                        o.count = c
            block = es.enter_context(nc.Block())

            def run(ename, eng):
                for o in self.streams[ename]:
                    for d in o.deps:
                        if d.dma:
                            eng.wait_ge(dsem[d.eng][d.dsem], d.dval)
                        else:
                            eng.wait_ge(prog[d.eng], d.count)
                    ins = o.fn(eng)
                    if o.dma:
                        ins.then_inc(dsem[ename][o.dsem], 16)
                    elif o.needs_inc:
                        ins.then_inc(prog[ename], 1)
                if ename in dsem:
                    for s in range(N_DMA_SEMS):
                        last = self.dma_last[ename][s]
                        if last is not None:
                            eng.wait_ge(dsem[ename][s], last.dval)

            @block.sync
            def _(eng):
                run("sp", eng)

            @block.scalar
            def _(eng):
                run("act", eng)

            @block.vector
            def _(eng):
                run("dve", eng)

            @block.gpsimd
            def _(eng):
                run("pool", eng)

            @block.tensor
            def _(eng):
                run("pe", eng)


D = 1024
NH = 16
HD = 64
FOX_IN = 4112
MLA_QL, MLA_KVL, MLA_R = 384, 256, 32
MLA_DOWN = 672
MLA_SCALE = float(96 ** -0.5)
FOX_SCALE = 0.125
X_SCALE = 1.0 / 16.0
DFF = 2816
NMEM = 256
EPS = 1e-6
NEG = -30000.0
TS = 512
DS = 16
NSEQ = 4


class Cfg:
    def __init__(self, T=4096, P=2048, L=4, ncores=8):
        self.T, self.P, self.L, self.ncores = T, P, L, ncores
        self.skip = set()
        self.stop = None
        self.NF = (L + 1) // 2
        self.NM = L // 2
        assert T % TS == 0 and P % 1024 == 0


class StopBuild(Exception):
    pass


class Tile:
    def __init__(self, kind, j):
        self.kind = kind
        self.j = j
        if kind == "p":
            self.nsub, self.NT, self.pos0 = TS // 128, TS, j * TS
        else:
            self.nsub, self.NT, self.pos0 = 1, 128, 0


class Builder:
    def __init__(self, cfg):
        self.cfg = cfg
        self.nc = bass.Bass("TRN2", target_bir_lowering=False)
        self.S = Sched()
        self.es = ExitStack()
        self.dram = {}
        self.rrA = 0
        self.rrE = 0
        self.rrW = 0
        self.rrStage = 0
        self.rrPT = 0
        self.rrS = 0
        self.rrO = 0
        self.rrKT = 0
        self.rrT = 0
        self.rrAcc = 0
        self.rrKV = 0
        self.kv_w_loaded = {}
        self.pend = []
        self.deferred = []
        self.rrS3 = 0
        self.rrQM = 0
        self.cur_qm = None
        self.cur_qrm = None

    def kq(self, c):
        return ("R", c)

    def kk(self, c):
        return ("R", 8 + c)

    def ko(self, c):
        return ("R", 16 + c)

    def ka(self, f):
        return ("R", f)

    def din(self, name, shape, dt=F32):
        t = self.nc.dram_tensor(name, list(shape), dt, kind="ExternalInput")
        self.dram[name] = t
        return t

    def dout(self, name, shape, dt=F32):
        t = self.nc.dram_tensor(name, list(shape), dt, kind="ExternalOutput")
        self.dram[name] = t
        return t

    def dscr(self, name, shape, dt):
        t = self.nc.dram_tensor(name, list(shape), dt)
        self.dram[name] = t
        return t

    def sb(self, name, shape, dt):
        return self.es.enter_context(self.nc.sbuf_tensor(name, list(shape), dt))

    def ps(self, name, shape, dt):
        return self.es.enter_context(self.nc.psum_tensor(name, list(shape), dt))

    def dma(self, q, out, in_, reads=(), writes=()):
        return self.S.op(q, lambda e: e.dma_start(out=out, in_=in_), reads, writes, dma=True)

    def mm(self, out, lhsT, rhs, start, stop, reads, writes):
        return self.S.op("pe", lambda e: e.matmul(out, lhsT=lhsT, rhs=rhs, start=start, stop=stop),
                         reads, writes)

    def tr(self, out, in_, reads, writes):
        idn = self.ident[0:in_.shape[0], 0:in_.shape[0]]
        return self.S.op("pe", lambda e: e.transpose(out, in_, idn), list(reads) + ["cst"], writes)

    def act(self, out, in_, func, reads, writes, bias=None, scale=None, accum_out=None):
        kw = {}
        if bias is not None:
            kw["bias"] = bias
        if scale is not None:
            kw["scale"] = scale
        if accum_out is not None:
            kw["accum_out"] = accum_out
        return self.S.op("act", lambda e: e.activation(out=out, in_=in_, func=func, **kw), reads, writes)

    def tt(self, out, in0, in1, op, reads, writes, eng="dve"):
        return self.S.op(eng, lambda e: e.tensor_tensor(out=out, in0=in0, in1=in1, op=op), reads, writes)

    def ts(self, out, in0, s1, s2, op0, op1, reads, writes, eng="dve"):
        if op1 is None:
            return self.S.op(eng, lambda e: e.tensor_scalar(out=out, in0=in0, scalar1=s1, scalar2=None, op0=op0),
                             reads, writes)
        return self.S.op(eng, lambda e: e.tensor_scalar(out=out, in0=in0, scalar1=s1, scalar2=s2, op0=op0, op1=op1),
                         reads, writes)

    def stt(self, out, in0, scalar, in1, op0, op1, reads, writes):
        return self.S.op("dve", lambda e: e.scalar_tensor_tensor(out=out, in0=in0, scalar=scalar, in1=in1,
                                                                  op0=op0, op1=op1), reads, writes)

    def cp(self, out, in_, reads, writes, eng="dve"):
        if eng == "act":
            return self.S.op("act", lambda e: e.copy(out=out, in_=in_), reads, writes)
        return self.S.op(eng, lambda e: e.tensor_copy(out=out, in_=in_), reads, writes)

    def evac(self, out, in_, reads, writes):
        self.rrE ^= 1
        return self.cp(out, in_, reads, writes, eng=("act" if self.rrE else "dve"))

    def recip(self, out, in_, reads, writes):
        return self.S.op("dve", lambda e: e.reciprocal(out=out, in_=in_), reads, writes)

    def memset(self, ap, val, writes, eng="pool"):
        return self.S.op(eng, lambda e: e.memset(ap, val), (), writes)

    def nextA(self):
        self.rrA ^= 1
        return self.psA[self.rrA], ("psA", self.rrA)

    def nextS(self):
        self.rrS ^= 1
        return self.psS[self.rrS], ("psS", self.rrS)

    def nextO(self):
        self.rrO ^= 1
        return self.psO[self.rrO], ("psO", self.rrO)

    def nextPT(self):
        self.rrPT = (self.rrPT + 1) % len(self.PT)
        return self.PT[self.rrPT], ("PT", self.rrPT)

    def nextT(self):
        self.rrT ^= 1
        if self.rrT:
            return self.psT[:, 0:512], ("psT", 0)
        return self.psM16[:, 0:512], "psM"

    def nextStage(self):
        self.rrStage = (self.rrStage + 1) % len(self.stage)
        return self.stage[self.rrStage], ("stage", self.rrStage)

    def bcast_row(self, tname, row_off, n):
        return bass.AP(self.dram[tname], row_off, [[0, 128], [1, n]])

    def conv_weight(self, name, src_name, src_l, K, c0, c1, cb, src_cols):
        KC = K // 128
        M = c1 - c0
        nblk = (M + cb - 1) // cb
        scr = self.dscr(name, [nblk, 128, KC * cb], BF16)
        src = self.dram[src_name].ap()[src_l]
        for b in range(nblk):
            w = min(cb, M - b * cb)
            o = scr.ap()[b].rearrange("p (k c) -> p k c", k=KC)[:, :, 0:w]
            i = src[:, c0 + b * cb: c0 + b * cb + w].rearrange("(k p) c -> p k c", p=128)
            self.dma("pool", o, i, reads=(), writes=[("wb", name, b)])
        return dict(name=name, KC=KC, cb=cb, nblk=nblk, M=M, scr=scr)

    def wplan(self, items):
        self.plan = list(items)
        self.plan_i = 0
        self.plan_loaded = 0
        self.loaded = {}

    def _issue_load(self, i):
        W, b = self.plan[i]
        slot = self.rrW
        self.rrW = (self.rrW + 1) % len(self.wbuf)
        n = W["KC"] * W["cb"]
        w = min(W["cb"], W["M"] - b * W["cb"]) if W["name"][:2] != "wd" else W["cb"]
        kcv = W["KC"]
        if W["name"][:2] == "wd":
            nfb = W["nblk"] // 2
            kcv = min(4, DFF // 128 - (b % nfb) * 4)
        self.dma("sp", self.wbuf[slot][:, 0:n].rearrange("p (k c) -> p k c", k=W["KC"])[:, 0:kcv, 0:w],
                 W["scr"].ap()[b].rearrange("p (k c) -> p k c", k=W["KC"])[:, 0:kcv, 0:w],
                 reads=[("wb", W["name"], b)], writes=[("w", slot)])
        self.loaded[i] = slot

    def wnext(self, W, b):
        i = self.plan_i
        assert self.plan[i][0] is W and self.plan[i][1] == b, (self.plan[i][0]["name"], self.plan[i][1], W["name"], b)
        while self.plan_loaded < min(len(self.plan), i + 2):
            self._issue_load(self.plan_loaded)
            self.plan_loaded += 1
        slot = self.loaded.pop(i)
        self.plan_i += 1
        cb = W["cb"]
        buf = self.wbuf[slot]

        def view(kc, a, b2):
            return buf[:, kc * cb + a: kc * cb + b2]
        return view, ("w", slot)

    def rstd_of(self, src_fn, nsub, n, rkey, reads):
        for s in range(nsub):
            self.act(self.junk[:, 0:n], src_fn(s), AF.Square, reads(s) if callable(reads) else reads,
                     ["junk", (rkey, "ss", s)], accum_out=self.ss[:, s:s + 1])
        self.ts(self.ss2[:, 0:nsub], self.ss[:, 0:nsub], 1.0 / n, EPS, ALU.mult, ALU.add,
                [(rkey, "ss", s) for s in range(nsub)], ["ss2"])
        self.S.op("act", lambda e: e.sqrt(out=self.ss2[:, 0:nsub], in_=self.ss2[:, 0:nsub]), ["ss2"], ["ss2"])
        self.recip(self.rstd[:, 0:nsub], self.ss2[:, 0:nsub], ["ss2"], ["rstd"])

    def norm_hT(self, tl, gname, l):
        gk = ("gbc", gname)
        self.dma("sp", self.gbc[:, :], self.bcast_row(gname, l * D, D), writes=[gk])
        self.rstd_of(lambda s: self.x_sb[:, s, :], tl.nsub, D, "x", lambda s: [("x", s)])
        for s in range(tl.nsub):
            self.stt(self.h_bf[:, s, :], self.x_sb[:, s, :], self.rstd[:, s:s + 1], self.gbc[:, :],
                     ALU.mult, ALU.mult, [("x", s), "rstd", gk], [("h_bf", s)])
        self.transpose_to(self.h_bf, [("h_bf", s) for s in range(tl.nsub)], self.hT, "hT", 8, tl)

    def transpose_to(self, src, src_keys, dst, dkey, nchunks, tl, csz=128, dst_off=0, keyfn=None):
        for c in range(nchunks):
            pT, pk = self.nextT()
            for s in range(tl.nsub):
                self.tr(pT[0:csz, s * 128:(s + 1) * 128], src[:, s, c * csz:(c + 1) * csz], src_keys, [pk])
            self.evac(dst[0:csz, c, dst_off:dst_off + tl.NT], pT[0:csz, 0:tl.NT], [pk],
                      [keyfn(c) if keyfn else (dkey, c)])

    def rows_dma_in(self, tl, dst_fn, src_t, ncols, row_base=0, writes=()):
        if tl.kind == "p":
            for s in range(tl.nsub):
                r0 = row_base + tl.pos0 + s * 128
                self.dma("sp", dst_fn(s), src_t[r0:r0 + 128, :], writes=writes)
        else:
            for b in range(NSEQ):
                self.dma("sp", dst_fn(0)[32 * b:32 * b + DS, :], src_t[row_base + b * DS: row_base + (b + 1) * DS, :],
                         writes=writes)

    def rows_dma_out(self, tl, s, src_ap, dst_t, c0, c1, reads, row_base=0):
        if tl.kind == "p":
            r0 = row_base + tl.pos0 + s * 128
            self.dma("sp", dst_t[r0:r0 + 128, c0:c1], src_ap, reads=reads)
        else:
            for b in range(NSEQ):
                self.dma("sp", dst_t[row_base + b * DS: row_base + (b + 1) * DS, c0:c1],
                         src_ap[32 * b:32 * b + DS, :], reads=reads)

    def proj_fm(self, W, nchunks, tl, src, skey, KC, sink):
        per_blk = W["cb"] // 128
        for c in range(nchunks):
            blk, cc = divmod(c, per_blk)
            if cc == 0:
                wv, wk = self.wnext(W, blk)
            ps, pk = self.nextA()
            for kc in range(KC):
                self.mm(ps[:, 0:tl.NT], wv(kc, cc * 128, cc * 128 + 128), src[:, kc, 0:tl.NT],
                        kc == 0, kc == KC - 1, [wk] + [(skey, kc)], [pk])
            sink(c, ps, pk)

    def proj_tm(self, W, tl, src, skey, KC, sink, keyfn=None):
        for blk in range(W["nblk"]):
            wv, wk = self.wnext(W, blk)
            w = min(W["cb"], W["M"] - blk * W["cb"])
            for s in range(tl.nsub):
                ps, pk = self.nextA()
                for kc in range(KC):
                    self.mm(ps[:, 0:w], src[:, kc, s * 128:(s + 1) * 128], wv(kc, 0, w),
                            kc == 0, kc == KC - 1, [wk, keyfn(kc) if keyfn else (skey, kc)], [pk])
                sink(blk, s, ps, pk, w)

    def out_proj_residual(self, W, tl, src, skey, KC, keyfn=None):
        def sink(blk, s, ps, pk, w):
            self.tt(self.x_sb[:, s, blk * 512: blk * 512 + w], ps[:, 0:w], self.x_sb[:, s, blk * 512: blk * 512 + w],
                    ALU.add, [pk, ("x", s)], [("x", s)])
        self.proj_tm(W, tl, src, skey, KC, sink, keyfn=keyfn)

    def next_acc_pair(self, allow_psA):
        if allow_psA:
            self.rrAcc ^= 1
        else:
            self.rrAcc = 0
        if self.rrAcc == 0:
            return [(self.psO[0], ("psO", 0)), (self.psO[1], ("psO", 1))]
        return [(self.psA[0], ("psA", 0)), (self.psA[1], ("psA", 1))]

    def next_kv(self):
        self.rrKV = (self.rrKV + 1) % len(self.KC)
        return self.rrKV

    def normalize_head(self, psO, ok, r, c, q0, q1, n, gate):
        self.recip(self.rd[64:65, 0:n], psO[64:65, 0:n], [ok], ["rd"])
        self.mm(self.psM[0:64, 0:n], self.ones_f[64:65, 0:64], self.rd[64:65, 0:n], True, True, ["rd", "cst"], ["psM"])
        if gate:
            self.tt(self.bcg[0:64, 0:n], self.psM[0:64, 0:n], self.gT[r * 64:(r + 1) * 64, c, q0:q1], ALU.mult,
                    ["psM", ("gT", c)], ["bcg"])
        else:
            self.cp(self.bcg[0:64, 0:n], self.psM[0:64, 0:n], ["psM"], ["bcg"], eng="act")
        self.tt(self.oT[r * 64:(r + 1) * 64, c, q0:q1], psO[0:64, 0:n], self.bcg[0:64, 0:n], ALU.mult,
                [ok, "bcg"], [self.ko(c)])

    PIPE_DEPTH = 2

    def pipe_push(self, pv):
        self.pend.append(pv)
        while len(self.pend) > self.PIPE_DEPTH:
            self._pipe_emit()

    def _pipe_emit(self):
        pv = self.pend.pop(0)
        pv()
        ready = [fn for tok, fn in self.deferred if tok is pv]
        self.deferred = [(tok, fn) for tok, fn in self.deferred if tok is not pv]
        for fn in ready:
            fn()

    def pipe_defer(self, fn):
        if self.pend:
            self.deferred.append((self.pend[-1], fn))
        else:
            fn()

    def pipe_flush(self):
        while self.pend:
            self._pipe_emit()
        assert not self.deferred

    def nextS3(self):
        self.rrS3 = (self.rrS3 + 1) % 3
        if self.rrS3 == 2:
            return self.psT32, ("psT", 0)
        return self.psS[self.rrS3], ("psS", self.rrS3)

    def prep_qmask(self, c, q0, q1, mla):
        if mla:
            base = 0
        else:
            self.rrQM ^= 1
            base = 2 * self.rrQM
        self.cur_qm, self.cur_qrm = [], []
        for r in range(2):
            buf, key = self.qm[base + r], ("qm", base + r)
            self.cp(buf[r * 64:(r + 1) * 64, q0:q1], self.qT[r * 64:(r + 1) * 64, c, q0:q1], [self.kq(c)], [key], eng="pool")
            self.cur_qm.append((buf, key))
            if mla:
                rbuf, rkey = self.qm[2 + r], ("qm", 2 + r)
                self.cp(rbuf[r * 64:r * 64 + 32, q0:q1], self.qR[r * 64:r * 64 + 32, c, q0:q1], [("gT", c)], [rkey], eng="pool")
                self.cur_qrm.append((rbuf, rkey))

    def score_pv(self, c, r, kt, KCs, kcol, kkey, VCs, vt, vkey, psO, ok, q0, qn, n0, mla, mask, first, last):
        h = 2 * c + r
        scale = MLA_SCALE if mla else FOX_SCALE
        psS, sk = self.nextS3()
        diag = mask is not None
        qm, qmk = self.cur_qm[r]
        self.mm(psS[:, n0:qn], KCs[:, kcol:kcol + 128], qm[:, q0 + n0:q0 + qn],
                True, (not mla) and (not diag), [kkey, qmk], [sk])
        if mla:
            qrm, qrk = self.cur_qrm[r]
            self.mm(psS[:, n0:qn], self.KR[:, kt * 128:(kt + 1) * 128],
                    qrm[:, q0 + n0:q0 + qn], False, not diag, [("KR", kt // 4), qrk], [sk])
        if diag:
            self.mm(psS[:, n0:n0 + 128], self.ident[:, :], mask, False, True, ["cst"], [sk])
        PT, ptk = self.nextPT()
        if mla:
            self.act(PT[:, n0:qn], psS[:, n0:qn], AF.Exp, [sk], [ptk], scale=scale)
        else:
            self.act(PT[:, n0:qn], psS[:, n0:qn], AF.Exp, [sk, ("bias", kt)], [ptk],
                     bias=self.bias_all[:, kt, h:h + 1], scale=scale)
        self.pipe_push(lambda: self.mm(psO[0:65, n0:qn], VCs[:, vt, r, 0:65], PT[:, n0:qn], first, last,
                                       [vkey, ptk], [ok]))

    def attn_prompt_pair(self, tl, c, mla):
        NT = tl.NT
        nkt = (tl.j + 1) * (TS // 128)
        accs = self.next_acc_pair(True)
        self.prep_qmask(c, 0, NT, mla)
        mask = self.maskM[:, :] if mla else self.maskF[:, :]
        for g in range((nkt + 7) // 8):
            nt = min(8, nkt - g * 8)
            slot = self.next_kv()
            kkey, vkey = ("KC", slot), ("VC", slot)
            jj = [("KTd", c, j2) for j2 in range(g * 2, min(g * 2 + 2, tl.j + 1))]
            vv = [("Vd", c, j2) for j2 in range(g * 2, min(g * 2 + 2, tl.j + 1))]
            self.dma("sp", self.KC[slot][:, 0:nt * 128], self.KTd.ap()[c][:, g * 1024:g * 1024 + nt * 128],
                     reads=jj, writes=[kkey])
            self.dma("sp", self.VC[slot][:, 0:nt, :, :].rearrange("p t h e -> p t (h e)"),
                     self.Vd.ap()[c][:, g * 8:g * 8 + nt, :], reads=vv, writes=[vkey])
            for t in range(nt):
                for r in range(2):
                    psO, ok = accs[r]
                    kt = g * 8 + t
                    d = kt - tl.j * (TS // 128)
                    n0 = 0 if d < 0 else d * 128
                    self.score_pv(c, r, kt, self.KC[slot], t * 128, kkey, self.VC[slot], t, vkey, psO, ok,
                                  0, NT, n0, mla, mask if d >= 0 else None, kt == 0, kt == nkt - 1)

        def norm():
            for r in range(2):
                psO, ok = accs[r]
                self.normalize_head(psO, ok, r, c, 0, NT, NT, gate=not mla)
        self.pipe_defer(norm)

    def attn_sample_pair(self, l, b, c, mla):
        cfg = self.cfg
        lj = l // 2
        o = self.dram
        q0 = 32 * b
        npast = cfg.P // 128
        accs = self.next_acc_pair(not mla)
        self.prep_qmask(c, q0, q0 + DS, mla)
        for g in range(cfg.P // 1024):
            slot = self.next_kv()
            kkey, vkey = ("KC", slot), ("VC", slot)
            if not mla:
                src = o["cache_fox_k"].ap()[lj, b, g * 1024:(g + 1) * 1024, c * 128:(c + 1) * 128].rearrange("(t p) d -> p t d", p=128)
                self.dma("pool", self.kc_bf[:, :, :], src, writes=["kc_bf"])
                for g4 in range(2):
                    pT, pk = self.nextT()
                    for i in range(4):
                        self.tr(pT[:, i * 128:(i + 1) * 128], self.kc_bf[:, g4 * 4 + i, :], ["kc_bf"], [pk])
                    self.evac(self.KC[slot][:, g4 * 512:(g4 + 1) * 512], pT[:, 0:512], [pk], [kkey])
                for r2 in range(2):
                    srcv = o["cache_fox_v"].ap()[lj, b, g * 1024:(g + 1) * 1024, c * 128 + r2 * 64: c * 128 + (r2 + 1) * 64].rearrange(
                        "(t p) d -> p t d", p=128)
                    self.dma("pool", self.VC[slot][:, :, r2, 0:64], srcv, writes=[vkey])
            else:
                for half in range(2):
                    ps, pk = self.nextA()
                    for kc in range(2):
                        self.mm(ps[:, 0:512], self.wkvk[:, kc, c * 128:(c + 1) * 128],
                                self.ckvTs[:, kc, g * 1024 + half * 512: g * 1024 + (half + 1) * 512],
                                kc == 0, kc == 1, ["wkvk", ("hT", kc)], [pk])
                    self.evac(self.KC[slot][:, half * 512:(half + 1) * 512], ps[:, 0:512], [pk], [kkey])
                for t4 in range(2):
                    ps, pk = self.nextA()
                    for i in range(4):
                        t = t4 * 4 + i
                        for kc in range(2):
                            self.mm(ps[:, i * 128:(i + 1) * 128], self.ckvTs[:, kc, (g * 8 + t) * 128:(g * 8 + t + 1) * 128],
                                    self.wkvv[:, kc, c * 128:(c + 1) * 128], kc == 0, kc == 1, ["wkvv", ("hT", kc)], [pk])
                    self.evac(self.VC[slot][:, t4 * 4:(t4 + 1) * 4, :, 0:64],
                              ps[:, 0:512].rearrange("p (t h d) -> p t h d", t=4, d=64), [pk], [vkey])
            for t in range(8):
                for r in range(2):
                    psO, ok = accs[r]
                    kt = g * 8 + t
                    self.score_pv(c, r, kt, self.KC[slot], t * 128, kkey, self.VC[slot], t, vkey, psO, ok,
                                  q0, DS, 0, mla, None, kt == 0, False)
        for r in range(2):
            h = 2 * c + r
            psO, ok = accs[r]
            scale = MLA_SCALE if mla else FOX_SCALE
            psS, sk = self.nextS3()
            PT, ptk = self.nextPT()
            qm, qmk = self.cur_qm[r]
            self.mm(psS[0:DS, 0:DS], self.kT[:, c, q0:q0 + DS], qm[:, q0:q0 + DS],
                    True, False, [self.kk(c), qmk], [sk])
            if mla:
                qrm, qrk = self.cur_qrm[r]
                self.mm(psS[0:DS, 0:DS], self.krT[:, q0:q0 + DS], qrm[:, q0:q0 + DS],
                        False, True, ["krT", qrk], [sk])
                self.act(PT[0:DS, 0:DS], psS[0:DS, 0:DS], AF.Exp, [sk], [ptk], scale=scale)
            else:
                self.mm(psS[0:DS, 0:DS], self.ident[0:DS, 0:DS], self.maskF[0:DS, 0:DS], False, True, ["cst"], [sk])
                self.act(PT[0:DS, 0:DS], psS[0:DS, 0:DS], AF.Exp, [sk, "bias_new"], [ptk],
                         bias=self.bias_new[0:DS, h:h + 1], scale=scale)
            self.pipe_push(lambda psO=psO, ok=ok, h=h, PT=PT, ptk=ptk: self.mm(
                psO[0:65, 0:DS], self.vnew0[0:DS, h, 0:65], PT[0:DS, 0:DS], False, True, ["vnew0", ptk], [ok]))

        def norm():
            for r in range(2):
                psO, ok = accs[r]
                self.normalize_head(psO, ok, r, c, q0, q0 + DS, DS, gate=not mla)
        self.pipe_defer(norm)
        if mla:
            self.pipe_flush()

    def cumsum_tile(self, lf_ap, lf_key, dst_ap, dst_key):
        self.mm(self.psM[:, 0:16], self.Umat[:, :], lf_ap, True, False, [lf_key, "cst"], ["psM"])
        self.mm(self.psM[:, 0:16], self.ones_f[:, :], self.Sprev[:, :], False, True, ["Sprev", "cst"], ["psM"])
        self.cp(dst_ap, self.psM[:, 0:16], ["psM"], [dst_key])
        self.tt(self.Sprev[:, :], self.Sprev[:, :], lf_ap, ALU.add, ["Sprev", lf_key], ["Sprev"])

    def carry_bcast(self, dst_ap, dst_key):
        self.mm(self.psM[:, 0:16], self.ones_f[:, :], self.Sprev[:, :], True, True, ["Sprev", "cst"], ["psM"])
        self.cp(dst_ap, self.psM[:, 0:16], ["psM"], [dst_key])

    def store_v_tile(self, tl):
        for c in range(8):
            self.dma("sp", self.Vd.ap()[c][:, tl.j * 4:(tl.j + 1) * 4, :],
                     self.vt_bf[:, :, 2 * c:2 * c + 2, :].rearrange("p s h e -> p s (h e)"),
                     reads=[("vt_bf", s) for s in range(4)], writes=[("Vd", c, tl.j)])

    def fox_mixer(self, l, tl):
        cfg = self.cfg
        lj = l // 2
        W = self.W[l]
        NT, nsub = tl.NT, tl.nsub
        o = self.dram
        isp = tl.kind == "p"
        rb = lj * cfg.T if isp else lj * NSEQ * DS
        self.norm_hT(tl, "g_mix", l)
        self.proj_fm(W["q"], 8, tl, self.hT, "hT", 8,
                     lambda c, ps, pk: self.evac(self.qT[:, c, 0:NT], ps[:, 0:NT], [pk], [self.kq(c)]))

        self.phase("fox k")
        def k_sink(blk, s, ps, pk, w):
            st, sk = self.nextStage()
            self.cp(st[:, 0:w], ps[:, 0:w], [pk], [sk], eng="act")
            self.cp(self.h_bf[:, s, blk * 512: blk * 512 + w], st[:, 0:w], [sk], [("h_bf", s)], eng="pool")
            self.rows_dma_out(tl, s, st[:, 0:w], o["fk_p" if isp else "fk_s"].ap(), blk * 512, blk * 512 + w, [sk], row_base=rb)
        self.proj_tm(W["k"], tl, self.hT, "hT", 8, k_sink)
        self.transpose_to(self.h_bf, [("h_bf", s) for s in range(nsub)], self.kT, None, 8, tl, keyfn=self.kk)
        if isp:
            for c in range(8):
                self.dma("sp", self.KTd.ap()[c][:, tl.pos0:tl.pos0 + NT], self.kT[:, c, 0:NT],
                         reads=[self.kk(c)], writes=[("KTd", c, tl.j)])

        self.phase("fox v")
        def v_sink(blk, s, ps, pk, w):
            st, sk = self.nextStage()
            self.cp(st[:, 0:w], ps[:, 0:w], [pk], [sk], eng="act")
            if isp:
                self.cp(self.vt_bf[:, s, blk * 8:(blk + 1) * 8, 0:64], st[:, 0:w].rearrange("p (h d) -> p h d", d=64),
                        [sk], [("vt_bf", s)], eng="pool")
            else:
                self.cp(self.vnew[:, blk * 8:(blk + 1) * 8, 0:64], st[:, 0:w].rearrange("p (h d) -> p h d", d=64),
                        [sk], ["vnew"], eng="pool")
            self.rows_dma_out(tl, s, st[:, 0:w], o["fv_p" if isp else "fv_s"].ap(), blk * 512, blk * 512 + w, [sk], row_base=rb)
        self.proj_tm(W["v"], tl, self.hT, "hT", 8, v_sink)
        if isp:
            self.store_v_tile(tl)

        self.phase("fox f")
        bk = "bfb"
        self.dma("sp", self.bfb[:, :], self.bcast_row("b_fox_f", lj * NH, NH), writes=[bk])
        wv, wk = self.wnext(W["f"], 0)
        for s in range(nsub):
            for kc in range(8):
                self.mm(self.psM[:, 0:16], self.hT[:, kc, s * 128:(s + 1) * 128], wv(kc, 0, 16), kc == 0, kc == 7,
                        [wk, ("hT", kc)], ["psM"])
            self.tt(self.lf[:, s, :], self.psM[:, 0:16], self.bfb[:, :], ALU.add, ["psM", bk], [("lf", s)])
            self.act(self.lf[:, s, :], self.lf[:, s, :], AF.Exp, [("lf", s)], [("lf", s)], scale=-1.0)
            self.act(self.lf[:, s, :], self.lf[:, s, :], AF.Ln, [("lf", s), "one_col"], [("lf", s)], bias=self.one_col[:, 0:1], scale=1.0)
            self.ts(self.lf[:, s, :], self.lf[:, s, :], -1.0, None, ALU.mult, None, [("lf", s)], [("lf", s)])
            self.rows_dma_out(tl, s, self.lf[:, s, :], o["fl_p" if isp else "fl_s"].ap(), 0, NH, [("lf", s)], row_base=rb)

        self.phase("fox g")
        self.proj_fm(W["g"], 8, tl, self.hT, "hT", 8,
                     lambda c, ps, pk: self.act(self.gT[:, c, 0:NT], ps[:, 0:NT], AF.Sigmoid, [pk], [("gT", c)]))

        self.phase("fox attn")
        if isp:
            nkt = (tl.j + 1) * 4
            for s in range(nsub):
                kt = tl.j * 4 + s
                if s == 2:
                    self.carry_bcast(self.cref[:, :], "cref")
                self.cumsum_tile(self.lf[:, s, :], ("lf", s), self.c_all[:, kt, :], ("c_all", kt))
            for kt in range(nkt):
                self.tt(self.bias_all[:, kt, :], self.cref[:, :], self.c_all[:, kt, :], ALU.subtract,
                        ["cref", ("c_all", kt)], [("bias", kt)])
            for c in range(8):
                self.attn_prompt_pair(tl, c, False)
        else:
            npast = cfg.P // 128
            for c in range(8):
                self.memset(self.oT[:, c, 0:128], 0.0, [self.ko(c)])
            for b in range(NSEQ):
                self.dma("sp", self.lfs[:, 0:npast, :],
                         o["cache_fox_logf"].ap()[lj, b].rearrange("(t p) h -> p t h", p=128), writes=["lfs"])
                self.memset(self.Sprev[:, :], 0.0, ["Sprev"])
                for kt in range(npast):
                    self.cumsum_tile(self.lfs[:, kt, :], "lfs", self.c_all[:, kt, :], ("c_all", kt))
                self.carry_bcast(self.cref[:, :], "cref")
                for kt in range(npast):
                    self.tt(self.bias_all[:, kt, :], self.cref[:, :], self.c_all[:, kt, :], ALU.subtract,
                            ["cref", ("c_all", kt)], [("bias", kt)])
                self.phase(f"fox s-attn b{b} newbias")
                self.cp(self.lfn0[0:DS, :], self.lf[32 * b:32 * b + DS, 0, :], [("lf", 0)], ["lfn0"])
                self.mm(self.psM[0:DS, 0:16], self.Umat[0:DS, 0:DS], self.lfn0[0:DS, :], True, True, ["lfn0", "cst"], ["psM"])
                self.ts(self.bias_new[0:DS, :], self.psM[0:DS, 0:16], -1.0, None, ALU.mult, None, ["psM"], ["bias_new"])
                self.cp(self.vnew0[0:DS, :, :], self.vnew[32 * b:32 * b + DS, :, :], ["vnew"], ["vnew0"])
                self.phase(f"fox s-attn b{b} pairs")
                for c in range(8):
                    self.attn_sample_pair(l, b, c, False)
                self.pipe_flush()
        self.pipe_flush()
        self.phase("fox out")
        self.out_proj_residual(W["o"], tl, self.oT, None, 8, keyfn=self.ko)

    def mla_mixer(self, l, tl):
        cfg = self.cfg
        lj = l // 2
        W = self.W[l]
        NT, nsub = tl.NT, tl.nsub
        o = self.dram
        isp = tl.kind == "p"
        rb = lj * cfg.T if isp else lj * NSEQ * DS
        self.norm_hT(tl, "g_mix", l)
        if not self.kv_w_loaded.get(l):
            self.kv_w_loaded[l] = True
            self.dma("sp", self.wkvk[:, :, :].rearrange("p k c -> p (k c)"), W["kvk"]["scr"].ap()[0],
                     reads=[("wb", W["kvk"]["name"], 0)], writes=["wkvk"])
            self.dma("sp", self.wkvv[:, :, :].rearrange("p k c -> p (k c)"), W["kvv"]["scr"].ap()[0],
                     reads=[("wb", W["kvv"]["name"], 0)], writes=["wkvv"])

        def a_sink(blk, s, ps, pk, w):
            self.evac(self.a_sb[:, s, blk * 512: blk * 512 + w], ps[:, 0:w], [pk], [("a_sb", s, blk)])
        self.proj_tm(W["a"], tl, self.hT, "hT", 8, a_sink)
        akeys = [("a_sb", s, blk) for s in range(nsub) for blk in range(2)]
        self.dma("sp", self.gq[:, :], self.bcast_row("g_mla_q", lj * MLA_QL, MLA_QL), writes=["gq"])
        self.dma("sp", self.gkv[:, :], self.bcast_row("g_mla_kv", lj * MLA_KVL, MLA_KVL), writes=["gkv"])
        self.rstd_of(lambda s: self.a_sb[:, s, 0:MLA_QL], nsub, MLA_QL, "cq", akeys)
        for s in range(nsub):
            self.stt(self.cq_bf[:, s, :], self.a_sb[:, s, 0:MLA_QL], self.rstd[:, s:s + 1], self.gq[:, :],
                     ALU.mult, ALU.mult, akeys + ["rstd", "gq"], [("cq_bf", s)])
        self.transpose_to(self.cq_bf, [("cq_bf", s) for s in range(nsub)], self.cqT, "cqT", 3, tl)
        self.rstd_of(lambda s: self.a_sb[:, s, MLA_QL:MLA_QL + MLA_KVL], nsub, MLA_KVL, "ckv", akeys)
        for s in range(nsub):
            st, sk = self.nextStage()
            self.stt(st[:, 0:MLA_KVL], self.a_sb[:, s, MLA_QL:MLA_QL + MLA_KVL], self.rstd[:, s:s + 1], self.gkv[:, :],
                     ALU.mult, ALU.mult, akeys + ["rstd", "gkv"], [sk])
            self.cp(self.ckv_bf[:, s, :], st[:, 0:MLA_KVL], [sk], [("ckv_bf", s)], eng="act")
            self.rows_dma_out(tl, s, st[:, 0:MLA_KVL], o["mc_p" if isp else "mc_s"].ap(), 0, MLA_KVL, [sk], row_base=rb)
        self.transpose_to(self.ckv_bf, [("ckv_bf", s) for s in range(nsub)], self.ckvT, "ckvT", 2, tl)
        if isp:
            self.dma("sp", self.rtm[:, 0:nsub, :],
                     o["rope_tm_p"].ap()[tl.pos0:tl.pos0 + NT, :].rearrange("(s p) c -> p s c", p=128), writes=["rtm"])
        else:
            self.dma("sp", self.rtm[:, 0, :], o["rope_tm_s"].ap(), writes=["rtm"])
        A0 = MLA_QL + MLA_KVL
        for s in range(nsub):
            x1 = self.a_sb[:, s, A0:A0 + 16]
            x2 = self.a_sb[:, s, A0 + 16:A0 + 32]
            cs = self.rtm[:, s, 0:16]
            sn = self.rtm[:, s, 16:32]
            st, sk = self.nextStage()
            rk = akeys + ["rtm"]
            self.tt(st[:, 0:16], x1, cs, ALU.mult, rk, [sk])
            self.tt(st[:, 32:48], x2, sn, ALU.mult, rk, [sk])
            self.tt(st[:, 0:16], st[:, 0:16], st[:, 32:48], ALU.subtract, [sk], [sk])
            self.tt(st[:, 16:32], x1, sn, ALU.mult, rk, [sk])
            self.tt(st[:, 32:48], x2, cs, ALU.mult, rk, [sk])
            self.tt(st[:, 16:32], st[:, 16:32], st[:, 32:48], ALU.add, [sk], [sk])
            self.cp(self.kr_bf[:, s, :], st[:, 0:32], [sk], [("kr_bf", s)], eng="act")
            self.rows_dma_out(tl, s, st[:, 0:32], o["mr_p" if isp else "mr_s"].ap(), 0, MLA_R, [sk], row_base=rb)
        pT, pk = self.nextT()
        for s in range(nsub):
            self.tr(pT[0:32, s * 128:(s + 1) * 128], self.kr_bf[:, s, :], [("kr_bf", s)], [pk])
        for r in range(2):
            if isp:
                self.evac(self.KR[r * 64:r * 64 + 32, tl.pos0:tl.pos0 + NT], pT[0:32, 0:NT], [pk], [("KR", tl.j)])
            else:
                self.evac(self.krT[r * 64:r * 64 + 32, 0:NT], pT[0:32, 0:NT], [pk], ["krT"])

        if isp:
            for i in range(2):
                self.dma("sp", self.rfm[64:96, i, 0:NT], o["rope_fm_p"].ap()[i, :, tl.pos0:tl.pos0 + NT], writes=["rfm"])
        else:
            for i in range(2):
                self.dma("sp", self.rfm[64:96, i, 0:NT], o["rope_fm_s"].ap()[i], writes=["rfm"])
        Wq = W["qb"]
        for h in range(NH):
            blk, hh = divmod(h, 4)
            if hh == 0:
                wv, wk = self.wnext(Wq, blk)
            c, r = divmod(h, 2)
            ps, pk = self.nextA()
            for kc in range(3):
                self.mm(ps[0:96, 0:NT], wv(kc, hh * 96, hh * 96 + 96), self.cqT[:, kc, 0:NT], kc == 0, kc == 2,
                        [wk, ("cqT", kc)], [pk])
            self.cp(self.qT[r * 64:(r + 1) * 64, c, 0:NT], ps[0:64, 0:NT], [pk], [self.kq(c)], eng="act")
            self.cp(self.xr[64:96, 0:NT], ps[64:96, 0:NT], [pk], ["xr"], eng="act")
            self.mm(self.psM[64:96, 0:NT], self.Rrot[64:96, 0:32], self.xr[64:96, 0:NT], True, True, ["xr", "cst"], ["psM"])
            self.tt(self.t1[64:96, 0:NT], ps[64:96, 0:NT], self.rfm[64:96, 0, 0:NT], ALU.mult, [pk, "rfm"], ["t1"])
            self.tt(self.t2[64:96, 0:NT], self.psM[64:96, 0:NT], self.rfm[64:96, 1, 0:NT], ALU.mult, ["psM", "rfm"], ["t2"])
            self.tt(self.qR[r * 64:r * 64 + 32, c, 0:NT], self.t1[64:96, 0:NT], self.t2[64:96, 0:NT], ALU.add,
                    ["t1", "t2"], [("gT", c)])

        for c in range(8):
            ps, pk = self.nextA()
            for kc in range(2):
                self.mm(ps[:, 0:NT], self.wkvk[:, kc, c * 128:(c + 1) * 128], self.ckvT[:, kc, 0:NT], kc == 0, kc == 1,
                        ["wkvk", ("ckvT", kc)], [pk])
            self.evac(self.kT[:, c, 0:NT], ps[:, 0:NT], [pk], [self.kk(c)])
            if isp:
                self.dma("sp", self.KTd.ap()[c][:, tl.pos0:tl.pos0 + NT], self.kT[:, c, 0:NT],
                         reads=[self.kk(c)], writes=[("KTd", c, tl.j)])
        for s in range(nsub):
            for half in range(2):
                ps, pk = self.nextA()
                for kc in range(2):
                    self.mm(ps[:, 0:512], self.ckvT[:, kc, s * 128:(s + 1) * 128], self.wkvv[:, kc, half * 512:(half + 1) * 512],
                            kc == 0, kc == 1, ["wkvv", ("ckvT", kc)], [pk])
                if isp:
                    self.evac(self.vt_bf[:, s, half * 8:(half + 1) * 8, 0:64],
                              ps[:, 0:512].rearrange("p (h d) -> p h d", d=64), [pk], [("vt_bf", s)])
                else:
                    self.evac(self.vnew[:, half * 8:(half + 1) * 8, 0:64],
                              ps[:, 0:512].rearrange("p (h d) -> p h d", d=64), [pk], ["vnew"])
        self.phase("mla attn")
        if isp:
            self.store_v_tile(tl)
            for c in range(8):
                self.attn_prompt_pair(tl, c, True)
        else:
            npast = cfg.P // 128
            for c in range(8):
                self.memset(self.oT[:, c, 0:128], 0.0, [self.ko(c)])
            for b in range(NSEQ):
                self.dma("pool", self.ckvc[:, 0:npast, :],
                         o["cache_mla_ckv"].ap()[lj, b].rearrange("(t p) d -> p t d", p=128),
                         writes=[("h_bf", s) for s in range(4)])
                for kc in range(2):
                    for g in range(0, npast, 4):
                        pT, pk = self.nextT()
                        for i in range(4):
                            self.tr(pT[:, i * 128:(i + 1) * 128], self.ckvc[:, g + i, kc * 128:(kc + 1) * 128],
                                    [("h_bf", s) for s in range(4)], [pk])
                        self.evac(self.ckvTs[:, kc, g * 128:(g + 4) * 128], pT[:, 0:512], [pk], [("hT", kk2) for kk2 in range(8)])
                self.dma("pool", self.krc[:, 0:npast, :],
                         o["cache_mla_krope"].ap()[lj, b].rearrange("(t p) d -> p t d", p=128), writes=["krc"])
                for g in range(0, npast, 4):
                    pT, pk = self.nextT()
                    for i in range(4):
                        self.tr(pT[0:32, i * 128:(i + 1) * 128], self.krc[:, g + i, :], ["krc"], [pk])
                    for r in range(2):
                        self.evac(self.KR[r * 64:r * 64 + 32, g * 128:(g + 4) * 128], pT[0:32, 0:512], [pk], [("KR", g // 4)])
                self.cp(self.vnew0[0:DS, :, :], self.vnew[32 * b:32 * b + DS, :, :], ["vnew"], ["vnew0"])
                for c in range(8):
                    self.attn_sample_pair(l, b, c, True)
                self.pipe_flush()
        self.pipe_flush()
        self.phase("mla out")
        self.out_proj_residual(W["o"], tl, self.oT, None, 8, keyfn=self.ko)

    def load_mem_kv(self, ksrc, vsrc, reads):
        self.dma("pool", self.h_bf[:, 0:2, :], ksrc.rearrange("(t p) d -> p t d", p=128), reads=reads,
                 writes=[("h_bf", 0), ("h_bf", 1)])
        self.dma("pool", self.MV[:, :, :], vsrc.rearrange("(t p) d -> p t d", p=128), reads=reads, writes=["MV"])
        for g in range(0, 8, 2):
            pT, pk = self.nextT()
            for i in range(2):
                for t in range(2):
                    self.tr(pT[:, i * 256 + t * 128: i * 256 + (t + 1) * 128], self.h_bf[:, t, (g + i) * 128:(g + i + 1) * 128],
                            [("h_bf", 0), ("h_bf", 1)], [pk])
            self.evac(self.MKT[:, g:g + 2, :], pT[:, 0:512].rearrange("p (i k) -> p i k", i=2), [pk], ["MKT"])

    def cross_heads(self, n0, n1):
        for hx in range(4):
            pts = []
            for mkt in range(2):
                psS, sk = self.nextS()
                for dc in range(2):
                    self.mm(psS[:, n0:n1], self.MKT[:, hx * 2 + dc, mkt * 128:(mkt + 1) * 128], self.qT[:, hx * 2 + dc, n0:n1],
                            dc == 0, dc == 1, ["MKT", self.kq(hx * 2 + dc)], [sk])
                PT, ptk = self.nextPT()
                self.act(PT[:, n0:n1], psS[:, n0:n1], AF.Exp, [sk], [ptk], scale=X_SCALE)
                pts.append((PT, ptk))
            for mkt in range(2):
                self.mm(self.psM[:, n0:n1], self.ones_b[:, :], pts[mkt][0][:, n0:n1], mkt == 0, mkt == 1,
                        [pts[mkt][1], "cst"], ["psM"])
            self.recip(self.rden[:, n0:n1], self.psM[:, n0:n1], ["psM"], ["bcg"])
            for dc in range(2):
                psO, ok = self.nextO()
                for mkt in range(2):
                    self.mm(psO[:, n0:n1], self.MV[:, mkt, hx * 256 + dc * 128: hx * 256 + (dc + 1) * 128],
                            pts[mkt][0][:, n0:n1], mkt == 0, mkt == 1, ["MV", pts[mkt][1]], [ok])
                self.tt(self.oT[:, hx * 2 + dc, n0:n1], psO[:, n0:n1], self.rden[:, n0:n1], ALU.mult,
                        [ok, "bcg"], [self.ko(hx * 2 + dc)])

    def cross_attn(self, l, tl):
        o = self.dram
        W = self.W[l]
        NT = tl.NT
        self.norm_hT(tl, "g_cross", l)
        self.proj_fm(W["xq"], 8, tl, self.hT, "hT", 8,
                     lambda c, ps, pk: self.evac(self.qT[:, c, 0:NT], ps[:, 0:NT], [pk], [self.kq(c)]))
        if tl.kind == "p":
            if tl.j == 0:
                self.load_mem_kv(o["memk_p"].ap()[l], o["memv_p"].ap()[l], [("memkv_out", l)])
            self.cross_heads(0, NT)
        else:
            for c in range(8):
                self.memset(self.oT[:, c, 0:128], 0.0, [self.ko(c)])
            for b in range(NSEQ):
                self.load_mem_kv(o["cache_mem_k"].ap()[l, b], o["cache_mem_v"].ap()[l, b], [])
                self.cross_heads(32 * b, 32 * b + DS)
        self.out_proj_residual(W["xo"], tl, self.oT, None, 8, keyfn=self.ko)

    def ffn(self, l, tl):
        W = self.W[l]
        NT = tl.NT
        self.norm_hT(tl, "g_ffn", l)
        NFC = DFF // 128
        for grp in range(0, NFC, 4):
            n = min(4, NFC - grp)
            gv, gk = self.wnext(W["gu_g"], grp // 4)
            uv, uk = self.wnext(W["gu_u"], grp // 4)
            for i in range(n):
                f = grp + i
                psg, pgk = self.nextA()
                for kc in range(8):
                    self.mm(psg[:, 0:NT], gv(kc, i * 128, i * 128 + 128), self.hT[:, kc, 0:NT], kc == 0, kc == 7,
                            [gk, ("hT", kc)], [pgk])
                psu, puk = self.nextS()
                for kc in range(8):
                    self.mm(psu[:, 0:NT], uv(kc, i * 128, i * 128 + 128), self.hT[:, kc, 0:NT], kc == 0, kc == 7,
                            [uk, ("hT", kc)], [puk])
                sg, sgk = self.nextStage()
                self.act(sg[:, 0:NT], psg[:, 0:NT], AF.Silu, [pgk], [sgk])
                self.tt(self.aT[:, f, 0:NT], psu[:, 0:NT], sg[:, 0:NT], ALU.mult, [puk, sgk], [self.ka(f)])
        Wd = W["down"]
        for half in range(2):
            accs = [(self.psA[0], ("psA", 0)), (self.psA[1], ("psA", 1)), (self.psS[0], ("psS", 0)), (self.psS[1], ("psS", 1))]
            for fb in range(Wd["nblk"] // 2):
                dv, dk = self.wnext(Wd, half * (Wd["nblk"] // 2) + fb)
                nf = min(4, NFC - fb * 4)
                for s in range(tl.nsub):
                    ps, pk = accs[s]
                    for i in range(nf):
                        f = fb * 4 + i
                        self.mm(ps[:, 0:512], self.aT[:, f, s * 128:(s + 1) * 128], dv(i, 0, 512), f == 0, f == NFC - 1,
                                [dk, self.ka(f)], [pk])
            for s in range(tl.nsub):
                ps, pk = accs[s]
                self.tt(self.x_sb[:, s, half * 512:(half + 1) * 512], ps[:, 0:512], self.x_sb[:, s, half * 512:(half + 1) * 512],
                        ALU.add, [pk, ("x", s)], [("x", s)])

    def memory_kv(self):
        cfg = self.cfg
        o = self.dram
        tl = Tile("p", 0)
        tl.nsub, tl.NT = 2, 256
        for s in range(2):
            self.dma("sp", self.x_sb[:, s, :], o["mem_prompt"].ap()[s * 128:(s + 1) * 128, :], writes=[("x", s)])
        self.rstd_of(lambda s: self.x_sb[:, s, :], 2, D, "x", lambda s: [("x", s)])
        for l in range(cfg.L):
            gk = ("gbc", "g_mem")
            self.dma("sp", self.gbc[:, :], self.bcast_row("g_mem", l * D, D), writes=[gk])
            for s in range(2):
                self.stt(self.h_bf[:, s, :], self.x_sb[:, s, :], self.rstd[:, s:s + 1], self.gbc[:, :],
                         ALU.mult, ALU.mult, [("x", s), "rstd", gk], [("h_bf", s)])
            self.transpose_to(self.h_bf, [("h_bf", s) for s in range(2)], self.hT, "hT", 8, tl)
            for blk in range(4):
                slot = self.rrW
                self.rrW = (self.rrW + 1) % len(self.wbuf)
                src = o["w_x_kv"].ap()[l][:, blk * 512:(blk + 1) * 512].rearrange("(k p) c -> p k c", p=128)
                self.dma("pool", self.wbuf[slot][:, 0:4096].rearrange("p (k c) -> p k c", k=8), src, writes=[("w", slot)])
                for s in range(2):
                    ps, pk = self.nextA()
                    for kc in range(8):
                        self.mm(ps[:, 0:512], self.hT[:, kc, s * 128:(s + 1) * 128],
                                self.wbuf[slot][:, kc * 512:(kc + 1) * 512], kc == 0, kc == 7, [("w", slot), ("hT", kc)], [pk])
                    st, sk = self.nextStage()
                    self.evac(st[:, 0:512], ps[:, 0:512], [pk], [sk])
                    dst = o["memk_p"] if blk < 2 else o["memv_p"]
                    cb = (blk % 2) * 512
                    self.dma("sp", dst.ap()[l][s * 128:(s + 1) * 128, cb:cb + 512], st[:, 0:512], reads=[sk],
                             writes=[("memkv_out", l)])

    def weight_plan(self, l):
        W = self.W[l]
        p = []
        if l % 2 == 0:
            p += [(W["q"], 0), (W["q"], 1), (W["k"], 0), (W["k"], 1), (W["v"], 0), (W["v"], 1), (W["f"], 0),
                  (W["g"], 0), (W["g"], 1), (W["o"], 0), (W["o"], 1)]
        else:
            p += [(W["a"], 0), (W["a"], 1)] + [(W["qb"], i) for i in range(4)] + [(W["o"], 0), (W["o"], 1)]
        if "cross" not in self.cfg.skip:
            p += [(W["xq"], 0), (W["xq"], 1), (W["xo"], 0), (W["xo"], 1)]
        if "ffn" not in self.cfg.skip:
            for g in range(6):
                p += [(W["gu_g"], g), (W["gu_u"], g)]
            nb = W["down"]["nblk"]
            p += [(W["down"], i) for i in range(nb)]
        return p

    def convert_layer(self, l):
        cw = self.conv_weight
        W = {}
        lj = l // 2
        if l % 2 == 0:
            W["q"] = cw(f"wq{l}", "w_fox_in", lj, D, 0, 1024, 512, FOX_IN)
            W["k"] = cw(f"wk{l}", "w_fox_in", lj, D, 1024, 2048, 512, FOX_IN)
            W["v"] = cw(f"wv{l}", "w_fox_in", lj, D, 2048, 3072, 512, FOX_IN)
            W["f"] = cw(f"wf{l}", "w_fox_in", lj, D, 3072, 3088, 16, FOX_IN)
            W["g"] = cw(f"wg{l}", "w_fox_in", lj, D, 3088, 4112, 512, FOX_IN)
            W["o"] = cw(f"wo{l}", "w_fox_out", lj, D, 0, 1024, 512, D)
        else:
            W["a"] = cw(f"wa{l}", "w_mla_a", lj, D, 0, MLA_DOWN, 512, MLA_DOWN)
            W["qb"] = cw(f"wqb{l}", "w_mla_qb", lj, MLA_QL, 0, 1536, 384, 1536)
            for nm, off in (("kvk", 0), ("kvv", 64)):
                scr = self.dscr(f"w{nm}{l}", [1, 128, 2 * 1024], BF16)
                srcw = self.dram["w_mla_kvb"].ap()[lj].rearrange("(k p) (h e) -> p k h e", p=128, e=128)
                dstw = scr.ap()[0].rearrange("p (k h d) -> p k h d", k=2, d=64)
                for kc in range(2):
                    self.dma("pool", dstw[:, kc, :, :], srcw[:, kc, :, off:off + 64], writes=[("wb", f"w{nm}{l}", 0)])
                W[nm] = dict(name=f"w{nm}{l}", KC=2, cb=1024, nblk=1, M=1024, scr=scr)
            W["o"] = cw(f"wo{l}", "w_mla_out", lj, D, 0, 1024, 512, D)
        W["xq"] = cw(f"wxq{l}", "w_x_q", l, D, 0, 1024, 512, D)
        W["xo"] = cw(f"wxo{l}", "w_x_o", l, D, 0, 1024, 512, D)
        W["gu_g"] = cw(f"wgg{l}", "w_ffn_gu", l, D, 0, DFF, 512, 2 * DFF)
        W["gu_u"] = cw(f"wgu{l}", "w_ffn_gu", l, D, DFF, 2 * DFF, 512, 2 * DFF)
        name = f"wd{l}"
        nfb = (DFF // 128 + 3) // 4
        scr = self.dscr(name, [2 * nfb, 128, 4 * 512], BF16)
        src = self.dram["w_ffn_down"].ap()[l]
        for half in range(2):
            for fb in range(nfb):
                nf = min(4, DFF // 128 - fb * 4)
                i = src[fb * 512: fb * 512 + nf * 128, half * 512:(half + 1) * 512].rearrange("(k p) c -> p k c", p=128)
                ov = scr.ap()[half * nfb + fb].rearrange("p (k c) -> p k c", k=4)[:, 0:nf, :]
                self.dma("pool", ov, i, writes=[("wb", name, half * nfb + fb)])
        W["down"] = dict(name=name, KC=4, cb=512, nblk=2 * nfb, M=1024, scr=scr)
        self.W[l] = W

    def phase(self, name):
        self.nphase = getattr(self, "nphase", 0) + 1
        if self.cfg.stop is not None and self.nphase > self.cfg.stop:
            print("STOP before phase", self.nphase, name)
            raise StopBuild()
        if self.cfg.stop is not None:
            print("phase", self.nphase, name)

    def layer_tile(self, l, tl):
        cfg = self.cfg
        o = self.dram
        last = l == cfg.L - 1
        self.phase(f"L{l} {tl.kind}{tl.j} load+mixer")
        if tl.kind == "s":
            self.memset(self.x_sb[:, 0, :], 0.0, [("x", 0)])
            src = o["x_sample"] if l == 0 else o["xres_s"]
            self.rows_dma_in(tl, lambda s: self.x_sb[:, s, :], src.ap(), D, writes=[("x", 0)])
        else:
            src = o["x_prompt"] if l == 0 else o["xres_p"]
            rd = [] if l == 0 else [("xres_p", tl.j)]
            for s in range(tl.nsub):
                r0 = tl.pos0 + s * 128
                self.dma("sp", self.x_sb[:, s, :], src.ap()[r0:r0 + 128, :], reads=rd, writes=[("x", s)])
        self.wplan(self.weight_plan(l))
        if l % 2 == 0:
            self.fox_mixer(l, tl)
        else:
            self.mla_mixer(l, tl)
        if "cross" not in cfg.skip:
            self.phase(f"L{l} {tl.kind}{tl.j} cross")
            self.cross_attn(l, tl)
        if "ffn" not in cfg.skip:
            self.phase(f"L{l} {tl.kind}{tl.j} ffn")
            self.ffn(l, tl)
        assert self.plan_i == len(self.plan)
        if not last:
            if tl.kind == "s":
                self.rows_dma_out(tl, 0, self.x_sb[:, 0, :], o["xres_s"].ap(), 0, D, [("x", 0)])
            else:
                for s in range(tl.nsub):
                    r0 = tl.pos0 + s * 128
                    self.dma("sp", o["xres_p"].ap()[r0:r0 + 128, :], self.x_sb[:, s, :], reads=[("x", s)],
                             writes=[("xres_p", tl.j)])
        else:
            gk = ("gbc", "g_final")
            self.dma("sp", self.gbc[:, :], self.bcast_row("g_final", 0, D), writes=[gk])
            self.rstd_of(lambda s: self.x_sb[:, s, :], tl.nsub, D, "x", lambda s: [("x", s)])
            for s in range(tl.nsub):
                for half in range(2):
                    st, sk = self.nextStage()
                    self.stt(st[:, 0:512], self.x_sb[:, s, half * 512:(half + 1) * 512], self.rstd[:, s:s + 1],
                             self.gbc[:, half * 512:(half + 1) * 512], ALU.mult, ALU.mult, [("x", s), "rstd", gk], [sk])
                    self.rows_dma_out(tl, s, st[:, 0:512], o["y_p" if tl.kind == "p" else "y_s"].ap(),
                                      half * 512, (half + 1) * 512, [sk])

    def build(self):
        cfg = self.cfg
        T, P, L, NF, NM = cfg.T, cfg.P, cfg.L, cfg.NF, cfg.NM
        din, dout = self.din, self.dout
        din("x_prompt", [T, D]); din("x_sample", [NSEQ * DS, D]); din("mem_prompt", [NMEM, D])
        din("cache_fox_k", [NF, NSEQ, P, D]); din("cache_fox_v", [NF, NSEQ, P, D]); din("cache_fox_logf", [NF, NSEQ, P, NH])
        din("cache_mla_ckv", [max(NM, 1), NSEQ, P, MLA_KVL]); din("cache_mla_krope", [max(NM, 1), NSEQ, P, MLA_R])
        din("cache_mem_k", [L, NSEQ, NMEM, D]); din("cache_mem_v", [L, NSEQ, NMEM, D])
        din("g_mix", [L * D]); din("g_cross", [L * D]); din("g_mem", [L * D]); din("g_ffn", [L * D]); din("g_final", [D])
        din("w_fox_in", [NF, D, FOX_IN]); din("b_fox_f", [NF * NH]); din("w_fox_out", [NF, D, D])
        din("w_mla_a", [max(NM, 1), D, MLA_DOWN]); din("g_mla_q", [max(NM, 1) * MLA_QL]); din("g_mla_kv", [max(NM, 1) * MLA_KVL])
        din("w_mla_qb", [max(NM, 1), MLA_QL, 1536]); din("w_mla_kvb", [max(NM, 1), MLA_KVL, 2048]); din("w_mla_out", [max(NM, 1), D, D])
        din("w_x_q", [L, D, D]); din("w_x_kv", [L, D, 2 * D]); din("w_x_o", [L, D, D])
        din("w_ffn_gu", [L, D, 2 * DFF]); din("w_ffn_down", [L, DFF, D])
        din("cst_bf", [128, 640], BF16); din("cst_f", [128, 256])
        din("rope_tm_p", [T, 32]); din("rope_tm_s", [128, 32]); din("rope_fm_p", [2, 32, T]); din("rope_fm_s", [2, 32, 128])
        dout("y_p", [T, D]); dout("y_s", [NSEQ * DS, D])
        dout("fk_p", [NF * T, D]); dout("fv_p", [NF * T, D]); dout("fl_p", [NF * T, NH])
        dout("mc_p", [max(NM, 1) * T, MLA_KVL]); dout("mr_p", [max(NM, 1) * T, MLA_R])
        dout("memk_p", [L, NMEM, D]); dout("memv_p", [L, NMEM, D])
        dout("fk_s", [NF * NSEQ * DS, D]); dout("fv_s", [NF * NSEQ * DS, D]); dout("fl_s", [NF * NSEQ * DS, NH])
        dout("mc_s", [max(NM, 1) * NSEQ * DS, MLA_KVL]); dout("mr_s", [max(NM, 1) * NSEQ * DS, MLA_R])
        self.dscr("xres_p", [T, D], F32); self.dscr("xres_s", [NSEQ * DS, D], F32)
        self.KTd = self.dscr("KTd", [8, 128, T], BF16)

        sb, ps = self.sb, self.ps
        NKT = max(T, P) // 128
        self.Vd = self.dscr("Vd", [8, 128, T // 128, 130], BF16)
        self.x_sb = sb("x_sb", [128, 4, D], F32)
        self.h_bf = sb("h_bf", [128, 4, D], BF16)
        self.ckvc = self.h_bf[:, :, :].rearrange("p s d -> p (s d)")[:, 0:(P // 128) * MLA_KVL].rearrange("p (t d) -> p t d", d=MLA_KVL)
        self.hT = sb("hT", [128, 8, TS], BF16)
        self.ckvTs = self.hT[:, :, :].rearrange("p c n -> p (c n)")[:, 0:2 * P].rearrange("p (k n) -> p k n", k=2)
        assert (P // 128) * MLA_KVL <= 4 * D and 2 * P <= 8 * TS
        self.R = sb("R", [128, 24, TS], BF16)
        self.qT = self.R[:, 0:8, :]
        self.kT = self.R[:, 8:16, :]
        self.oT = self.R[:, 16:24, :]
        self.aT = self.R[:, 0:22, :]
        self.gT = sb("gT", [128, 8, TS], BF16)
        self.qR = self.gT
        self.vt_bf = sb("vt_bf", [128, 4, NH, 65], BF16)
        self.KC = [sb(f"KC{i}", [128, 1024], BF16) for i in range(3)]
        self.VC = [sb(f"VC{i}", [128, 8, 2, 65], BF16) for i in range(3)]
        self.KR = sb("KR", [128, max(T, P)], BF16)
        self.wbuf = [sb(f"wbuf{i}", [128, 4096], BF16) for i in range(3)]
        self.wkvk = sb("wkvk", [128, 2, 1024], BF16); self.wkvv = sb("wkvv", [128, 2, 1024], BF16)
        self.stage = [sb(f"stage{i}", [128, 512], F32) for i in range(4)]
        self.PT = [sb(f"PT{i}", [128, 512], BF16) for i in range(4)]
        self.gbc = sb("gbc", [128, D], F32)
        self.junk = sb("junk", [128, D], BF16)
        self.ss = sb("ss", [128, 4], F32); self.ss2 = sb("ss2", [128, 4], F32); self.rstd = sb("rstd", [128, 4], F32)
        self.one_col = sb("one_col", [128, 1], F32)
        self.cst_bf = sb("cst_bf_sb", [128, 640], BF16); self.cst_f = sb("cst_f_sb", [128, 256], F32)
        self.ident = self.cst_bf[:, 0:128]; self.maskF = self.cst_bf[:, 128:256]; self.maskM = self.cst_bf[:, 256:384]
        self.ones_b = self.cst_bf[:, 384:512]; self.Rrot = self.cst_bf[:, 512:544]
        self.Umat = self.cst_f[:, 0:128]; self.ones_f = self.cst_f[:, 128:256]
        self.bcg = sb("bcg", [128, 512], F32); self.rden = self.bcg
        self.rd = self.bcg
        self.qm = [sb(f"qm{i}", [128, TS], BF16) for i in range(4)]
        self.lf = sb("lf", [128, 4, NH], F32); self.bfb = sb("bfb", [128, NH], F32)
        self.c_all = sb("c_all", [128, NKT, NH], F32); self.bias_all = sb("bias_all", [128, NKT, NH], F32)
        self.Sprev = sb("Sprev", [128, NH], F32); self.cref = sb("cref", [128, NH], F32)
        self.lfs = sb("lfs", [128, P // 128, NH], F32); self.lfn0 = sb("lfn0", [DS, NH], F32)
        self.bias_new = sb("bias_new", [DS, NH], F32)
        self.vnew = sb("vnew", [128, NH, 65], BF16); self.vnew0 = sb("vnew0", [DS, NH, 65], BF16)
        self.kc_bf = sb("kc_bf", [128, 8, 128], BF16)
        self.MKT = sb("MKT", [128, 8, NMEM], BF16); self.MV = sb("MV", [128, 2, D], BF16)
        self.a_sb = sb("a_sb", [128, 4, MLA_DOWN], F32)
        self.cq_bf = sb("cq_bf", [128, 4, MLA_QL], BF16); self.cqT = sb("cqT", [128, 3, TS], BF16)
        self.ckv_bf = sb("ckv_bf", [128, 4, MLA_KVL], BF16); self.ckvT = sb("ckvT", [128, 2, TS], BF16)
        self.kr_bf = sb("kr_bf", [128, 4, MLA_R], BF16); self.krT = sb("krT", [128, 128], BF16)
        self.gq = sb("gq", [128, MLA_QL], F32); self.gkv = sb("gkv", [128, MLA_KVL], F32)
        self.rtm = sb("rtm", [128, 4, 32], F32); self.rfm = sb("rfm", [128, 2, TS], F32)
        self.xr = sb("xr", [128, TS], BF16)
        self.t1 = sb("t1", [128, TS], F32); self.t2 = sb("t2", [128, TS], F32)
        self.krc = sb("krc", [128, P // 128, MLA_R], BF16)
        self.psA = [ps(f"psA{i}", [128, 512], F32) for i in range(2)]
        self.psS = [ps(f"psS{i}", [128, 512], F32) for i in range(2)]
        self.psO = [ps(f"psO{i}", [128, 512], F32) for i in range(2)]
        self.psM = ps("psM", [128, 512], F32)
        self.psT = ps("psT", [128, 1024], BF16)
        self.psM16 = self.psM[:, :].bitcast(BF16)
        assert list(self.psM16.shape) == [128, 1024], self.psM16.shape
        self.psT32 = self.psT[:, :].bitcast(F32)
        assert list(self.psT32.shape) == [128, 512], self.psT32.shape

        o = self.dram
        self.dma("sp", self.cst_bf[:, :], o["cst_bf"].ap(), writes=["cst"])
        self.dma("sp", self.cst_f[:, :], o["cst_f"].ap(), writes=["cst"])
        for i in range(3):
            self.memset(self.VC[i][:, :, :, 64:65], 1.0, [("VC", i)])
        self.memset(self.vt_bf[:, :, :, 64:65], 1.0, [("vt_bf", s) for s in range(4)])
        self.memset(self.vnew[:, :, 64:65], 1.0, ["vnew"])
        self.memset(self.one_col[:, :], 1.0, ["one_col"])
        for i in range(4):
            self.memset(self.qm[i][:, :], 0.0, [("qm", i)])
        self.memset(self.KR[:, :], 0.0, [("KR", i) for i in range(max(T, P) // 512)])
        self.memset(self.krT[:, :], 0.0, ["krT"])
        self.W = {}
        try:
            self.phase("convert0")
            self.convert_layer(0)
            self.phase("memory_kv")
            self.memory_kv()
            tiles = [Tile("s", 0)] + [Tile("p", j) for j in range(T // TS)]
            for l in range(L):
                if l + 1 < L:
                    self.convert_layer(l + 1)
                if l % 2 == 0:
                    self.memset(self.Sprev[:, :], 0.0, ["Sprev"])
                for tl in tiles:
                    if tl.kind == "p" and tl.j == 0 and l % 2 == 0:
                        self.memset(self.Sprev[:, :], 0.0, ["Sprev"])
                    self.layer_tile(l, tl)
        except StopBuild:
            pass
        self.S.emit(self.nc)
        self.es.close()
        return self.nc


def _constants(cfg):
    bf = ml_dtypes.bfloat16
    cb = np.zeros((128, 640), np.float32)
    k = np.arange(128)[:, None]
    q = np.arange(128)[None, :]
    cb[:, 0:128] = np.eye(128)
    cb[:, 128:256] = np.where(k > q, NEG, 0.0)
    cb[:, 256:384] = np.where((k // 64) > (q // 64), NEG, 0.0)
    cb[:, 384:512] = 1.0
    R = np.zeros((32, 32), np.float32)
    for m in range(16):
        R[m + 16, m] = -1.0
        R[m, m + 16] = 1.0
    cb[64:96, 512:544] = R
    cf = np.zeros((128, 256), np.float32)
    cf[:, 0:128] = (k <= q)
    cf[:, 128:256] = 1.0
    half = 16
    inv = (np.float32(10000.0) ** (-(np.arange(half, dtype=np.float32) / np.float32(half)))).astype(np.float32)

    def tables(pos):
        ang = pos.astype(np.float32)[:, None] * inv[None, :]
        return np.cos(ang).astype(np.float32), np.sin(ang).astype(np.float32)
    cp, sp_ = tables(np.arange(cfg.T))
    rope_tm_p = np.concatenate([cp, sp_], axis=1)
    rope_fm_p = np.stack([np.concatenate([cp, cp], 1).T, np.concatenate([sp_, sp_], 1).T]).astype(np.float32)
    cs, ss = tables(cfg.P + np.arange(DS))
    rope_tm_s = np.zeros((128, 32), np.float32)
    rope_fm_s = np.zeros((2, 32, 128), np.float32)
    for b in range(NSEQ):
        rope_tm_s[32 * b:32 * b + DS] = np.concatenate([cs, ss], 1)
        rope_fm_s[0, :, 32 * b:32 * b + DS] = np.concatenate([cs, cs], 1).T
        rope_fm_s[1, :, 32 * b:32 * b + DS] = np.concatenate([ss, ss], 1).T
    return dict(cst_bf=cb.astype(bf), cst_f=cf, rope_tm_p=np.ascontiguousarray(rope_tm_p),
                rope_tm_s=rope_tm_s, rope_fm_p=np.ascontiguousarray(rope_fm_p), rope_fm_s=rope_fm_s)


def run(cfg, inputs):
    nc = Builder(cfg).build()
    T, P, L, NF, NM = cfg.T, cfg.P, cfg.L, cfg.NF, cfg.NM
    consts = _constants(cfg)
    c32 = lambda a: np.ascontiguousarray(a, dtype=np.float32)
    shared = {}
    for k in ("g_mix", "g_cross", "g_mem", "g_ffn", "g_final", "b_fox_f", "g_mla_q", "g_mla_kv"):
        shared[k] = c32(inputs[k]).reshape(-1)
    for k in ("w_fox_in", "w_fox_out", "w_mla_a", "w_mla_qb", "w_mla_kvb", "w_mla_out", "w_x_q", "w_x_kv", "w_x_o",
              "w_ffn_gu", "w_ffn_down"):
        shared[k] = c32(inputs[k])
    shared.update(consts)
    in_maps = []
    for c in range(cfg.ncores):
        m = dict(shared)
        m["x_prompt"] = c32(inputs["x_prompt"][c])
        m["x_sample"] = c32(inputs["x_sample"][NSEQ * c:NSEQ * (c + 1)]).reshape(NSEQ * DS, D)
        m["mem_prompt"] = c32(inputs["mem_prompt"][c])
        sl = slice(NSEQ * c, NSEQ * (c + 1))
        m["cache_fox_k"] = c32(inputs["cache_fox_k"][:, sl]).reshape(NF, NSEQ, P, D)
        m["cache_fox_v"] = c32(inputs["cache_fox_v"][:, sl]).reshape(NF, NSEQ, P, D)
        m["cache_fox_logf"] = c32(inputs["cache_fox_logf"][:, sl])
        m["cache_mla_ckv"] = c32(inputs["cache_mla_ckv"][:, sl])
        m["cache_mla_krope"] = c32(inputs["cache_mla_krope"][:, sl])
        m["cache_mem_k"] = c32(inputs["cache_mem_k"][:, sl]).reshape(L, NSEQ, NMEM, D)
        m["cache_mem_v"] = c32(inputs["cache_mem_v"][:, sl]).reshape(L, NSEQ, NMEM, D)
        in_maps.append(m)
    res = run_bass_kernel_spmd(nc, in_maps, core_ids=list(range(cfg.ncores)))
    R = res.results
    B = cfg.ncores

    def gat(name, shape_per_core, axis):
        return np.stack([np.asarray(R[c][name], dtype=np.float32).reshape(shape_per_core) for c in range(B)], axis=axis)

    y_p = gat("y_p", (T, D), 0)
    y_s = gat("y_s", (NSEQ, DS, D), 0).reshape(B * NSEQ, DS, D)
    fk_p = gat("fk_p", (NF, T, NH, HD), 1)
    fv_p = gat("fv_p", (NF, T, NH, HD), 1)
    fl_p = gat("fl_p", (NF, T, NH), 1)
    mc_p = gat("mc_p", (NM, T, MLA_KVL), 1)
    mr_p = gat("mr_p", (NM, T, MLA_R), 1)
    mk_p = gat("memk_p", (L, NMEM, 4, 256), 1)
    mv_p = gat("memv_p", (L, NMEM, 4, 256), 1)
    fk_s = gat("fk_s", (NF, NSEQ, DS, NH, HD), 1).reshape(NF, B * NSEQ, DS, NH, HD)
    fv_s = gat("fv_s", (NF, NSEQ, DS, NH, HD), 1).reshape(NF, B * NSEQ, DS, NH, HD)
    fl_s = gat("fl_s", (NF, NSEQ, DS, NH), 1).reshape(NF, B * NSEQ, DS, NH)
    mc_s = gat("mc_s", (NM, NSEQ, DS, MLA_KVL), 1).reshape(NM, B * NSEQ, DS, MLA_KVL)
    mr_s = gat("mr_s", (NM, NSEQ, DS, MLA_R), 1).reshape(NM, B * NSEQ, DS, MLA_R)
    return (y_p, y_s, fk_p, fv_p, fl_p, mc_p, mr_p, mk_p, mv_p, fk_s, fv_s, fl_s, mc_s, mr_s)


def kernel(**inputs):
    cfg = Cfg(T=4096, P=2048, L=4, ncores=8)
    return run(cfg, inputs)
```

```python
import numpy as np
import ml_dtypes
from contextlib import ExitStack
import concourse.bass as bass
import concourse.mybir as mybir
from concourse.bass_utils import run_bass_kernel_spmd

F32 = mybir.dt.float32
BF16 = mybir.dt.bfloat16
AF = mybir.ActivationFunctionType
ALU = mybir.AluOpType

ENGINES = ("pe", "act", "dve", "pool", "sp")
N_DMA_SEMS = 20


class Op:
    __slots__ = ("eng", "fn", "deps", "needs_inc", "count", "idx", "dma", "dsem", "dval")

    def __init__(self, eng, fn):
        self.eng = eng
        self.fn = fn
        self.deps = []
        self.needs_inc = False
        self.count = None
        self.idx = None
        self.dma = False
        self.dsem = None
        self.dval = None


class Sched:
    def __init__(self):
        self.streams = {e: [] for e in ENGINES}
        self.state = {}
        self.seen = {e: {} for e in ENGINES}
        self.dma_rr = {e: 0 for e in ENGINES}
        self.dma_last = {e: [None] * N_DMA_SEMS for e in ENGINES}
        self.dma_cnt = {e: [0] * N_DMA_SEMS for e in ENGINES}

    def _add_dep(self, op, d):
        if d is None or d is op:
            return
        if d.dma:
            key = ("d", id(d))
            if key in self.seen[op.eng]:
                return
            self.seen[op.eng][key] = True
            op.deps.append(d)
            return
        if d.eng == op.eng and op.eng == "pe" and not op.dma:
            return
        prev = self.seen[op.eng].get(d.eng, -1)
        if d.idx <= prev:
            return
        self.seen[op.eng][d.eng] = d.idx
        d.needs_inc = True
        op.deps.append(d)

    @staticmethod
    def _is_psum(k):
        return k == "psM" or (isinstance(k, tuple) and k[0] in ("psA", "psS", "psO", "psT"))

    def op(self, eng, fn, reads=(), writes=(), dma=False):
        excl = [k for k in reads if self._is_psum(k)]
        if excl:
            reads = [k for k in reads if not self._is_psum(k)]
            writes = list(writes) + [k for k in excl if k not in writes]
        o = Op(eng, fn)
        o.dma = dma
        o.idx = len(self.streams[eng])
        if dma:
            s = self.dma_rr[eng]
            self.dma_rr[eng] = (s + 1) % N_DMA_SEMS
            prev = self.dma_last[eng][s]
            if prev is not None:
                key = ("d", id(prev))
                if key not in self.seen[eng]:
                    self.seen[eng][key] = True
                    o.deps.append(prev)
            self.dma_cnt[eng][s] += 1
            o.dsem = s
            o.dval = 16 * self.dma_cnt[eng][s]
            self.dma_last[eng][s] = o
        for k in reads:
            st = self.state.get(k)
            if st is not None:
                self._add_dep(o, st[0])
        for k in writes:
            st = self.state.get(k)
            if st is not None:
                self._add_dep(o, st[0])
                for r in st[1]:
                    self._add_dep(o, r)
        for k in reads:
            st = self.state.setdefault(k, [None, []])
            st[1].append(o)
        for k in writes:
            self.state[k] = [o, []]
        self.streams[eng].append(o)
        return o

    def emit(self, nc):
        with ExitStack() as es:
            prog = {e: es.enter_context(nc.semaphore(f"prog_{e}")) for e in ENGINES}
            dsem = {e: [es.enter_context(nc.semaphore(f"dma_{e}_{i}")) for i in range(N_DMA_SEMS)]
                    for e in ("sp", "act", "pool")}
            for e in ENGINES:
                c = 0
                for o in self.streams[e]:
                    if o.needs_inc and not o.dma:
                        c += 1
                        o.count = c
            block = es.enter_context(nc.Block())

            def run(ename, eng):
                for o in self.streams[ename]:
                    for d in o.deps:
                        if d.dma:
                            eng.wait_ge(dsem[d.eng][d.dsem], d.dval)
                        else:
                            eng.wait_ge(prog[d.eng], d.count)
                    ins = o.fn(eng)
                    if o.dma:
                        ins.then_inc(dsem[ename][o.dsem], 16)
                    elif o.needs_inc:
                        ins.then_inc(prog[ename], 1)
                if ename in dsem:
                    for s in range(N_DMA_SEMS):
                        last = self.dma_last[ename][s]
                        if last is not None:
                            eng.wait_ge(dsem[ename][s], last.dval)

            @block.sync
            def _(eng):
                run("sp", eng)

            @block.scalar
            def _(eng):
                run("act", eng)

            @block.vector
            def _(eng):
                run("dve", eng)

            @block.gpsimd
            def _(eng):
                run("pool", eng)

            @block.tensor
            def _(eng):
                run("pe", eng)


D = 1024
NH = 16
HD = 64
FOX_IN = 4112
MLA_QL, MLA_KVL, MLA_R = 384, 256, 32
MLA_DOWN = 672
MLA_SCALE = float(96 ** -0.5)
FOX_SCALE = 0.125
X_SCALE = 1.0 / 16.0
DFF = 2816
NMEM = 256
EPS = 1e-6
NEG = -30000.0
TS = 512
DS = 16
NSEQ = 4


class Cfg:
    def __init__(self, T=4096, P=2048, L=4, ncores=8):
        self.T, self.P, self.L, self.ncores = T, P, L, ncores
        self.skip = set()
        self.stop = None
        self.NF = (L + 1) // 2
        self.NM = L // 2
        assert T % TS == 0 and P % 1024 == 0


class StopBuild(Exception):
    pass


class Tile:
    def __init__(self, kind, j):
        self.kind = kind
        self.j = j
        if kind == "p":
            self.nsub, self.NT, self.pos0 = TS // 128, TS, j * TS
        else:
            self.nsub, self.NT, self.pos0 = 1, 128, 0


class Builder:
    def __init__(self, cfg):
        self.cfg = cfg
        self.nc = bass.Bass("TRN2", target_bir_lowering=False)
        self.S = Sched()
        self.es = ExitStack()
        self.dram = {}
        self.rrA = 0
        self.rrE = 0
        self.rrW = 0
        self.rrStage = 0
        self.rrPT = 0
        self.rrS = 0
        self.rrO = 0
        self.rrKT = 0
        self.rrT = 0
        self.rrAcc = 0
        self.rrKV = 0
        self.kv_w_loaded = {}
        self.pend = []
        self.deferred = []
        self.rrS3 = 0
        self.rrQM = 0
        self.cur_qm = None
        self.cur_qrm = None

    def kq(self, c):
        return ("R", c)

    def kk(self, c):
        return ("R", 8 + c)

    def ko(self, c):
        return ("R", 16 + c)

    def ka(self, f):
        return ("R", f)

    def din(self, name, shape, dt=F32):
        t = self.nc.dram_tensor(name, list(shape), dt, kind="ExternalInput")
        self.dram[name] = t
        return t

    def dout(self, name, shape, dt=F32):
        t = self.nc.dram_tensor(name, list(shape), dt, kind="ExternalOutput")
        self.dram[name] = t
        return t

    def dscr(self, name, shape, dt):
        t = self.nc.dram_tensor(name, list(shape), dt)
        self.dram[name] = t
        return t

    def sb(self, name, shape, dt):
        return self.es.enter_context(self.nc.sbuf_tensor(name, list(shape), dt))

    def ps(self, name, shape, dt):
        return self.es.enter_context(self.nc.psum_tensor(name, list(shape), dt))

    def dma(self, q, out, in_, reads=(), writes=()):
        return self.S.op(q, lambda e: e.dma_start(out=out, in_=in_), reads, writes, dma=True)

    def mm(self, out, lhsT, rhs, start, stop, reads, writes):
        return self.S.op("pe", lambda e: e.matmul(out, lhsT=lhsT, rhs=rhs, start=start, stop=stop),
                         reads, writes)

    def tr(self, out, in_, reads, writes):
        idn = self.ident[0:in_.shape[0], 0:in_.shape[0]]
        return self.S.op("pe", lambda e: e.transpose(out, in_, idn), list(reads) + ["cst"], writes)

    def act(self, out, in_, func, reads, writes, bias=None, scale=None, accum_out=None):
        kw = {}
        if bias is not None:
            kw["bias"] = bias
        if scale is not None:
            kw["scale"] = scale
        if accum_out is not None:
            kw["accum_out"] = accum_out
        return self.S.op("act", lambda e: e.activation(out=out, in_=in_, func=func, **kw), reads, writes)

    def tt(self, out, in0, in1, op, reads, writes, eng="dve"):
        return self.S.op(eng, lambda e: e.tensor_tensor(out=out, in0=in0, in1=in1, op=op), reads, writes)

    def ts(self, out, in0, s1, s2, op0, op1, reads, writes, eng="dve"):
        if op1 is None:
            return self.S.op(eng, lambda e: e.tensor_scalar(out=out, in0=in0, scalar1=s1, scalar2=None, op0=op0),
                             reads, writes)
        return self.S.op(eng, lambda e: e.tensor_scalar(out=out, in0=in0, scalar1=s1, scalar2=s2, op0=op0, op1=op1),
                         reads, writes)

    def stt(self, out, in0, scalar, in1, op0, op1, reads, writes):
        return self.S.op("dve", lambda e: e.scalar_tensor_tensor(out=out, in0=in0, scalar=scalar, in1=in1,
                                                                  op0=op0, op1=op1), reads, writes)

    def cp(self, out, in_, reads, writes, eng="dve"):
        if eng == "act":
            return self.S.op("act", lambda e: e.copy(out=out, in_=in_), reads, writes)
        return self.S.op(eng, lambda e: e.tensor_copy(out=out, in_=in_), reads, writes)

    def evac(self, out, in_, reads, writes):
        self.rrE ^= 1
        return self.cp(out, in_, reads, writes, eng=("act" if self.rrE else "dve"))

    def recip(self, out, in_, reads, writes):
        return self.S.op("dve", lambda e: e.reciprocal(out=out, in_=in_), reads, writes)

    def memset(self, ap, val, writes, eng="pool"):
        return self.S.op(eng, lambda e: e.memset(ap, val), (), writes)

    def nextA(self):
        self.rrA ^= 1
        return self.psA[self.rrA], ("psA", self.rrA)

    def nextS(self):
        self.rrS ^= 1
        return self.psS[self.rrS], ("psS", self.rrS)

    def nextO(self):
        self.rrO ^= 1
        return self.psO[self.rrO], ("psO", self.rrO)

    def nextPT(self):
        self.rrPT = (self.rrPT + 1) % len(self.PT)
        return self.PT[self.rrPT], ("PT", self.rrPT)

    def nextT(self):
        self.rrT ^= 1
        if self.rrT:
            return self.psT[:, 0:512], ("psT", 0)
        return self.psM16[:, 0:512], "psM"

    def nextStage(self):
        self.rrStage = (self.rrStage + 1) % len(self.stage)
        return self.stage[self.rrStage], ("stage", self.rrStage)

    def bcast_row(self, tname, row_off, n):
        return bass.AP(self.dram[tname], row_off, [[0, 128], [1, n]])

    def conv_weight(self, name, src_name, src_l, K, c0, c1, cb, src_cols):
        KC = K // 128
        M = c1 - c0
        nblk = (M + cb - 1) // cb
        scr = self.dscr(name, [nblk, 128, KC * cb], BF16)
        src = self.dram[src_name].ap()[src_l]
        for b in range(nblk):
            w = min(cb, M - b * cb)
            o = scr.ap()[b].rearrange("p (k c) -> p k c", k=KC)[:, :, 0:w]
            i = src[:, c0 + b * cb: c0 + b * cb + w].rearrange("(k p) c -> p k c", p=128)
            self.dma("pool", o, i, reads=(), writes=[("wb", name, b)])
        return dict(name=name, KC=KC, cb=cb, nblk=nblk, M=M, scr=scr)

    def wplan(self, items):
        self.plan = list(items)
        self.plan_i = 0
        self.plan_loaded = 0
        self.loaded = {}

    def _issue_load(self, i):
        W, b = self.plan[i]
        slot = self.rrW
        self.rrW = (self.rrW + 1) % len(self.wbuf)
        n = W["KC"] * W["cb"]
        w = min(W["cb"], W["M"] - b * W["cb"]) if W["name"][:2] != "wd" else W["cb"]
        kcv = W["KC"]
        if W["name"][:2] == "wd":
            nfb = W["nblk"] // 2
            kcv = min(4, DFF // 128 - (b % nfb) * 4)
        self.dma("sp", self.wbuf[slot][:, 0:n].rearrange("p (k c) -> p k c", k=W["KC"])[:, 0:kcv, 0:w],
                 W["scr"].ap()[b].rearrange("p (k c) -> p k c", k=W["KC"])[:, 0:kcv, 0:w],
                 reads=[("wb", W["name"], b)], writes=[("w", slot)])
        self.loaded[i] = slot

    def wnext(self, W, b):
        i = self.plan_i
        assert self.plan[i][0] is W and self.plan[i][1] == b, (self.plan[i][0]["name"], self.plan[i][1], W["name"], b)
        while self.plan_loaded < min(len(self.plan), i + 2):
            self._issue_load(self.plan_loaded)
            self.plan_loaded += 1
        slot = self.loaded.pop(i)
        self.plan_i += 1
        cb = W["cb"]
        buf = self.wbuf[slot]

        def view(kc, a, b2):
            return buf[:, kc * cb + a: kc * cb + b2]
        return view, ("w", slot)

    def rstd_of(self, src_fn, nsub, n, rkey, reads):
        for s in range(nsub):
            self.act(self.junk[:, 0:n], src_fn(s), AF.Square, reads(s) if callable(reads) else reads,
                     ["junk", (rkey, "ss", s)], accum_out=self.ss[:, s:s + 1])
        self.ts(self.ss2[:, 0:nsub], self.ss[:, 0:nsub], 1.0 / n, EPS, ALU.mult, ALU.add,
                [(rkey, "ss", s) for s in range(nsub)], ["ss2"])
        self.S.op("act", lambda e: e.sqrt(out=self.ss2[:, 0:nsub], in_=self.ss2[:, 0:nsub]), ["ss2"], ["ss2"])
        self.recip(self.rstd[:, 0:nsub], self.ss2[:, 0:nsub], ["ss2"], ["rstd"])

    def norm_hT(self, tl, gname, l):
        gk = ("gbc", gname)
        self.dma("sp", self.gbc[:, :], self.bcast_row(gname, l * D, D), writes=[gk])
        self.rstd_of(lambda s: self.x_sb[:, s, :], tl.nsub, D, "x", lambda s: [("x", s)])
        for s in range(tl.nsub):
            self.stt(self.h_bf[:, s, :], self.x_sb[:, s, :], self.rstd[:, s:s + 1], self.gbc[:, :],
                     ALU.mult, ALU.mult, [("x", s), "rstd", gk], [("h_bf", s)])
        self.transpose_to(self.h_bf, [("h_bf", s) for s in range(tl.nsub)], self.hT, "hT", 8, tl)

    def transpose_to(self, src, src_keys, dst, dkey, nchunks, tl, csz=128, dst_off=0, keyfn=None):
        for c in range(nchunks):
            pT, pk = self.nextT()
            for s in range(tl.nsub):
                self.tr(pT[0:csz, s * 128:(s + 1) * 128], src[:, s, c * csz:(c + 1) * csz], src_keys, [pk])
            self.evac(dst[0:csz, c, dst_off:dst_off + tl.NT], pT[0:csz, 0:tl.NT], [pk],
                      [keyfn(c) if keyfn else (dkey, c)])

    def rows_dma_in(self, tl, dst_fn, src_t, ncols, row_base=0, writes=()):
        if tl.kind == "p":
            for s in range(tl.nsub):
                r0 = row_base + tl.pos0 + s * 128
                self.dma("sp", dst_fn(s), src_t[r0:r0 + 128, :], writes=writes)
        else:
            for b in range(NSEQ):
                self.dma("sp", dst_fn(0)[32 * b:32 * b + DS, :], src_t[row_base + b * DS: row_base + (b + 1) * DS, :],
                         writes=writes)

    def rows_dma_out(self, tl, s, src_ap, dst_t, c0, c1, reads, row_base=0):
        if tl.kind == "p":
            r0 = row_base + tl.pos0 + s * 128
            self.dma("sp", dst_t[r0:r0 + 128, c0:c1], src_ap, reads=reads)
        else:
            for b in range(NSEQ):
                self.dma("sp", dst_t[row_base + b * DS: row_base + (b + 1) * DS, c0:c1],
                         src_ap[32 * b:32 * b + DS, :], reads=reads)

    def proj_fm(self, W, nchunks, tl, src, skey, KC, sink):
        per_blk = W["cb"] // 128
        for c in range(nchunks):
            blk, cc = divmod(c, per_blk)
            if cc == 0:
                wv, wk = self.wnext(W, blk)
            ps, pk = self.nextA()
            for kc in range(KC):
                self.mm(ps[:, 0:tl.NT], wv(kc, cc * 128, cc * 128 + 128), src[:, kc, 0:tl.NT],
                        kc == 0, kc == KC - 1, [wk] + [(skey, kc)], [pk])
            sink(c, ps, pk)

    def proj_tm(self, W, tl, src, skey, KC, sink, keyfn=None):
        for blk in range(W["nblk"]):
            wv, wk = self.wnext(W, blk)
            w = min(W["cb"], W["M"] - blk * W["cb"])
            for s in range(tl.nsub):
                ps, pk = self.nextA()
                for kc in range(KC):
                    self.mm(ps[:, 0:w], src[:, kc, s * 128:(s + 1) * 128], wv(kc, 0, w),
                            kc == 0, kc == KC - 1, [wk, keyfn(kc) if keyfn else (skey, kc)], [pk])
                sink(blk, s, ps, pk, w)

    def out_proj_residual(self, W, tl, src, skey, KC, keyfn=None):
        def sink(blk, s, ps, pk, w):
            self.tt(self.x_sb[:, s, blk * 512: blk * 512 + w], ps[:, 0:w], self.x_sb[:, s, blk * 512: blk * 512 + w],
                    ALU.add, [pk, ("x", s)], [("x", s)])
        self.proj_tm(W, tl, src, skey, KC, sink, keyfn=keyfn)

    def next_acc_pair(self, allow_psA):
        if allow_psA:
            self.rrAcc ^= 1
        else:
            self.rrAcc = 0
        if self.rrAcc == 0:
            return [(self.psO[0], ("psO", 0)), (self.psO[1], ("psO", 1))]
        return [(self.psA[0], ("psA", 0)), (self.psA[1], ("psA", 1))]

    def next_kv(self):
        self.rrKV = (self.rrKV + 1) % len(self.KC)
        return self.rrKV

    def normalize_head(self, psO, ok, r, c, q0, q1, n, gate):
        self.recip(self.rd[64:65, 0:n], psO[64:65, 0:n], [ok], ["rd"])
        self.mm(self.psM[0:64, 0:n], self.ones_f[64:65, 0:64], self.rd[64:65, 0:n], True, True, ["rd", "cst"], ["psM"])
        if gate:
            self.tt(self.bcg[0:64, 0:n], self.psM[0:64, 0:n], self.gT[r * 64:(r + 1) * 64, c, q0:q1], ALU.mult,
                    ["psM", ("gT", c)], ["bcg"])
        else:
            self.cp(self.bcg[0:64, 0:n], self.psM[0:64, 0:n], ["psM"], ["bcg"], eng="act")
        self.tt(self.oT[r * 64:(r + 1) * 64, c, q0:q1], psO[0:64, 0:n], self.bcg[0:64, 0:n], ALU.mult,
                [ok, "bcg"], [self.ko(c)])

    PIPE_DEPTH = 2

    def pipe_push(self, pv):
        self.pend.append(pv)
        while len(self.pend) > self.PIPE_DEPTH:
            self._pipe_emit()

    def _pipe_emit(self):
        pv = self.pend.pop(0)
        pv()
        ready = [fn for tok, fn in self.deferred if tok is pv]
        self.deferred = [(tok, fn) for tok, fn in self.deferred if tok is not pv]
        for fn in ready:
            fn()

    def pipe_defer(self, fn):
        if self.pend:
            self.deferred.append((self.pend[-1], fn))
        else:
            fn()

    def pipe_flush(self):
        while self.pend:
            self._pipe_emit()
        assert not self.deferred

    def nextS3(self):
        self.rrS3 = (self.rrS3 + 1) % 3
        if self.rrS3 == 2:
            return self.psT32, ("psT", 0)
        return self.psS[self.rrS3], ("psS", self.rrS3)

    def prep_qmask(self, c, q0, q1, mla):
        if mla:
            base = 0
        else:
            self.rrQM ^= 1
            base = 2 * self.rrQM
        self.cur_qm, self.cur_qrm = [], []
        for r in range(2):
            buf, key = self.qm[base + r], ("qm", base + r)
            ceng = "dve" if mla else "pool"
            self.cp(buf[r * 64:(r + 1) * 64, q0:q1], self.qT[r * 64:(r + 1) * 64, c, q0:q1], [self.kq(c)], [key], eng=ceng)
            self.cur_qm.append((buf, key))
            if mla:
                rbuf, rkey = self.qm[2 + r], ("qm", 2 + r)
                self.cp(rbuf[r * 64:r * 64 + 32, q0:q1], self.qR[r * 64:r * 64 + 32, c, q0:q1], [("gT", c)], [rkey], eng=ceng)
                self.cur_qrm.append((rbuf, rkey))

    def score_pv(self, c, r, kt, KCs, kcol, kkey, VCs, vt, vkey, psO, ok, q0, qn, n0, mla, mask, first, last):
        h = 2 * c + r
        scale = MLA_SCALE if mla else FOX_SCALE
        psS, sk = self.nextS3()
        diag = mask is not None
        qm, qmk = self.cur_qm[r]
        self.mm(psS[:, n0:qn], KCs[:, kcol:kcol + 128], qm[:, q0 + n0:q0 + qn],
                True, (not mla) and (not diag), [kkey, qmk], [sk])
        if mla:
            qrm, qrk = self.cur_qrm[r]
            self.mm(psS[:, n0:qn], self.KR[:, kt * 128:(kt + 1) * 128],
                    qrm[:, q0 + n0:q0 + qn], False, not diag, [("KR", kt // 4), qrk], [sk])
        if diag:
            self.mm(psS[:, n0:n0 + 128], self.ident[:, :], mask, False, True, ["cst"], [sk])
        PT, ptk = self.nextPT()
        if mla:
            self.act(PT[:, n0:qn], psS[:, n0:qn], AF.Exp, [sk], [ptk], scale=scale)
        else:
            self.act(PT[:, n0:qn], psS[:, n0:qn], AF.Exp, [sk, ("bias", kt)], [ptk],
                     bias=self.bias_all[:, kt, h:h + 1], scale=scale)
        self.pipe_push(lambda: self.mm(psO[0:65, n0:qn], VCs[:, vt, r, 0:65], PT[:, n0:qn], first, last,
                                       [vkey, ptk], [ok]))

    def attn_prompt_pair(self, tl, c, mla):
        NT = tl.NT
        nkt = (tl.j + 1) * (TS // 128)
        accs = self.next_acc_pair(True)
        self.prep_qmask(c, 0, NT, mla)
        mask = self.maskM[:, :] if mla else self.maskF[:, :]
        for g in range((nkt + 7) // 8):
            nt = min(8, nkt - g * 8)
            slot = self.next_kv()
            kkey, vkey = ("KC", slot), ("VC", slot)
            jj = [("KTd", c, j2) for j2 in range(g * 2, min(g * 2 + 2, tl.j + 1))]
            vv = [("Vd", c, j2) for j2 in range(g * 2, min(g * 2 + 2, tl.j + 1))]
            self.dma("sp", self.KC[slot][:, 0:nt * 128], self.KTd.ap()[c][:, g * 1024:g * 1024 + nt * 128],
                     reads=jj, writes=[kkey])
            self.dma("sp", self.VC[slot][:, 0:nt, :, :].rearrange("p t h e -> p t (h e)"),
                     self.Vd.ap()[c][:, g * 8:g * 8 + nt, :], reads=vv, writes=[vkey])
            for t in range(nt):
                for r in range(2):
                    psO, ok = accs[r]
                    kt = g * 8 + t
                    d = kt - tl.j * (TS // 128)
                    n0 = 0 if d < 0 else d * 128
                    self.score_pv(c, r, kt, self.KC[slot], t * 128, kkey, self.VC[slot], t, vkey, psO, ok,
                                  0, NT, n0, mla, mask if d >= 0 else None, kt == 0, kt == nkt - 1)

        def norm():
            for r in range(2):
                psO, ok = accs[r]
                self.normalize_head(psO, ok, r, c, 0, NT, NT, gate=not mla)
        self.pipe_defer(norm)

    def attn_sample_pair(self, l, b, c, mla):
        cfg = self.cfg
        lj = l // 2
        o = self.dram
        q0 = 32 * b
        npast = cfg.P // 128
        accs = self.next_acc_pair(not mla)
        self.prep_qmask(c, q0, q0 + DS, mla)
        for g in range(cfg.P // 1024):
            slot = self.next_kv()
            kkey, vkey = ("KC", slot), ("VC", slot)
            if not mla:
                src = o["cache_fox_k"].ap()[lj, b, g * 1024:(g + 1) * 1024, c * 128:(c + 1) * 128].rearrange("(t p) d -> p t d", p=128)
                self.dma("pool", self.kc_bf[:, :, :], src, writes=["kc_bf"])
                for g4 in range(2):
                    pT, pk = self.nextT()
                    for i in range(4):
                        self.tr(pT[:, i * 128:(i + 1) * 128], self.kc_bf[:, g4 * 4 + i, :], ["kc_bf"], [pk])
                    self.evac(self.KC[slot][:, g4 * 512:(g4 + 1) * 512], pT[:, 0:512], [pk], [kkey])
                for r2 in range(2):
                    srcv = o["cache_fox_v"].ap()[lj, b, g * 1024:(g + 1) * 1024, c * 128 + r2 * 64: c * 128 + (r2 + 1) * 64].rearrange(
                        "(t p) d -> p t d", p=128)
                    self.dma("pool", self.VC[slot][:, :, r2, 0:64], srcv, writes=[vkey])
            else:
                for half in range(2):
                    ps, pk = self.nextA()
                    for kc in range(2):
                        self.mm(ps[:, 0:512], self.wkvk[:, kc, c * 128:(c + 1) * 128],
                                self.ckvTs[:, kc, g * 1024 + half * 512: g * 1024 + (half + 1) * 512],
                                kc == 0, kc == 1, ["wkvk", ("hT", kc)], [pk])
                    self.evac(self.KC[slot][:, half * 512:(half + 1) * 512], ps[:, 0:512], [pk], [kkey])
                for t4 in range(2):
                    ps, pk = self.nextA()
                    for i in range(4):
                        t = t4 * 4 + i
                        for kc in range(2):
                            self.mm(ps[:, i * 128:(i + 1) * 128], self.ckvTs[:, kc, (g * 8 + t) * 128:(g * 8 + t + 1) * 128],
                                    self.wkvv[:, kc, c * 128:(c + 1) * 128], kc == 0, kc == 1, ["wkvv", ("hT", kc)], [pk])
                    self.evac(self.VC[slot][:, t4 * 4:(t4 + 1) * 4, :, 0:64],
                              ps[:, 0:512].rearrange("p (t h d) -> p t h d", t=4, d=64), [pk], [vkey])
            for t in range(8):
                for r in range(2):
                    psO, ok = accs[r]
                    kt = g * 8 + t
                    self.score_pv(c, r, kt, self.KC[slot], t * 128, kkey, self.VC[slot], t, vkey, psO, ok,
                                  q0, DS, 0, mla, None, kt == 0, False)
        for r in range(2):
            h = 2 * c + r
            psO, ok = accs[r]
            scale = MLA_SCALE if mla else FOX_SCALE
            psS, sk = self.nextS3()
            PT, ptk = self.nextPT()
            qm, qmk = self.cur_qm[r]
            self.mm(psS[0:DS, 0:DS], self.kT[:, c, q0:q0 + DS], qm[:, q0:q0 + DS],
                    True, False, [self.kk(c), qmk], [sk])
            if mla:
                qrm, qrk = self.cur_qrm[r]
                self.mm(psS[0:DS, 0:DS], self.krT[:, q0:q0 + DS], qrm[:, q0:q0 + DS],
                        False, True, ["krT", qrk], [sk])
                self.act(PT[0:DS, 0:DS], psS[0:DS, 0:DS], AF.Exp, [sk], [ptk], scale=scale)
            else:
                self.mm(psS[0:DS, 0:DS], self.ident[0:DS, 0:DS], self.maskF[0:DS, 0:DS], False, True, ["cst"], [sk])
                self.act(PT[0:DS, 0:DS], psS[0:DS, 0:DS], AF.Exp, [sk, "bias_new"], [ptk],
                         bias=self.bias_new[0:DS, h:h + 1], scale=scale)
            self.pipe_push(lambda psO=psO, ok=ok, h=h, PT=PT, ptk=ptk: self.mm(
                psO[0:65, 0:DS], self.vnew0[0:DS, h, 0:65], PT[0:DS, 0:DS], False, True, ["vnew0", ptk], [ok]))

        def norm():
            for r in range(2):
                psO, ok = accs[r]
                self.normalize_head(psO, ok, r, c, q0, q0 + DS, DS, gate=not mla)
        self.pipe_defer(norm)
        if mla:
            self.pipe_flush()

    def cumsum_tile(self, lf_ap, lf_key, dst_ap, dst_key):
        self.mm(self.psM[:, 0:16], self.Umat[:, :], lf_ap, True, False, [lf_key, "cst"], ["psM"])
        self.mm(self.psM[:, 0:16], self.ones_f[:, :], self.Sprev[:, :], False, True, ["Sprev", "cst"], ["psM"])
        self.cp(dst_ap, self.psM[:, 0:16], ["psM"], [dst_key])
        self.tt(self.Sprev[:, :], self.Sprev[:, :], lf_ap, ALU.add, ["Sprev", lf_key], ["Sprev"])

    def carry_bcast(self, dst_ap, dst_key):
        self.mm(self.psM[:, 0:16], self.ones_f[:, :], self.Sprev[:, :], True, True, ["Sprev", "cst"], ["psM"])
        self.cp(dst_ap, self.psM[:, 0:16], ["psM"], [dst_key])

    def store_v_tile(self, tl):
        for c in range(8):
            self.dma("sp", self.Vd.ap()[c][:, tl.j * 4:(tl.j + 1) * 4, :],
                     self.vt_bf[:, :, 2 * c:2 * c + 2, :].rearrange("p s h e -> p s (h e)"),
                     reads=[("vt_bf", s) for s in range(4)], writes=[("Vd", c, tl.j)])

    def fox_mixer(self, l, tl):
        cfg = self.cfg
        lj = l // 2
        W = self.W[l]
        NT, nsub = tl.NT, tl.nsub
        o = self.dram
        isp = tl.kind == "p"
        rb = lj * cfg.T if isp else lj * NSEQ * DS
        self.norm_hT(tl, "g_mix", l)
        self.proj_fm(W["q"], 8, tl, self.hT, "hT", 8,
                     lambda c, ps, pk: self.evac(self.qT[:, c, 0:NT], ps[:, 0:NT], [pk], [self.kq(c)]))

        self.phase("fox k")
        def k_sink(blk, s, ps, pk, w):
            st, sk = self.nextStage()
            self.cp(st[:, 0:w], ps[:, 0:w], [pk], [sk], eng="act")
            self.cp(self.h_bf[:, s, blk * 512: blk * 512 + w], st[:, 0:w], [sk], [("h_bf", s)])
            self.rows_dma_out(tl, s, st[:, 0:w], o["fk_p" if isp else "fk_s"].ap(), blk * 512, blk * 512 + w, [sk], row_base=rb)
        self.proj_tm(W["k"], tl, self.hT, "hT", 8, k_sink)
        self.transpose_to(self.h_bf, [("h_bf", s) for s in range(nsub)], self.kT, None, 8, tl, keyfn=self.kk)
        if isp:
            for c in range(8):
                self.dma("sp", self.KTd.ap()[c][:, tl.pos0:tl.pos0 + NT], self.kT[:, c, 0:NT],
                         reads=[self.kk(c)], writes=[("KTd", c, tl.j)])

        self.phase("fox v")
        def v_sink(blk, s, ps, pk, w):
            st, sk = self.nextStage()
            self.cp(st[:, 0:w], ps[:, 0:w], [pk], [sk], eng="act")
            if isp:
                self.cp(self.vt_bf[:, s, blk * 8:(blk + 1) * 8, 0:64], st[:, 0:w].rearrange("p (h d) -> p h d", d=64),
                        [sk], [("vt_bf", s)])
            else:
                self.cp(self.vnew[:, blk * 8:(blk + 1) * 8, 0:64], st[:, 0:w].rearrange("p (h d) -> p h d", d=64),
                        [sk], ["vnew"])
            self.rows_dma_out(tl, s, st[:, 0:w], o["fv_p" if isp else "fv_s"].ap(), blk * 512, blk * 512 + w, [sk], row_base=rb)
        self.proj_tm(W["v"], tl, self.hT, "hT", 8, v_sink)
        if isp:
            self.store_v_tile(tl)

        self.phase("fox f")
        bk = "bfb"
        self.dma("sp", self.bfb[:, :], self.bcast_row("b_fox_f", lj * NH, NH), writes=[bk])
        wv, wk = self.wnext(W["f"], 0)
        for s in range(nsub):
            for kc in range(8):
                self.mm(self.psM[:, 0:16], self.hT[:, kc, s * 128:(s + 1) * 128], wv(kc, 0, 16), kc == 0, kc == 7,
                        [wk, ("hT", kc)], ["psM"])
            self.tt(self.lf[:, s, :], self.psM[:, 0:16], self.bfb[:, :], ALU.add, ["psM", bk], [("lf", s)])
            self.act(self.lf[:, s, :], self.lf[:, s, :], AF.Exp, [("lf", s)], [("lf", s)], scale=-1.0)
            self.act(self.lf[:, s, :], self.lf[:, s, :], AF.Ln, [("lf", s), "one_col"], [("lf", s)], bias=self.one_col[:, 0:1], scale=1.0)
            self.ts(self.lf[:, s, :], self.lf[:, s, :], -1.0, None, ALU.mult, None, [("lf", s)], [("lf", s)])
            self.rows_dma_out(tl, s, self.lf[:, s, :], o["fl_p" if isp else "fl_s"].ap(), 0, NH, [("lf", s)], row_base=rb)

        self.phase("fox g")
        self.proj_fm(W["g"], 8, tl, self.hT, "hT", 8,
                     lambda c, ps, pk: self.act(self.gT[:, c, 0:NT], ps[:, 0:NT], AF.Sigmoid, [pk], [("gT", c)]))

        self.phase("fox attn")
        if isp:
            nkt = (tl.j + 1) * 4
            for s in range(nsub):
                kt = tl.j * 4 + s
                if s == 2:
                    self.carry_bcast(self.cref[:, :], "cref")
                self.cumsum_tile(self.lf[:, s, :], ("lf", s), self.c_all[:, kt, :], ("c_all", kt))
            for kt in range(nkt):
                self.tt(self.bias_all[:, kt, :], self.cref[:, :], self.c_all[:, kt, :], ALU.subtract,
                        ["cref", ("c_all", kt)], [("bias", kt)])
            for c in range(8):
                self.attn_prompt_pair(tl, c, False)
        else:
            npast = cfg.P // 128
            for c in range(8):
                self.memset(self.oT[:, c, 0:128], 0.0, [self.ko(c)])
            for b in range(NSEQ):
                self.dma("sp", self.lfs[:, 0:npast, :],
                         o["cache_fox_logf"].ap()[lj, b].rearrange("(t p) h -> p t h", p=128), writes=["lfs"])
                self.memset(self.Sprev[:, :], 0.0, ["Sprev"])
                for kt in range(npast):
                    self.cumsum_tile(self.lfs[:, kt, :], "lfs", self.c_all[:, kt, :], ("c_all", kt))
                self.carry_bcast(self.cref[:, :], "cref")
                for kt in range(npast):
                    self.tt(self.bias_all[:, kt, :], self.cref[:, :], self.c_all[:, kt, :], ALU.subtract,
                            ["cref", ("c_all", kt)], [("bias", kt)])
                self.phase(f"fox s-attn b{b} newbias")
                self.cp(self.lfn0[0:DS, :], self.lf[32 * b:32 * b + DS, 0, :], [("lf", 0)], ["lfn0"])
                self.mm(self.psM[0:DS, 0:16], self.Umat[0:DS, 0:DS], self.lfn0[0:DS, :], True, True, ["lfn0", "cst"], ["psM"])
                self.ts(self.bias_new[0:DS, :], self.psM[0:DS, 0:16], -1.0, None, ALU.mult, None, ["psM"], ["bias_new"])
                self.cp(self.vnew0[0:DS, :, :], self.vnew[32 * b:32 * b + DS, :, :], ["vnew"], ["vnew0"])
                self.phase(f"fox s-attn b{b} pairs")
                for c in range(8):
                    self.attn_sample_pair(l, b, c, False)
                self.pipe_flush()
        self.pipe_flush()
        self.phase("fox out")
        self.out_proj_residual(W["o"], tl, self.oT, None, 8, keyfn=self.ko)

    def mla_mixer(self, l, tl):
        cfg = self.cfg
        lj = l // 2
        W = self.W[l]
        NT, nsub = tl.NT, tl.nsub
        o = self.dram
        isp = tl.kind == "p"
        rb = lj * cfg.T if isp else lj * NSEQ * DS
        self.norm_hT(tl, "g_mix", l)
        if not self.kv_w_loaded.get(l):
            self.kv_w_loaded[l] = True
            self.dma("sp", self.wkvk[:, :, :].rearrange("p k c -> p (k c)"), W["kvk"]["scr"].ap()[0],
                     reads=[("wb", W["kvk"]["name"], 0)], writes=["wkvk"])
            self.dma("sp", self.wkvv[:, :, :].rearrange("p k c -> p (k c)"), W["kvv"]["scr"].ap()[0],
                     reads=[("wb", W["kvv"]["name"], 0)], writes=["wkvv"])

        def a_sink(blk, s, ps, pk, w):
            self.evac(self.a_sb[:, s, blk * 512: blk * 512 + w], ps[:, 0:w], [pk], [("a_sb", s, blk)])
        self.proj_tm(W["a"], tl, self.hT, "hT", 8, a_sink)
        akeys = [("a_sb", s, blk) for s in range(nsub) for blk in range(2)]
        self.dma("sp", self.gq[:, :], self.bcast_row("g_mla_q", lj * MLA_QL, MLA_QL), writes=["gq"])
        self.dma("sp", self.gkv[:, :], self.bcast_row("g_mla_kv", lj * MLA_KVL, MLA_KVL), writes=["gkv"])
        self.rstd_of(lambda s: self.a_sb[:, s, 0:MLA_QL], nsub, MLA_QL, "cq", akeys)
        for s in range(nsub):
            self.stt(self.cq_bf[:, s, :], self.a_sb[:, s, 0:MLA_QL], self.rstd[:, s:s + 1], self.gq[:, :],
                     ALU.mult, ALU.mult, akeys + ["rstd", "gq"], [("cq_bf", s)])
        self.transpose_to(self.cq_bf, [("cq_bf", s) for s in range(nsub)], self.cqT, "cqT", 3, tl)
        self.rstd_of(lambda s: self.a_sb[:, s, MLA_QL:MLA_QL + MLA_KVL], nsub, MLA_KVL, "ckv", akeys)
        for s in range(nsub):
            st, sk = self.nextStage()
            self.stt(st[:, 0:MLA_KVL], self.a_sb[:, s, MLA_QL:MLA_QL + MLA_KVL], self.rstd[:, s:s + 1], self.gkv[:, :],
                     ALU.mult, ALU.mult, akeys + ["rstd", "gkv"], [sk])
            self.cp(self.ckv_bf[:, s, :], st[:, 0:MLA_KVL], [sk], [("ckv_bf", s)], eng="act")
            self.rows_dma_out(tl, s, st[:, 0:MLA_KVL], o["mc_p" if isp else "mc_s"].ap(), 0, MLA_KVL, [sk], row_base=rb)
        self.transpose_to(self.ckv_bf, [("ckv_bf", s) for s in range(nsub)], self.ckvT, "ckvT", 2, tl)
        if isp:
            self.dma("sp", self.rtm[:, 0:nsub, :],
                     o["rope_tm_p"].ap()[tl.pos0:tl.pos0 + NT, :].rearrange("(s p) c -> p s c", p=128), writes=["rtm"])
        else:
            self.dma("sp", self.rtm[:, 0, :], o["rope_tm_s"].ap(), writes=["rtm"])
        A0 = MLA_QL + MLA_KVL
        for s in range(nsub):
            x1 = self.a_sb[:, s, A0:A0 + 16]
            x2 = self.a_sb[:, s, A0 + 16:A0 + 32]
            cs = self.rtm[:, s, 0:16]
            sn = self.rtm[:, s, 16:32]
            st, sk = self.nextStage()
            rk = akeys + ["rtm"]
            self.tt(st[:, 0:16], x1, cs, ALU.mult, rk, [sk])
            self.tt(st[:, 32:48], x2, sn, ALU.mult, rk, [sk])
            self.tt(st[:, 0:16], st[:, 0:16], st[:, 32:48], ALU.subtract, [sk], [sk])
            self.tt(st[:, 16:32], x1, sn, ALU.mult, rk, [sk])
            self.tt(st[:, 32:48], x2, cs, ALU.mult, rk, [sk])
            self.tt(st[:, 16:32], st[:, 16:32], st[:, 32:48], ALU.add, [sk], [sk])
            self.cp(self.kr_bf[:, s, :], st[:, 0:32], [sk], [("kr_bf", s)], eng="act")
            self.rows_dma_out(tl, s, st[:, 0:32], o["mr_p" if isp else "mr_s"].ap(), 0, MLA_R, [sk], row_base=rb)
        pT, pk = self.nextT()
        for s in range(nsub):
            self.tr(pT[0:32, s * 128:(s + 1) * 128], self.kr_bf[:, s, :], [("kr_bf", s)], [pk])
        for r in range(2):
            if isp:
                self.evac(self.KR[r * 64:r * 64 + 32, tl.pos0:tl.pos0 + NT], pT[0:32, 0:NT], [pk], [("KR", tl.j)])
            else:
                self.evac(self.krT[r * 64:r * 64 + 32, 0:NT], pT[0:32, 0:NT], [pk], ["krT"])

        if isp:
            for i in range(2):
                self.dma("sp", self.rfm[64:96, i, 0:NT], o["rope_fm_p"].ap()[i, :, tl.pos0:tl.pos0 + NT], writes=["rfm"])
        else:
            for i in range(2):
                self.dma("sp", self.rfm[64:96, i, 0:NT], o["rope_fm_s"].ap()[i], writes=["rfm"])
        Wq = W["qb"]
        for h in range(NH):
            blk, hh = divmod(h, 4)
            if hh == 0:
                wv, wk = self.wnext(Wq, blk)
            c, r = divmod(h, 2)
            ps, pk = self.nextA()
            for kc in range(3):
                self.mm(ps[0:96, 0:NT], wv(kc, hh * 96, hh * 96 + 96), self.cqT[:, kc, 0:NT], kc == 0, kc == 2,
                        [wk, ("cqT", kc)], [pk])
            self.cp(self.qT[r * 64:(r + 1) * 64, c, 0:NT], ps[0:64, 0:NT], [pk], [self.kq(c)], eng="act")
            self.cp(self.xr[64:96, 0:NT], ps[64:96, 0:NT], [pk], ["xr"], eng="act")
            self.mm(self.psM[64:96, 0:NT], self.Rrot[64:96, 0:32], self.xr[64:96, 0:NT], True, True, ["xr", "cst"], ["psM"])
            self.tt(self.t1[64:96, 0:NT], ps[64:96, 0:NT], self.rfm[64:96, 0, 0:NT], ALU.mult, [pk, "rfm"], ["t1"])
            self.tt(self.t2[64:96, 0:NT], self.psM[64:96, 0:NT], self.rfm[64:96, 1, 0:NT], ALU.mult, ["psM", "rfm"], ["t2"])
            self.tt(self.qR[r * 64:r * 64 + 32, c, 0:NT], self.t1[64:96, 0:NT], self.t2[64:96, 0:NT], ALU.add,
                    ["t1", "t2"], [("gT", c)])

        for c in range(8):
            ps, pk = self.nextA()
            for kc in range(2):
                self.mm(ps[:, 0:NT], self.wkvk[:, kc, c * 128:(c + 1) * 128], self.ckvT[:, kc, 0:NT], kc == 0, kc == 1,
                        ["wkvk", ("ckvT", kc)], [pk])
            self.evac(self.kT[:, c, 0:NT], ps[:, 0:NT], [pk], [self.kk(c)])
            if isp:
                self.dma("sp", self.KTd.ap()[c][:, tl.pos0:tl.pos0 + NT], self.kT[:, c, 0:NT],
                         reads=[self.kk(c)], writes=[("KTd", c, tl.j)])
        for s in range(nsub):
            for half in range(2):
                ps, pk = self.nextA()
                for kc in range(2):
                    self.mm(ps[:, 0:512], self.ckvT[:, kc, s * 128:(s + 1) * 128], self.wkvv[:, kc, half * 512:(half + 1) * 512],
                            kc == 0, kc == 1, ["wkvv", ("ckvT", kc)], [pk])
                if isp:
                    self.evac(self.vt_bf[:, s, half * 8:(half + 1) * 8, 0:64],
                              ps[:, 0:512].rearrange("p (h d) -> p h d", d=64), [pk], [("vt_bf", s)])
                else:
                    self.evac(self.vnew[:, half * 8:(half + 1) * 8, 0:64],
                              ps[:, 0:512].rearrange("p (h d) -> p h d", d=64), [pk], ["vnew"])
        self.phase("mla attn")
        if isp:
            self.store_v_tile(tl)
            for c in range(8):
                self.attn_prompt_pair(tl, c, True)
        else:
            npast = cfg.P // 128
            for c in range(8):
                self.memset(self.oT[:, c, 0:128], 0.0, [self.ko(c)])
            for b in range(NSEQ):
                self.dma("pool", self.ckvc[:, 0:npast, :],
                         o["cache_mla_ckv"].ap()[lj, b].rearrange("(t p) d -> p t d", p=128),
                         writes=[("h_bf", s) for s in range(4)])
                for kc in range(2):
                    for g in range(0, npast, 4):
                        pT, pk = self.nextT()
                        for i in range(4):
                            self.tr(pT[:, i * 128:(i + 1) * 128], self.ckvc[:, g + i, kc * 128:(kc + 1) * 128],
                                    [("h_bf", s) for s in range(4)], [pk])
                        self.evac(self.ckvTs[:, kc, g * 128:(g + 4) * 128], pT[:, 0:512], [pk], [("hT", kk2) for kk2 in range(8)])
                self.dma("pool", self.krc[:, 0:npast, :],
                         o["cache_mla_krope"].ap()[lj, b].rearrange("(t p) d -> p t d", p=128), writes=["krc"])
                for g in range(0, npast, 4):
                    pT, pk = self.nextT()
                    for i in range(4):
                        self.tr(pT[0:32, i * 128:(i + 1) * 128], self.krc[:, g + i, :], ["krc"], [pk])
                    for r in range(2):
                        self.evac(self.KR[r * 64:r * 64 + 32, g * 128:(g + 4) * 128], pT[0:32, 0:512], [pk], [("KR", g // 4)])
                self.cp(self.vnew0[0:DS, :, :], self.vnew[32 * b:32 * b + DS, :, :], ["vnew"], ["vnew0"])
                for c in range(8):
                    self.attn_sample_pair(l, b, c, True)
                self.pipe_flush()
        self.pipe_flush()
        self.phase("mla out")
        self.out_proj_residual(W["o"], tl, self.oT, None, 8, keyfn=self.ko)

    def load_mem_kv(self, ksrc, vsrc, reads):
        self.dma("pool", self.h_bf[:, 0:2, :], ksrc.rearrange("(t p) d -> p t d", p=128), reads=reads,
                 writes=[("h_bf", 0), ("h_bf", 1)])
        self.dma("pool", self.MV[:, :, :], vsrc.rearrange("(t p) d -> p t d", p=128), reads=reads, writes=["MV"])
        for g in range(0, 8, 2):
            pT, pk = self.nextT()
            for i in range(2):
                for t in range(2):
                    self.tr(pT[:, i * 256 + t * 128: i * 256 + (t + 1) * 128], self.h_bf[:, t, (g + i) * 128:(g + i + 1) * 128],
                            [("h_bf", 0), ("h_bf", 1)], [pk])
            self.evac(self.MKT[:, g:g + 2, :], pT[:, 0:512].rearrange("p (i k) -> p i k", i=2), [pk], ["MKT"])

    def cross_heads(self, n0, n1):
        for hx in range(4):
            pts = []
            for mkt in range(2):
                psS, sk = self.nextS()
                for dc in range(2):
                    self.mm(psS[:, n0:n1], self.MKT[:, hx * 2 + dc, mkt * 128:(mkt + 1) * 128], self.qT[:, hx * 2 + dc, n0:n1],
                            dc == 0, dc == 1, ["MKT", self.kq(hx * 2 + dc)], [sk])
                PT, ptk = self.nextPT()
                self.act(PT[:, n0:n1], psS[:, n0:n1], AF.Exp, [sk], [ptk], scale=X_SCALE)
                pts.append((PT, ptk))
            for mkt in range(2):
                self.mm(self.psM[:, n0:n1], self.ones_b[:, :], pts[mkt][0][:, n0:n1], mkt == 0, mkt == 1,
                        [pts[mkt][1], "cst"], ["psM"])
            self.recip(self.rden[:, n0:n1], self.psM[:, n0:n1], ["psM"], ["bcg"])
            for dc in range(2):
                psO, ok = self.nextO()
                for mkt in range(2):
                    self.mm(psO[:, n0:n1], self.MV[:, mkt, hx * 256 + dc * 128: hx * 256 + (dc + 1) * 128],
                            pts[mkt][0][:, n0:n1], mkt == 0, mkt == 1, ["MV", pts[mkt][1]], [ok])
                self.tt(self.oT[:, hx * 2 + dc, n0:n1], psO[:, n0:n1], self.rden[:, n0:n1], ALU.mult,
                        [ok, "bcg"], [self.ko(hx * 2 + dc)])

    def cross_attn(self, l, tl):
        o = self.dram
        W = self.W[l]
        NT = tl.NT
        self.norm_hT(tl, "g_cross", l)
        self.proj_fm(W["xq"], 8, tl, self.hT, "hT", 8,
                     lambda c, ps, pk: self.evac(self.qT[:, c, 0:NT], ps[:, 0:NT], [pk], [self.kq(c)]))
        if tl.kind == "p":
            if tl.j == 0:
                self.load_mem_kv(o["memk_p"].ap()[l], o["memv_p"].ap()[l], [("memkv_out", l)])
            self.cross_heads(0, NT)
        else:
            for c in range(8):
                self.memset(self.oT[:, c, 0:128], 0.0, [self.ko(c)])
            for b in range(NSEQ):
                self.load_mem_kv(o["cache_mem_k"].ap()[l, b], o["cache_mem_v"].ap()[l, b], [])
                self.cross_heads(32 * b, 32 * b + DS)
        self.out_proj_residual(W["xo"], tl, self.oT, None, 8, keyfn=self.ko)

    def ffn(self, l, tl):
        W = self.W[l]
        NT = tl.NT
        self.norm_hT(tl, "g_ffn", l)
        NFC = DFF // 128
        for grp in range(0, NFC, 4):
            n = min(4, NFC - grp)
            gv, gk = self.wnext(W["gu_g"], grp // 4)
            uv, uk = self.wnext(W["gu_u"], grp // 4)
            for i in range(n):
                f = grp + i
                psg, pgk = self.nextA()
                for kc in range(8):
                    self.mm(psg[:, 0:NT], gv(kc, i * 128, i * 128 + 128), self.hT[:, kc, 0:NT], kc == 0, kc == 7,
                            [gk, ("hT", kc)], [pgk])
                psu, puk = self.nextS()
                for kc in range(8):
                    self.mm(psu[:, 0:NT], uv(kc, i * 128, i * 128 + 128), self.hT[:, kc, 0:NT], kc == 0, kc == 7,
                            [uk, ("hT", kc)], [puk])
                sg, sgk = self.nextStage()
                self.act(sg[:, 0:NT], psg[:, 0:NT], AF.Silu, [pgk], [sgk])
                self.tt(self.aT[:, f, 0:NT], psu[:, 0:NT], sg[:, 0:NT], ALU.mult, [puk, sgk], [self.ka(f)])
        Wd = W["down"]
        for half in range(2):
            accs = [(self.psA[0], ("psA", 0)), (self.psA[1], ("psA", 1)), (self.psS[0], ("psS", 0)), (self.psS[1], ("psS", 1))]
            for fb in range(Wd["nblk"] // 2):
                dv, dk = self.wnext(Wd, half * (Wd["nblk"] // 2) + fb)
                nf = min(4, NFC - fb * 4)
                for s in range(tl.nsub):
                    ps, pk = accs[s]
                    for i in range(nf):
                        f = fb * 4 + i
                        self.mm(ps[:, 0:512], self.aT[:, f, s * 128:(s + 1) * 128], dv(i, 0, 512), f == 0, f == NFC - 1,
                                [dk, self.ka(f)], [pk])
            for s in range(tl.nsub):
                ps, pk = accs[s]
                self.tt(self.x_sb[:, s, half * 512:(half + 1) * 512], ps[:, 0:512], self.x_sb[:, s, half * 512:(half + 1) * 512],
                        ALU.add, [pk, ("x", s)], [("x", s)])

    def memory_kv(self):
        cfg = self.cfg
        o = self.dram
        tl = Tile("p", 0)
        tl.nsub, tl.NT = 2, 256
        for s in range(2):
            self.dma("sp", self.x_sb[:, s, :], o["mem_prompt"].ap()[s * 128:(s + 1) * 128, :], writes=[("x", s)])
        self.rstd_of(lambda s: self.x_sb[:, s, :], 2, D, "x", lambda s: [("x", s)])
        for l in range(cfg.L):
            gk = ("gbc", "g_mem")
            self.dma("sp", self.gbc[:, :], self.bcast_row("g_mem", l * D, D), writes=[gk])
            for s in range(2):
                self.stt(self.h_bf[:, s, :], self.x_sb[:, s, :], self.rstd[:, s:s + 1], self.gbc[:, :],
                         ALU.mult, ALU.mult, [("x", s), "rstd", gk], [("h_bf", s)])
            self.transpose_to(self.h_bf, [("h_bf", s) for s in range(2)], self.hT, "hT", 8, tl)
            for blk in range(4):
                slot = self.rrW
                self.rrW = (self.rrW + 1) % len(self.wbuf)
                src = o["w_x_kv"].ap()[l][:, blk * 512:(blk + 1) * 512].rearrange("(k p) c -> p k c", p=128)
                self.dma("pool", self.wbuf[slot][:, 0:4096].rearrange("p (k c) -> p k c", k=8), src, writes=[("w", slot)])
                for s in range(2):
                    ps, pk = self.nextA()
                    for kc in range(8):
                        self.mm(ps[:, 0:512], self.hT[:, kc, s * 128:(s + 1) * 128],
                                self.wbuf[slot][:, kc * 512:(kc + 1) * 512], kc == 0, kc == 7, [("w", slot), ("hT", kc)], [pk])
                    st, sk = self.nextStage()
                    self.evac(st[:, 0:512], ps[:, 0:512], [pk], [sk])
                    dst = o["memk_p"] if blk < 2 else o["memv_p"]
                    cb = (blk % 2) * 512
                    self.dma("sp", dst.ap()[l][s * 128:(s + 1) * 128, cb:cb + 512], st[:, 0:512], reads=[sk],
                             writes=[("memkv_out", l)])

    def weight_plan(self, l):
        W = self.W[l]
        p = []
        if l % 2 == 0:
            p += [(W["q"], 0), (W["q"], 1), (W["k"], 0), (W["k"], 1), (W["v"], 0), (W["v"], 1), (W["f"], 0),
                  (W["g"], 0), (W["g"], 1), (W["o"], 0), (W["o"], 1)]
        else:
            p += [(W["a"], 0), (W["a"], 1)] + [(W["qb"], i) for i in range(4)] + [(W["o"], 0), (W["o"], 1)]
        if "cross" not in self.cfg.skip:
            p += [(W["xq"], 0), (W["xq"], 1), (W["xo"], 0), (W["xo"], 1)]
        if "ffn" not in self.cfg.skip:
            for g in range(6):
                p += [(W["gu_g"], g), (W["gu_u"], g)]
            nb = W["down"]["nblk"]
            p += [(W["down"], i) for i in range(nb)]
        return p

    def convert_layer(self, l):
        cw = self.conv_weight
        W = {}
        lj = l // 2
        if l % 2 == 0:
            W["q"] = cw(f"wq{l}", "w_fox_in", lj, D, 0, 1024, 512, FOX_IN)
            W["k"] = cw(f"wk{l}", "w_fox_in", lj, D, 1024, 2048, 512, FOX_IN)
            W["v"] = cw(f"wv{l}", "w_fox_in", lj, D, 2048, 3072, 512, FOX_IN)
            W["f"] = cw(f"wf{l}", "w_fox_in", lj, D, 3072, 3088, 16, FOX_IN)
            W["g"] = cw(f"wg{l}", "w_fox_in", lj, D, 3088, 4112, 512, FOX_IN)
            W["o"] = cw(f"wo{l}", "w_fox_out", lj, D, 0, 1024, 512, D)
        else:
            W["a"] = cw(f"wa{l}", "w_mla_a", lj, D, 0, MLA_DOWN, 512, MLA_DOWN)
            W["qb"] = cw(f"wqb{l}", "w_mla_qb", lj, MLA_QL, 0, 1536, 384, 1536)
            for nm, off in (("kvk", 0), ("kvv", 64)):
                scr = self.dscr(f"w{nm}{l}", [1, 128, 2 * 1024], BF16)
                srcw = self.dram["w_mla_kvb"].ap()[lj].rearrange("(k p) (h e) -> p k h e", p=128, e=128)
                dstw = scr.ap()[0].rearrange("p (k h d) -> p k h d", k=2, d=64)
                for kc in range(2):
                    self.dma("pool", dstw[:, kc, :, :], srcw[:, kc, :, off:off + 64], writes=[("wb", f"w{nm}{l}", 0)])
                W[nm] = dict(name=f"w{nm}{l}", KC=2, cb=1024, nblk=1, M=1024, scr=scr)
            W["o"] = cw(f"wo{l}", "w_mla_out", lj, D, 0, 1024, 512, D)
        W["xq"] = cw(f"wxq{l}", "w_x_q", l, D, 0, 1024, 512, D)
        W["xo"] = cw(f"wxo{l}", "w_x_o", l, D, 0, 1024, 512, D)
        W["gu_g"] = cw(f"wgg{l}", "w_ffn_gu", l, D, 0, DFF, 512, 2 * DFF)
        W["gu_u"] = cw(f"wgu{l}", "w_ffn_gu", l, D, DFF, 2 * DFF, 512, 2 * DFF)
        name = f"wd{l}"
        nfb = (DFF // 128 + 3) // 4
        scr = self.dscr(name, [2 * nfb, 128, 4 * 512], BF16)
        src = self.dram["w_ffn_down"].ap()[l]
        for half in range(2):
            for fb in range(nfb):
                nf = min(4, DFF // 128 - fb * 4)
                i = src[fb * 512: fb * 512 + nf * 128, half * 512:(half + 1) * 512].rearrange("(k p) c -> p k c", p=128)
                ov = scr.ap()[half * nfb + fb].rearrange("p (k c) -> p k c", k=4)[:, 0:nf, :]
                self.dma("pool", ov, i, writes=[("wb", name, half * nfb + fb)])
        W["down"] = dict(name=name, KC=4, cb=512, nblk=2 * nfb, M=1024, scr=scr)
        self.W[l] = W

    def phase(self, name):
        self.nphase = getattr(self, "nphase", 0) + 1
        if self.cfg.stop is not None and self.nphase > self.cfg.stop:
            print("STOP before phase", self.nphase, name)
            raise StopBuild()
        if self.cfg.stop is not None:
            print("phase", self.nphase, name)

    def layer_tile(self, l, tl):
        cfg = self.cfg
        o = self.dram
        last = l == cfg.L - 1
        self.phase(f"L{l} {tl.kind}{tl.j} load+mixer")
        if tl.kind == "s":
            self.memset(self.x_sb[:, 0, :], 0.0, [("x", 0)])
            src = o["x_sample"] if l == 0 else o["xres_s"]
            self.rows_dma_in(tl, lambda s: self.x_sb[:, s, :], src.ap(), D, writes=[("x", 0)])
        else:
            src = o["x_prompt"] if l == 0 else o["xres_p"]
            rd = [] if l == 0 else [("xres_p", tl.j)]
            for s in range(tl.nsub):
                r0 = tl.pos0 + s * 128
                self.dma("sp", self.x_sb[:, s, :], src.ap()[r0:r0 + 128, :], reads=rd, writes=[("x", s)])
        self.wplan(self.weight_plan(l))
        if l % 2 == 0:
            self.fox_mixer(l, tl)
        else:
            self.mla_mixer(l, tl)
        if "cross" not in cfg.skip:
            self.phase(f"L{l} {tl.kind}{tl.j} cross")
            self.cross_attn(l, tl)
        if "ffn" not in cfg.skip:
            self.phase(f"L{l} {tl.kind}{tl.j} ffn")
            self.ffn(l, tl)
        assert self.plan_i == len(self.plan)
        if not last:
            if tl.kind == "s":
                self.rows_dma_out(tl, 0, self.x_sb[:, 0, :], o["xres_s"].ap(), 0, D, [("x", 0)])
            else:
                for s in range(tl.nsub):
                    r0 = tl.pos0 + s * 128
                    self.dma("sp", o["xres_p"].ap()[r0:r0 + 128, :], self.x_sb[:, s, :], reads=[("x", s)],
                             writes=[("xres_p", tl.j)])
        else:
            gk = ("gbc", "g_final")
            self.dma("sp", self.gbc[:, :], self.bcast_row("g_final", 0, D), writes=[gk])
            self.rstd_of(lambda s: self.x_sb[:, s, :], tl.nsub, D, "x", lambda s: [("x", s)])
            for s in range(tl.nsub):
                for half in range(2):
                    st, sk = self.nextStage()
                    self.stt(st[:, 0:512], self.x_sb[:, s, half * 512:(half + 1) * 512], self.rstd[:, s:s + 1],
                             self.gbc[:, half * 512:(half + 1) * 512], ALU.mult, ALU.mult, [("x", s), "rstd", gk], [sk])
                    self.rows_dma_out(tl, s, st[:, 0:512], o["y_p" if tl.kind == "p" else "y_s"].ap(),
                                      half * 512, (half + 1) * 512, [sk])

    def build(self):
        cfg = self.cfg
        T, P, L, NF, NM = cfg.T, cfg.P, cfg.L, cfg.NF, cfg.NM
        din, dout = self.din, self.dout
        din("x_prompt", [T, D]); din("x_sample", [NSEQ * DS, D]); din("mem_prompt", [NMEM, D])
        din("cache_fox_k", [NF, NSEQ, P, D]); din("cache_fox_v", [NF, NSEQ, P, D]); din("cache_fox_logf", [NF, NSEQ, P, NH])
        din("cache_mla_ckv", [max(NM, 1), NSEQ, P, MLA_KVL]); din("cache_mla_krope", [max(NM, 1), NSEQ, P, MLA_R])
        din("cache_mem_k", [L, NSEQ, NMEM, D]); din("cache_mem_v", [L, NSEQ, NMEM, D])
        din("g_mix", [L * D]); din("g_cross", [L * D]); din("g_mem", [L * D]); din("g_ffn", [L * D]); din("g_final", [D])
        din("w_fox_in", [NF, D, FOX_IN]); din("b_fox_f", [NF * NH]); din("w_fox_out", [NF, D, D])
        din("w_mla_a", [max(NM, 1), D, MLA_DOWN]); din("g_mla_q", [max(NM, 1) * MLA_QL]); din("g_mla_kv", [max(NM, 1) * MLA_KVL])
        din("w_mla_qb", [max(NM, 1), MLA_QL, 1536]); din("w_mla_kvb", [max(NM, 1), MLA_KVL, 2048]); din("w_mla_out", [max(NM, 1), D, D])
        din("w_x_q", [L, D, D]); din("w_x_kv", [L, D, 2 * D]); din("w_x_o", [L, D, D])
        din("w_ffn_gu", [L, D, 2 * DFF]); din("w_ffn_down", [L, DFF, D])
        din("cst_bf", [128, 640], BF16); din("cst_f", [128, 256])
        din("rope_tm_p", [T, 32]); din("rope_tm_s", [128, 32]); din("rope_fm_p", [2, 32, T]); din("rope_fm_s", [2, 32, 128])
        dout("y_p", [T, D]); dout("y_s", [NSEQ * DS, D])
        dout("fk_p", [NF * T, D]); dout("fv_p", [NF * T, D]); dout("fl_p", [NF * T, NH])
        dout("mc_p", [max(NM, 1) * T, MLA_KVL]); dout("mr_p", [max(NM, 1) * T, MLA_R])
        dout("memk_p", [L, NMEM, D]); dout("memv_p", [L, NMEM, D])
        dout("fk_s", [NF * NSEQ * DS, D]); dout("fv_s", [NF * NSEQ * DS, D]); dout("fl_s", [NF * NSEQ * DS, NH])
        dout("mc_s", [max(NM, 1) * NSEQ * DS, MLA_KVL]); dout("mr_s", [max(NM, 1) * NSEQ * DS, MLA_R])
        self.dscr("xres_p", [T, D], F32); self.dscr("xres_s", [NSEQ * DS, D], F32)
        self.KTd = self.dscr("KTd", [8, 128, T], BF16)

        sb, ps = self.sb, self.ps
        NKT = max(T, P) // 128
        self.Vd = self.dscr("Vd", [8, 128, T // 128, 130], BF16)
        self.x_sb = sb("x_sb", [128, 4, D], F32)
        self.h_bf = sb("h_bf", [128, 4, D], BF16)
        self.ckvc = self.h_bf[:, :, :].rearrange("p s d -> p (s d)")[:, 0:(P // 128) * MLA_KVL].rearrange("p (t d) -> p t d", d=MLA_KVL)
        self.hT = sb("hT", [128, 8, TS], BF16)
        self.ckvTs = self.hT[:, :, :].rearrange("p c n -> p (c n)")[:, 0:2 * P].rearrange("p (k n) -> p k n", k=2)
        assert (P // 128) * MLA_KVL <= 4 * D and 2 * P <= 8 * TS
        self.R = sb("R", [128, 24, TS], BF16)
        self.qT = self.R[:, 0:8, :]
        self.kT = self.R[:, 8:16, :]
        self.oT = self.R[:, 16:24, :]
        self.aT = self.R[:, 0:22, :]
        self.gT = sb("gT", [128, 8, TS], BF16)
        self.qR = self.gT
        self.vt_bf = sb("vt_bf", [128, 4, NH, 65], BF16)
        self.KC = [sb(f"KC{i}", [128, 1024], BF16) for i in range(3)]
        self.VC = [sb(f"VC{i}", [128, 8, 2, 65], BF16) for i in range(3)]
        self.KR = sb("KR", [128, max(T, P)], BF16)
        self.wbuf = [sb(f"wbuf{i}", [128, 4096], BF16) for i in range(3)]
        self.wkvk = sb("wkvk", [128, 2, 1024], BF16); self.wkvv = sb("wkvv", [128, 2, 1024], BF16)
        self.stage = [sb(f"stage{i}", [128, 512], F32) for i in range(4)]
        self.PT = [sb(f"PT{i}", [128, 512], BF16) for i in range(4)]
        self.gbc = sb("gbc", [128, D], F32)
        self.junk = sb("junk", [128, D], BF16)
        self.ss = sb("ss", [128, 4], F32); self.ss2 = sb("ss2", [128, 4], F32); self.rstd = sb("rstd", [128, 4], F32)
        self.one_col = sb("one_col", [128, 1], F32)
        self.cst_bf = sb("cst_bf_sb", [128, 640], BF16); self.cst_f = sb("cst_f_sb", [128, 256], F32)
        self.ident = self.cst_bf[:, 0:128]; self.maskF = self.cst_bf[:, 128:256]; self.maskM = self.cst_bf[:, 256:384]
        self.ones_b = self.cst_bf[:, 384:512]; self.Rrot = self.cst_bf[:, 512:544]
        self.Umat = self.cst_f[:, 0:128]; self.ones_f = self.cst_f[:, 128:256]
        self.bcg = sb("bcg", [128, 512], F32); self.rden = self.bcg
        self.rd = self.bcg
        self.qm = [sb(f"qm{i}", [128, TS], BF16) for i in range(4)]
        self.lf = sb("lf", [128, 4, NH], F32); self.bfb = sb("bfb", [128, NH], F32)
        self.c_all = sb("c_all", [128, NKT, NH], F32); self.bias_all = sb("bias_all", [128, NKT, NH], F32)
        self.Sprev = sb("Sprev", [128, NH], F32); self.cref = sb("cref", [128, NH], F32)
        self.lfs = sb("lfs", [128, P // 128, NH], F32); self.lfn0 = sb("lfn0", [DS, NH], F32)
        self.bias_new = sb("bias_new", [DS, NH], F32)
        self.vnew = sb("vnew", [128, NH, 65], BF16); self.vnew0 = sb("vnew0", [DS, NH, 65], BF16)
        self.kc_bf = sb("kc_bf", [128, 8, 128], BF16)
        self.MKT = sb("MKT", [128, 8, NMEM], BF16); self.MV = sb("MV", [128, 2, D], BF16)
        self.a_sb = sb("a_sb", [128, 4, MLA_DOWN], F32)
        self.cq_bf = sb("cq_bf", [128, 4, MLA_QL], BF16); self.cqT = sb("cqT", [128, 3, TS], BF16)
        self.ckv_bf = sb("ckv_bf", [128, 4, MLA_KVL], BF16); self.ckvT = sb("ckvT", [128, 2, TS], BF16)
        self.kr_bf = sb("kr_bf", [128, 4, MLA_R], BF16); self.krT = sb("krT", [128, 128], BF16)
        self.gq = sb("gq", [128, MLA_QL], F32); self.gkv = sb("gkv", [128, MLA_KVL], F32)
        self.rtm = sb("rtm", [128, 4, 32], F32); self.rfm = sb("rfm", [128, 2, TS], F32)
        self.xr = sb("xr", [128, TS], BF16)
        self.t1 = sb("t1", [128, TS], F32); self.t2 = sb("t2", [128, TS], F32)
        self.krc = sb("krc", [128, P // 128, MLA_R], BF16)
        self.psA = [ps(f"psA{i}", [128, 512], F32) for i in range(2)]
        self.psS = [ps(f"psS{i}", [128, 512], F32) for i in range(2)]
        self.psO = [ps(f"psO{i}", [128, 512], F32) for i in range(2)]
        self.psM = ps("psM", [128, 512], F32)
        self.psT = ps("psT", [128, 1024], BF16)
        self.psM16 = self.psM[:, :].bitcast(BF16)
        assert list(self.psM16.shape) == [128, 1024], self.psM16.shape
        self.psT32 = self.psT[:, :].bitcast(F32)
        assert list(self.psT32.shape) == [128, 512], self.psT32.shape

        o = self.dram
        self.dma("sp", self.cst_bf[:, :], o["cst_bf"].ap(), writes=["cst"])
        self.dma("sp", self.cst_f[:, :], o["cst_f"].ap(), writes=["cst"])
        for i in range(3):
            self.memset(self.VC[i][:, :, :, 64:65], 1.0, [("VC", i)])
        self.memset(self.vt_bf[:, :, :, 64:65], 1.0, [("vt_bf", s) for s in range(4)])
        self.memset(self.vnew[:, :, 64:65], 1.0, ["vnew"])
        self.memset(self.one_col[:, :], 1.0, ["one_col"])
        for i in range(4):
            self.memset(self.qm[i][:, :], 0.0, [("qm", i)])
        self.memset(self.KR[:, :], 0.0, [("KR", i) for i in range(max(T, P) // 512)])
        self.memset(self.krT[:, :], 0.0, ["krT"])
        self.W = {}
        try:
            self.phase("convert0")
            self.convert_layer(0)
            self.phase("memory_kv")
            self.memory_kv()
            tiles = [Tile("s", 0)] + [Tile("p", j) for j in range(T // TS)]
            for l in range(L):
                if l + 1 < L:
                    self.convert_layer(l + 1)
                if l % 2 == 0:
                    self.memset(self.Sprev[:, :], 0.0, ["Sprev"])
                for tl in tiles:
                    if tl.kind == "p" and tl.j == 0 and l % 2 == 0:
                        self.memset(self.Sprev[:, :], 0.0, ["Sprev"])
                    self.layer_tile(l, tl)
        except StopBuild:
            pass
        self.S.emit(self.nc)
        self.es.close()
        return self.nc


def _constants(cfg):
    bf = ml_dtypes.bfloat16
    cb = np.zeros((128, 640), np.float32)
    k = np.arange(128)[:, None]
    q = np.arange(128)[None, :]
    cb[:, 0:128] = np.eye(128)
    cb[:, 128:256] = np.where(k > q, NEG, 0.0)
    cb[:, 256:384] = np.where((k // 64) > (q // 64), NEG, 0.0)
    cb[:, 384:512] = 1.0
    R = np.zeros((32, 32), np.float32)
    for m in range(16):
        R[m + 16, m] = -1.0
        R[m, m + 16] = 1.0
    cb[64:96, 512:544] = R
    cf = np.zeros((128, 256), np.float32)
    cf[:, 0:128] = (k <= q)
    cf[:, 128:256] = 1.0
    half = 16
    inv = (np.float32(10000.0) ** (-(np.arange(half, dtype=np.float32) / np.float32(half)))).astype(np.float32)

    def tables(pos):
        ang = pos.astype(np.float32)[:, None] * inv[None, :]
        return np.cos(ang).astype(np.float32), np.sin(ang).astype(np.float32)
    cp, sp_ = tables(np.arange(cfg.T))
    rope_tm_p = np.concatenate([cp, sp_], axis=1)
    rope_fm_p = np.stack([np.concatenate([cp, cp], 1).T, np.concatenate([sp_, sp_], 1).T]).astype(np.float32)
    cs, ss = tables(cfg.P + np.arange(DS))
    rope_tm_s = np.zeros((128, 32), np.float32)
    rope_fm_s = np.zeros((2, 32, 128), np.float32)
    for b in range(NSEQ):
        rope_tm_s[32 * b:32 * b + DS] = np.concatenate([cs, ss], 1)
        rope_fm_s[0, :, 32 * b:32 * b + DS] = np.concatenate([cs, cs], 1).T
        rope_fm_s[1, :, 32 * b:32 * b + DS] = np.concatenate([ss, ss], 1).T
    return dict(cst_bf=cb.astype(bf), cst_f=cf, rope_tm_p=np.ascontiguousarray(rope_tm_p),
                rope_tm_s=rope_tm_s, rope_fm_p=np.ascontiguousarray(rope_fm_p), rope_fm_s=rope_fm_s)


def run(cfg, inputs):
    nc = Builder(cfg).build()
    T, P, L, NF, NM = cfg.T, cfg.P, cfg.L, cfg.NF, cfg.NM
    consts = _constants(cfg)
    c32 = lambda a: np.ascontiguousarray(a, dtype=np.float32)
    shared = {}
    for k in ("g_mix", "g_cross", "g_mem", "g_ffn", "g_final", "b_fox_f", "g_mla_q", "g_mla_kv"):
        shared[k] = c32(inputs[k]).reshape(-1)
    for k in ("w_fox_in", "w_fox_out", "w_mla_a", "w_mla_qb", "w_mla_kvb", "w_mla_out", "w_x_q", "w_x_kv", "w_x_o",
              "w_ffn_gu", "w_ffn_down"):
        shared[k] = c32(inputs[k])
    shared.update(consts)
    in_maps = []
    for c in range(cfg.ncores):
        m = dict(shared)
        m["x_prompt"] = c32(inputs["x_prompt"][c])
        m["x_sample"] = c32(inputs["x_sample"][NSEQ * c:NSEQ * (c + 1)]).reshape(NSEQ * DS, D)
        m["mem_prompt"] = c32(inputs["mem_prompt"][c])
        sl = slice(NSEQ * c, NSEQ * (c + 1))
        m["cache_fox_k"] = c32(inputs["cache_fox_k"][:, sl]).reshape(NF, NSEQ, P, D)
        m["cache_fox_v"] = c32(inputs["cache_fox_v"][:, sl]).reshape(NF, NSEQ, P, D)
        m["cache_fox_logf"] = c32(inputs["cache_fox_logf"][:, sl])
        m["cache_mla_ckv"] = c32(inputs["cache_mla_ckv"][:, sl])
        m["cache_mla_krope"] = c32(inputs["cache_mla_krope"][:, sl])
        m["cache_mem_k"] = c32(inputs["cache_mem_k"][:, sl]).reshape(L, NSEQ, NMEM, D)
        m["cache_mem_v"] = c32(inputs["cache_mem_v"][:, sl]).reshape(L, NSEQ, NMEM, D)
        in_maps.append(m)
    res = run_bass_kernel_spmd(nc, in_maps, core_ids=list(range(cfg.ncores)))
    R = res.results
    B = cfg.ncores

    def gat(name, shape_per_core, axis):
        return np.stack([np.asarray(R[c][name], dtype=np.float32).reshape(shape_per_core) for c in range(B)], axis=axis)

    y_p = gat("y_p", (T, D), 0)
    y_s = gat("y_s", (NSEQ, DS, D), 0).reshape(B * NSEQ, DS, D)
    fk_p = gat("fk_p", (NF, T, NH, HD), 1)
    fv_p = gat("fv_p", (NF, T, NH, HD), 1)
    fl_p = gat("fl_p", (NF, T, NH), 1)
    mc_p = gat("mc_p", (NM, T, MLA_KVL), 1)
    mr_p = gat("mr_p", (NM, T, MLA_R), 1)
    mk_p = gat("memk_p", (L, NMEM, 4, 256), 1)
    mv_p = gat("memv_p", (L, NMEM, 4, 256), 1)
    fk_s = gat("fk_s", (NF, NSEQ, DS, NH, HD), 1).reshape(NF, B * NSEQ, DS, NH, HD)
    fv_s = gat("fv_s", (NF, NSEQ, DS, NH, HD), 1).reshape(NF, B * NSEQ, DS, NH, HD)
    fl_s = gat("fl_s", (NF, NSEQ, DS, NH), 1).reshape(NF, B * NSEQ, DS, NH)
    mc_s = gat("mc_s", (NM, NSEQ, DS, MLA_KVL), 1).reshape(NM, B * NSEQ, DS, MLA_KVL)
    mr_s = gat("mr_s", (NM, NSEQ, DS, MLA_R), 1).reshape(NM, B * NSEQ, DS, MLA_R)
    return (y_p, y_s, fk_p, fv_p, fl_p, mc_p, mr_p, mk_p, mv_p, fk_s, fv_s, fl_s, mc_s, mr_s)


def kernel(**inputs):
    cfg = Cfg(T=4096, P=2048, L=4, ncores=8)
    return run(cfg, inputs)
```
